# Optimizing a Trainium2 kernel written in Bass

```python
import jax, jax.numpy as jnp
from jax import lax
import numpy as np

D_MODEL = 2048
BATCH = 2
SEQ = 16384
DEPTH = 2

CHUNK = 64
Q_BLOCK = 128
N_BRANCH = 3
D_RNN = D_MODEL // 2
N_RNN_BLOCKS = 8
RNN_BLOCK = D_RNN // N_RNN_BLOCKS
CONV_WIDTH = 4
LRU_C = 8.0
N_HEADS = 8
HEAD_DIM = 128
D_ATTN = N_HEADS * HEAD_DIM
D_POOL = D_MODEL // 2
POOL_WINDOWS = (2, 4, 8, 16)
N_POOL_GROUPS = 4
POOL_GROUP = D_POOL // N_POOL_GROUPS
D_FF = ((-(-8 * D_MODEL // 3) + 255) // 256) * 256
NORM_EPS = 1e-6
IN_SIZES = (D_RNN, D_RNN, D_ATTN, D_ATTN, D_ATTN, N_HEADS, D_POOL, N_BRANCH * D_MODEL)
D_IN = D_RNN * 2 + D_ATTN * 3 + N_HEADS + D_POOL + N_BRANCH * D_MODEL

kernel_name = "hybrid_rglru_fox_pool_block"


def rms_norm(x, g):
    xf = x.astype(jnp.float32)
    y = xf * lax.rsqrt(jnp.mean(xf * xf, axis=-1, keepdims=True) + NORM_EPS)
    return (y * g.astype(jnp.float32)).astype(x.dtype)


def split_columns(p):
    offsets = []
    acc = 0
    for s in IN_SIZES[:-1]:
        acc += s
        offsets.append(acc)
    return jnp.split(p, offsets, axis=-1)


def causal_depthwise_conv(u, w, b):
    S = u.shape[1]
    up = jnp.pad(u, ((0, 0), (CONV_WIDTH - 1, 0), (0, 0)))
    out = b + up[:, 0:S] * w[0]
    for k in range(1, CONV_WIDTH):
        out = out + up[:, k:k + S] * w[k]
    return out


def rg_lru(u, w_r, b_r, w_i, b_i, lam):
    B, S, _ = u.shape
    ub = u.reshape(B, S, N_RNN_BLOCKS, RNN_BLOCK)
    r = jax.nn.sigmoid(jnp.einsum('bsnc,ncd->bsnd', ub, w_r).reshape(B, S, D_RNN) + b_r)
    i = jax.nn.sigmoid(jnp.einsum('bsnc,ncd->bsnd', ub, w_i).reshape(B, S, D_RNN) + b_i)
    log_a = (LRU_C * r.astype(jnp.float32)) * jax.nn.log_sigmoid(lam.astype(jnp.float32))
    a = jnp.exp(log_a)
    b_in = jnp.sqrt(-jnp.expm1(2.0 * log_a)) * (i * u).astype(jnp.float32)

    def combine(left, right):
        a1, b1 = left
        a2, b2 = right
        return a1 * a2, a2 * b1 + b2

    _, h = lax.associative_scan(combine, (a, b_in), axis=1)
    return h.astype(u.dtype)


def forgetting_attention(q, k, v, f_logit):
    B, S, H, Dh = q.shape
    nb = S // Q_BLOCK
    scale = Dh ** -0.5
    F = jnp.cumsum(jax.nn.log_sigmoid(f_logit.astype(jnp.float32)), axis=1).transpose(0, 2, 1)
    k_h = k.transpose(0, 2, 1, 3)
    v_h = v.transpose(0, 2, 1, 3)
    q_blocks = q.reshape(B, nb, Q_BLOCK, H, Dh).transpose(1, 0, 3, 2, 4)
    F_blocks = F.reshape(B, H, nb, Q_BLOCK).transpose(2, 0, 1, 3)
    k_pos = jnp.arange(S)

    def one_block(args):
        qb, Fq, blk = args
        s = jnp.einsum('bhqd,bhkd->bhqk', qb, k_h, preferred_element_type=jnp.float32) * scale
        s = s + Fq[..., None] - F[:, :, None, :]
        q_pos = blk * Q_BLOCK + jnp.arange(Q_BLOCK)
        s = jnp.where(q_pos[:, None] >= k_pos[None, :], s, -jnp.inf)
        p = jax.nn.softmax(s, axis=-1)
        return jnp.einsum('bhqk,bhkd->bhqd', p.astype(v_h.dtype), v_h)

    o = lax.map(one_block, (q_blocks, F_blocks, jnp.arange(nb)))
    return o.transpose(1, 0, 3, 2, 4).reshape(B, S, H * Dh)


def multiscale_pool(u, w_pool, scale):
    B, S, _ = u.shape
    uf = u.astype(jnp.float32).reshape(B, S, N_POOL_GROUPS, POOL_GROUP)
    cs = jnp.cumsum(uf, axis=1)
    t = jnp.arange(1, S + 1, dtype=jnp.float32)
    outs = []
    for g, w in enumerate(POOL_WINDOWS):
        c = cs[:, :, g]
        lagged = jnp.pad(c, ((0, 0), (w, 0), (0, 0)))[:, :S]
        cnt = jnp.minimum(t, w)[None, :, None]
        outs.append((c - lagged) / cnt - uf[:, :, g])
    pooled = jnp.stack(outs, axis=2).astype(u.dtype)
    y = jnp.einsum('bsgc,gcd->bsgd', pooled, w_pool).reshape(B, S, D_POOL)
    return y * scale


def hybrid_layer(x, g_mix, w_in, b_forget, conv_w, conv_b, w_rg, b_rg, w_ig, b_ig, lru_lambda,
                 w_pool, pool_scale, w_branch_rnn, w_branch_attn, w_branch_pool, w_out,
                 g_ffn, w_ffn_in, w_ffn_out):
    B, S, _ = x.shape
    h = rms_norm(x, g_mix)
    rnn_x, rnn_y, q, k, v, f_logit, pool_in, gate_logits = split_columns(h @ w_in)
    u = causal_depthwise_conv(rnn_x, conv_w, conv_b)
    y_a = jax.nn.gelu(rnn_y) * rg_lru(u, w_rg, b_rg, w_ig, b_ig, lru_lambda)
    y_b = forgetting_attention(q.reshape(B, S, N_HEADS, HEAD_DIM),
                               k.reshape(B, S, N_HEADS, HEAD_DIM),
                               v.reshape(B, S, N_HEADS, HEAD_DIM),
                               f_logit + b_forget)
    y_c = multiscale_pool(pool_in, w_pool, pool_scale)
    gates = jax.nn.sigmoid(gate_logits.astype(jnp.float32)).astype(x.dtype).reshape(B, S, N_BRANCH, D_MODEL)
    merged = (gates[:, :, 0] * (y_a @ w_branch_rnn)
              + gates[:, :, 1] * (y_b @ w_branch_attn)
              + gates[:, :, 2] * (y_c @ w_branch_pool))
    x = x + merged @ w_out
    h2 = rms_norm(x, g_ffn)
    gate, up = jnp.split(h2 @ w_ffn_in, 2, axis=-1)
    return x + (jax.nn.silu(gate) * up) @ w_ffn_out


def setup_inputs(seed: int = 0) -> dict:
    key = jax.random.key(seed)
    ks = jax.random.split(key, 24)
    f32 = jnp.float32
    nrm = lambda k, shape, s: jax.random.normal(k, shape, f32) * s
    a8 = jax.random.uniform(ks[10], (DEPTH, D_RNN), f32, 0.9, 0.999)
    a = a8 ** (1.0 / LRU_C)
    lru_lambda = jnp.log(a) - jnp.log1p(-a)
    return {
        "x": jax.random.normal(ks[0], (BATCH, SEQ, D_MODEL), f32),
        "g_mix": 1.0 + nrm(ks[1], (DEPTH, D_MODEL), 0.05),
        "w_in": nrm(ks[2], (DEPTH, D_MODEL, D_IN), D_MODEL ** -0.5),
        "b_forget": jax.random.uniform(ks[3], (DEPTH, N_HEADS), f32, 1.0, 4.0),
        "conv_w": nrm(ks[4], (DEPTH, CONV_WIDTH, D_RNN), CONV_WIDTH ** -0.5),
        "conv_b": nrm(ks[5], (DEPTH, D_RNN), 0.02),
        "w_rg": nrm(ks[6], (DEPTH, N_RNN_BLOCKS, RNN_BLOCK, RNN_BLOCK), RNN_BLOCK ** -0.5),
        "b_rg": nrm(ks[7], (DEPTH, D_RNN), 0.02),
        "w_ig": nrm(ks[8], (DEPTH, N_RNN_BLOCKS, RNN_BLOCK, RNN_BLOCK), RNN_BLOCK ** -0.5),
        "b_ig": nrm(ks[9], (DEPTH, D_RNN), 0.02),
        "lru_lambda": lru_lambda,
        "w_pool": nrm(ks[11], (DEPTH, N_POOL_GROUPS, POOL_GROUP, POOL_GROUP), POOL_GROUP ** -0.5),
        "pool_scale": 1.0 + nrm(ks[12], (DEPTH, D_POOL), 0.1),
        "w_branch_rnn": nrm(ks[13], (DEPTH, D_RNN, D_MODEL), D_RNN ** -0.5),
        "w_branch_attn": nrm(ks[14], (DEPTH, D_ATTN, D_MODEL), D_ATTN ** -0.5),
        "w_branch_pool": nrm(ks[15], (DEPTH, D_POOL, D_MODEL), D_POOL ** -0.5),
        "w_out": nrm(ks[16], (DEPTH, D_MODEL, D_MODEL), D_MODEL ** -0.5),
        "g_ffn": 1.0 + nrm(ks[17], (DEPTH, D_MODEL), 0.05),
        "w_ffn_in": nrm(ks[18], (DEPTH, D_MODEL, 2 * D_FF), D_MODEL ** -0.5),
        "w_ffn_out": nrm(ks[19], (DEPTH, D_FF, D_MODEL), D_FF ** -0.5),
        "g_final": 1.0 + nrm(ks[20], (D_MODEL,), 0.05),
    }


def reference(x, g_mix, w_in, b_forget, conv_w, conv_b, w_rg, b_rg, w_ig, b_ig, lru_lambda,
              w_pool, pool_scale, w_branch_rnn, w_branch_attn, w_branch_pool, w_out,
              g_ffn, w_ffn_in, w_ffn_out, g_final):
    for l in range(DEPTH):
        x = hybrid_layer(x, g_mix[l], w_in[l], b_forget[l], conv_w[l], conv_b[l], w_rg[l], b_rg[l],
                         w_ig[l], b_ig[l], lru_lambda[l], w_pool[l], pool_scale[l],
                         w_branch_rnn[l], w_branch_attn[l], w_branch_pool[l], w_out[l],
                         g_ffn[l], w_ffn_in[l], w_ffn_out[l])
    return rms_norm(x, g_final)
```

```python
import contextlib
import numpy as np
import ml_dtypes
import concourse.bass as bass
import concourse.mybir as mybir
from concourse.bass import ds
from concourse.bass_utils import run_bass_kernel_spmd

F32 = mybir.dt.float32
BF16 = mybir.dt.bfloat16
AF = mybir.ActivationFunctionType
ALU = mybir.AluOpType

D = 2048
KD = D // 128
DFF = 5632
KFF = DFF // 128
EPS = 1e-6
NCORES = 8
SEQ = 16384
SAME_ENGINE_SYNC = True
SEM_LIMIT = 30000


class Eng:
    def __init__(self, P, name, h, step):
        self.P, self.name, self.h, self.step = P, name, h, step
        self.n = 0
        self.sem = P.new_sem(name)
        self.nsem = 0
        self.waited = {}

    def tick(self):
        if self.n >= SEM_LIMIT * self.step:
            self.nsem += 1
            self.sem = self.P.new_sem("%s_%d" % (self.name, self.nsem))
            self.n = 0
        self.n += self.step
        return (self.sem, self.n)


class Buf:
    __slots__ = ("name", "writer", "readers")

    def __init__(self, name):
        self.name = name
        self.writer = None
        self.readers = {}


class Prog:
    def __init__(self):
        self.nc = bass.Bass("TRN2", target_bir_lowering=False)
        self.stack = contextlib.ExitStack()
        self.nsem_total = 0
        nc = self.nc
        self.pe = Eng(self, "pe", nc.tensor, 1)
        self.act = Eng(self, "act", nc.scalar, 1)
        self.dve = Eng(self, "dve", nc.vector, 1)
        self.pool = Eng(self, "pool", nc.gpsimd, 1)
        self.sp = Eng(self, "sp", nc.sync, 1)
        self.dmaq = {}
        self.scopes = []
        self.nname = 0
        self.xr = 0

    def new_sem(self, name):
        self.nsem_total += 1
        return self.stack.enter_context(self.nc.semaphore("s_%s_%d" % (name, self.nsem_total)))

    def sbuf(self, name, shape, dt):
        st = self.scopes[-1] if self.scopes else self.stack
        self.nname += 1
        return st.enter_context(self.nc.sbuf_tensor("%s_%d" % (name, self.nname), shape, dt))

    def push_scope(self):
        self.scopes.append(contextlib.ExitStack())

    def pop_scope(self):
        self.barrier()
        self.scopes.pop().close()

    def barrier(self):
        engs = [self.pe, self.act, self.dve, self.pool, self.sp]
        targets = {}
        for E in engs + list(self.dmaq.values()):
            if E.n > 0:
                targets[E.sem] = E.n
        for E in engs:
            self._wait(E, {s: n for s, n in targets.items() if s is not E.sem})

    def cc(self, reads, writes, emit):
        if "cc" not in self.dmaq:
            self.dmaq["cc"] = Eng(self, "cc", None, 1)
        S = self.dmaq["cc"]
        self._wait(self.pool, self._deps(reads, writes))
        ins = emit()
        tok = S.tick()
        ins.then_inc(tok[0])
        for b in reads:
            b.readers[id(S)] = tok
        for b in writes:
            b.writer = tok
            b.readers = {}
        return tok

    def psum(self, name, shape, dt=F32):
        return self.stack.enter_context(self.nc.psum_tensor(name, shape, dt))

    def dram(self, name, shape, dt, kind="Internal"):
        return self.nc.dram_tensor(name, shape, dt, kind=kind).ap()

    def _deps(self, reads, writes):
        deps = {}
        for b in reads:
            if b.writer is not None:
                s, n = b.writer
                deps[s] = max(deps.get(s, 0), n)
        for b in writes:
            if b.writer is not None:
                s, n = b.writer
                deps[s] = max(deps.get(s, 0), n)
            for (s, n) in b.readers.values():
                deps[s] = max(deps.get(s, 0), n)
        return deps

    def _wait(self, E, deps, skip_sem=None):
        for s, n in deps.items():
            if s is skip_sem:
                continue
            if E.waited.get(s, 0) < n:
                E.h.wait_ge(s, n)
                E.waited[s] = n

    def op(self, E, reads, writes, emit):
        deps = self._deps(reads, writes)
        skip = None
        if E is self.pe or not SAME_ENGINE_SYNC:
            skip = E.sem
        self._wait(E, deps, skip)
        ins = emit()
        tok = E.tick()
        ins.then_inc(tok[0], 1)
        for b in reads:
            b.readers[id(E)] = tok
        for b in writes:
            b.writer = tok
            b.readers = {}
        return tok

    def dma(self, Q, slot, reads, writes, emit):
        if slot not in self.dmaq:
            self.dmaq[slot] = Eng(self, "d" + slot, None, 16)
        S = self.dmaq[slot]
        deps = self._deps(reads, writes)
        self._wait(Q, deps)
        ins = emit()
        tok = S.tick()
        ins.then_inc(tok[0], 16)
        for b in reads:
            b.readers[id(S) + len(b.readers) * 0] = tok
        for b in writes:
            b.writer = tok
            b.readers = {}
        return tok

    def finish(self, bufs):
        deps = {}
        for b in bufs:
            if b.writer is not None:
                s, n = b.writer
                deps[s] = max(deps.get(s, 0), n)
        self._wait(self.sp, deps)
        self.stack.close()


class WRing:
    def __init__(self, P, nslots):
        self.P = P
        self.n = nslots
        self.t = P.sbuf("wring", [128, nslots, 16, 128], BF16)
        self.bufs = [Buf("w%d" % i) for i in range(nslots)]
        P.wring_gen = getattr(P, "wring_gen", 0) + 1
        self.gen = P.wring_gen
        self.i = 0

    def load(self, src, kc):
        P = self.P
        i = self.i
        self.i = (self.i + 1) % self.n
        dst = self.t[:, i, 0:kc, :]
        b = self.bufs[i]
        P.dma(P.pool, "w%d_%d" % (self.gen, i), [], [b],
              lambda: P.nc.gpsimd.dma_start(out=dst, in_=src, max_dma_last_dim=4096))
        return self.t[:, i], b


def emit_norm(P, C, src_chunk, src_buf, gcol, out_fn, TT):
    nc = P.nc
    nsub = TT // 512
    xring, xbufs, sq, sqbufs, ps, pb, rstd, rstdbuf, ones_bf, onesb = (
        C["xring"], C["xbufs"], C["sq"], C["sqbufs"], C["ps"], C["pb"], C["rstd"], C["rstdbuf"], C["ones_bf"],
        C["onesb"])
    nx = len(xbufs)
    for k in range(KD):
        xi = P.xr
        P.xr = (P.xr + 1) % nx
        sb = src_buf(k)
        P.dma(P.sp, "x%d" % xi, [sb] if sb else [], [xbufs[xi]],
              lambda xi=xi, k=k: nc.sync.dma_start(out=xring[:, xi, :], in_=src_chunk(k)))
        si = k % 2
        P.op(P.act, [xbufs[xi]], [sqbufs[si]],
             lambda xi=xi, si=si: nc.scalar.activation(out=sq[:, si, :], in_=xring[:, xi, :], func=AF.Square))
        for s in range(nsub):
            P.op(P.pe, [sqbufs[si], onesb], [pb[6 + s]],
                 lambda s=s, si=si, k=k: nc.tensor.matmul(ps[:, 6 + s, :], lhsT=ones_bf[:, :],
                                                          rhs=sq[:, si, s * 512:(s + 1) * 512],
                                                          start=(k == 0), stop=(k == KD - 1)))
    for s in range(nsub):
        P.op(P.act, [pb[6 + s]], [rstdbuf],
             lambda s=s: nc.scalar.activation(out=rstd[:, s * 512:(s + 1) * 512], in_=ps[:, 6 + s, :],
                                              func=AF.Sqrt, scale=1.0 / D, bias=C["eps"][:, 0:1]))
    P.op(P.dve, [rstdbuf], [rstdbuf], lambda: nc.vector.reciprocal(out=rstd[:, :], in_=rstd[:, :]))
    for k in range(KD):
        xi = P.xr
        P.xr = (P.xr + 1) % nx
        sb = src_buf(k)
        P.dma(P.sp, "x%d" % xi, [sb] if sb else [], [xbufs[xi]],
              lambda xi=xi, k=k: nc.sync.dma_start(out=xring[:, xi, :], in_=src_chunk(k)))
        out_fn(k, xring[:, xi, :], gcol[:, k:k + 1], rstd[:, :], [xbufs[xi], rstdbuf])


def common_setup(P, TT, ps, pb, NX=4):
    nc = P.nc
    C = {}
    C["ones_bf"] = P.sbuf("ones_bf", [128, 128], BF16)
    C["onesb"] = Buf("ones")
    C["eps"] = P.sbuf("eps", [128, 1], F32)
    C["xring"] = P.sbuf("xring", [128, NX, TT], F32)
    C["xbufs"] = [Buf("x%d" % i) for i in range(NX)]
    C["sq"] = P.sbuf("sq", [128, 2, TT], BF16)
    C["sqbufs"] = [Buf("sq0"), Buf("sq1")]
    C["rstd"] = P.sbuf("rstd", [128, TT], F32)
    C["rstdbuf"] = Buf("rstd")
    C["ps"] = ps
    C["pb"] = pb
    P.xr = 0
    P.op(P.dve, [], [C["onesb"]], lambda: nc.vector.memset(C["ones_bf"][:], 1.0))
    P.op(P.dve, [], [C["rstdbuf"]], lambda: nc.vector.memset(C["eps"][:], EPS))
    return C


def emit_B(P, ps_, pb_, NTOK, TT, d, last):
    nc = P.nc
    nsub = TT // 512
    xT, xsrc_buf = d["xsrc"], d["xsrc_buf"]
    yfull, ybuf = d["yfull"], d["ybuf"]
    wg, wbr, wo, wf1, wf2, gv = d["wg"], d["wbr"], d["wo"], d["wf1"], d["wf2"], d["gv"]
    x1T, x2T, dst = d["x1T"], d["x2T"], d["dst"]
    ntt = NTOK // TT
    x1bufs = [[Buf("x1_%d_%d" % (t, n)) for n in range(16)] for t in range(ntt)]
    x2bufs = d["x2bufs"]
    outbufs = d["outbufs"]
    final_norm = True

    P.push_scope()
    C = common_setup(P, TT, ps_, pb_)
    ps, pb, xring, xbufs = C["ps"], C["pb"], C["xring"], C["xbufs"]
    NX = len(xbufs)
    g_sb = P.sbuf("g_sb", [128, 3, 16], F32)
    gbuf = Buf("g")
    hT = P.sbuf("hT", [128, KD, TT], BF16)
    hbufs = [Buf("h%d" % k) for k in range(KD)]
    R = P.sbuf("R", [128, KFF, TT], BF16)
    rb = [Buf("r%d" % k) for k in range(KFF)]
    tmp = P.sbuf("tmp", [128, 3, TT], F32)
    tb = [Buf("t%d" % i) for i in range(3)]
    ost = P.sbuf("ost", [128, 2, TT], F32)
    ob = [Buf("o%d" % i) for i in range(2)]
    hst = P.sbuf("hst", [128, 2, TT], BF16)
    hsb_ = [Buf("hst%d" % i) for i in range(2)]
    wr = WRing(P, 8)

    P.dma(P.sp, "g", [], [gbuf], lambda: nc.sync.dma_start(out=g_sb[:], in_=gv))

    state = {"pset": 0, "oi": 0}

    def v3(ap2d):
        return ap2d.rearrange("p (s t) -> p s t", s=nsub)

    def pview(pset):
        return ps[:, pset * 2:pset * 2 + nsub, :]

    def pbufs(pset):
        return pb[pset * 2:pset * 2 + nsub]

    def gemm(wsrcs, kc_list, rhs_fn, rhs_bufs):
        pset = state["pset"]
        state["pset"] ^= 1
        slots = [wr.load(src, kc) for src, kc in zip(wsrcs, kc_list)]
        ktot = sum(kc_list)
        for s in range(nsub):
            def emit(s=s):
                kk = 0
                ins = None
                for (wt, wbuf), kc in zip(slots, kc_list):
                    for k in range(kc):
                        ins = nc.tensor.matmul(ps[:, pset * 2 + s, :], lhsT=wt[:, k, :], rhs=rhs_fn(kk, s),
                                               start=(kk == 0), stop=(kk == ktot - 1))
                        kk += 1
                return ins
            P.op(P.pe, [wb_ for (_, wb_) in slots] + rhs_bufs, [pb[pset * 2 + s]], emit)
        return pset

    def h_rhs(kk, s):
        return hT[:, kk, s * 512:(s + 1) * 512]

    def R_rhs(kk, s):
        return R[:, kk, s * 512:(s + 1) * 512]

    def norm_to_h(k, x_ap, g_ap, rstd_ap, reads):
        P.op(P.dve, reads + [gbuf], [hbufs[k]],
             lambda: nc.vector.scalar_tensor_tensor(out=hT[:, k, :], in0=x_ap, scalar=g_ap, in1=rstd_ap,
                                                    op0=ALU.mult, op1=ALU.mult))

    for tt in range(ntt):
        t0 = tt * TT
        emit_norm(P, C, lambda k: xT[k * 128:(k + 1) * 128, t0:t0 + TT], lambda k: xsrc_buf(tt, k), g_sb[:, 0, :],
                  norm_to_h, TT)
        for i in range(3):
            for k in range(8):
                r = 16 + i * 8 + k
                row0 = ((i * 2 + k % 2) * 4 + k // 2) * 128
                P.dma(P.sp, "y%d" % (i * 8 + k), [ybuf], [rb[r]],
                      lambda r=r, row0=row0: nc.sync.dma_start(
                          out=R[:, r, :], in_=yfull[row0:row0 + 128, t0:t0 + TT]))
        for m in range(16):
            for i in range(3):
                pset = gemm([wg[i * 16 + m]], [16], h_rhs, hbufs)
                P.op(P.act, pbufs(pset), [tb[0]],
                     lambda pset=pset: nc.scalar.activation(out=v3(tmp[:, 0, :]), in_=pview(pset), func=AF.Sigmoid))
                pset = gemm([wbr[i * 16 + m]], [8],
                            lambda kk, s, i=i: R[:, 16 + i * 8 + kk, s * 512:(s + 1) * 512],
                            rb[16 + i * 8:16 + i * 8 + 8])
                dsti = 1 if i == 0 else 2
                P.op(P.dve, pbufs(pset) + [tb[0]], [tb[dsti]],
                     lambda pset=pset, dsti=dsti: nc.vector.tensor_tensor(
                         out=v3(tmp[:, dsti, :]), in0=pview(pset), in1=v3(tmp[:, 0, :]), op=ALU.mult))
                if i == 1:
                    P.op(P.dve, [tb[1], tb[2]], [tb[1]],
                         lambda: nc.vector.tensor_tensor(out=tmp[:, 1, :], in0=tmp[:, 1, :], in1=tmp[:, 2, :],
                                                         op=ALU.add))
                elif i == 2:
                    P.op(P.dve, [tb[1], tb[2]], [rb[m]],
                         lambda m=m: nc.vector.tensor_tensor(out=R[:, m, :], in0=tmp[:, 1, :], in1=tmp[:, 2, :],
                                                             op=ALU.add))
        for n in range(16):
            pset = gemm([wo[n]], [16], R_rhs, rb[0:16])
            xi = P.xr
            P.xr = (P.xr + 1) % NX
            xsb = xsrc_buf(tt, n)
            P.dma(P.sp, "x%d" % xi, [xsb] if xsb else [], [xbufs[xi]],
                  lambda xi=xi, n=n: nc.sync.dma_start(out=xring[:, xi, :],
                                                       in_=xT[n * 128:(n + 1) * 128, t0:t0 + TT]))
            oi = state["oi"]
            state["oi"] ^= 1
            P.op(P.dve, pbufs(pset) + [xbufs[xi]], [ob[oi]],
                 lambda pset=pset, xi=xi, oi=oi: nc.vector.tensor_tensor(
                     out=v3(ost[:, oi, :]), in0=pview(pset), in1=v3(xring[:, xi, :]), op=ALU.add))
            P.dma(P.sp, "x1w%d" % oi, [ob[oi]], [x1bufs[tt][n]],
                  lambda oi=oi, n=n: nc.sync.dma_start(out=x1T[n * 128:(n + 1) * 128, t0:t0 + TT],
                                                       in_=ost[:, oi, :]))
        emit_norm(P, C, lambda k: x1T[k * 128:(k + 1) * 128, t0:t0 + TT], lambda k: x1bufs[tt][k],
                  g_sb[:, 1, :], norm_to_h, TT)
        for j in range(KFF):
            pg = gemm([wf1[2 * j]], [16], h_rhs, hbufs)
            ti = j % 2
            P.op(P.act, pbufs(pg), [tb[ti]],
                 lambda pg=pg, ti=ti: nc.scalar.activation(out=v3(tmp[:, ti, :]), in_=pview(pg), func=AF.Silu))
            pu = gemm([wf1[2 * j + 1]], [16], h_rhs, hbufs)
            P.op(P.dve, pbufs(pu) + [tb[ti]], [rb[j]],
                 lambda pu=pu, ti=ti, j=j: nc.vector.tensor_tensor(
                     out=v3(R[:, j, :]), in0=pview(pu), in1=v3(tmp[:, ti, :]), op=ALU.mult))
        for n in range(16):
            pset = gemm([wf2[n, :, 0:16, :], wf2[n, :, 16:32, :], wf2[n, :, 32:44, :]], [16, 16, 12], R_rhs, rb)
            xi = P.xr
            P.xr = (P.xr + 1) % NX
            P.dma(P.sp, "x%d" % xi, [x1bufs[tt][n]], [xbufs[xi]],
                  lambda xi=xi, n=n: nc.sync.dma_start(out=xring[:, xi, :],
                                                       in_=x1T[n * 128:(n + 1) * 128, t0:t0 + TT]))
            oi = state["oi"]
            state["oi"] ^= 1
            P.op(P.dve, pbufs(pset) + [xbufs[xi]], [ob[oi]],
                 lambda pset=pset, xi=xi, oi=oi: nc.vector.tensor_tensor(
                     out=v3(ost[:, oi, :]), in0=pview(pset), in1=v3(xring[:, xi, :]), op=ALU.add))
            if final_norm:
                P.dma(P.sp, "x2w%d" % oi, [ob[oi]], [x2bufs[tt][n]],
                      lambda oi=oi, n=n: nc.sync.dma_start(out=x2T[n * 128:(n + 1) * 128, t0:t0 + TT],
                                                           in_=ost[:, oi, :]))
            else:
                P.dma(P.sp, "ow", [ob[oi]], [outbufs[tt][n]],
                      lambda oi=oi, n=n: nc.sync.dma_start(out=oT[n * 128:(n + 1) * 128, t0:t0 + TT],
                                                           in_=ost[:, oi, :]))
        if final_norm:
            def norm_to_out(k, x_ap, g_ap, rstd_ap, reads):
                oi = state["oi"]
                state["oi"] ^= 1
                P.op(P.dve, reads + [gbuf], [ob[oi]],
                     lambda: nc.vector.scalar_tensor_tensor(out=ost[:, oi, :], in0=x_ap, scalar=g_ap, in1=rstd_ap,
                                                            op0=ALU.mult, op1=ALU.mult))
                P.dma(P.sp, "ow", [ob[oi]], [outbufs[tt][k]],
                      lambda: nc.sync.dma_start(out=dst[k * 128:(k + 1) * 128, t0:t0 + TT], in_=ost[:, oi, :]))

            def norm_to_hloc(k, x_ap, g_ap, rstd_ap, reads):
                oi = state["oi"]
                state["oi"] ^= 1
                P.op(P.dve, reads + [gbuf], [hsb_[oi]],
                     lambda: nc.vector.scalar_tensor_tensor(out=hst[:, oi, :], in0=x_ap, scalar=g_ap, in1=rstd_ap,
                                                            op0=ALU.mult, op1=ALU.mult))
                P.dma(P.sp, "ow", [hsb_[oi]], [outbufs[tt][k]],
                      lambda: nc.sync.dma_start(out=dst[k * 128:(k + 1) * 128, t0:t0 + TT], in_=hst[:, oi, :]))
            emit_norm(P, C, lambda k: x2T[k * 128:(k + 1) * 128, t0:t0 + TT], lambda k: x2bufs[tt][k],
                      g_sb[:, 2, :], norm_to_out if last else norm_to_hloc, TT)
    P.pop_scope()


def emit_N(P, ps_, pb_, NTOK, TT, xT, gm, hloc, outbufs):
    nc = P.nc
    P.push_scope()
    C = common_setup(P, TT, ps_, pb_)
    g_sb = P.sbuf("gN", [128, 16], F32)
    gbuf = Buf("gN")
    hst = P.sbuf("hstN", [128, 2, TT], BF16)
    hsb_ = [Buf("hstN%d" % i) for i in range(2)]
    P.dma(P.sp, "g", [], [gbuf], lambda: nc.sync.dma_start(out=g_sb[:], in_=gm))
    st = {"oi": 0}
    for tt in range(NTOK // TT):
        t0 = tt * TT

        def out_fn(k, x_ap, g_ap, rstd_ap, reads):
            oi = st["oi"]
            st["oi"] ^= 1
            P.op(P.dve, reads + [gbuf], [hsb_[oi]],
                 lambda: nc.vector.scalar_tensor_tensor(out=hst[:, oi, :], in0=x_ap, scalar=g_ap, in1=rstd_ap,
                                                        op0=ALU.mult, op1=ALU.mult))
            P.dma(P.sp, "ow", [hsb_[oi]], [outbufs[tt][k]],
                  lambda: nc.sync.dma_start(out=hloc[k * 128:(k + 1) * 128, t0:t0 + TT], in_=hst[:, oi, :]))
        emit_norm(P, C, lambda k: xT[k * 128:(k + 1) * 128, t0:t0 + TT], lambda k: None, g_sb, out_fn, TT)
    P.pop_scope()


def arrange_w(w, kc):
    K, N = w.shape
    return np.ascontiguousarray(w.reshape(K // 128, 128, N // 128, 128).transpose(2, 1, 0, 3))


def prep_B_weights(l, w_in, w_branch_rnn, w_branch_attn, w_branch_pool, w_out, w_ffn_in, w_ffn_out,
                   g_mix, g_ffn, g_final):
    off = D_IN_GATES
    wg = arrange_w(w_in[l][:, off:off + 3 * D], 16)
    wbr = np.concatenate([arrange_w(w_branch_rnn[l], 8), arrange_w(w_branch_attn[l], 8),
                          arrange_w(w_branch_pool[l], 8)], axis=0)
    wo = arrange_w(w_out[l], 16)
    f1 = w_ffn_in[l]
    f1i = np.stack([f1[:, :DFF].reshape(D, KFF, 128), f1[:, DFF:].reshape(D, KFF, 128)], axis=2).reshape(D, 2 * DFF)
    wf1 = arrange_w(f1i, 16)
    wf2 = arrange_w(w_ffn_out[l], 44)
    gv = np.ascontiguousarray(np.stack([g_mix[l].reshape(16, 128).T, g_ffn[l].reshape(16, 128).T,
                                        g_final.reshape(16, 128).T], axis=1)).astype(np.float32)
    return dict(wg=wg, wbr=wbr, wo=wo, wf1=wf1, wf2=wf2, gv=gv)


D_IN_GATES = 1024 * 2 + 1024 * 3 + 8 + 1024


NPAR = 24
SQRT_DH = 11.313708498984761
GELU_C = 1.5957691216057308


def emit_A(P, ps, pb, S, d):
    nc = P.nc
    TT = 512
    NT = S // TT
    NKB = S // 128
    NTOK = S // 4
    hfull, hfbuf = d["hfull"], d["hfbuf"]
    w1d, w2d, wrgd, wpld, pad, ptabd, yo = d["w1"], d["w2"], d["wrg"], d["wpl"], d["pa"], d["ptab"], d["yo"]
    yob = [Buf("yo%d" % i) for i in range(8)]
    P.push_scope()
    ones_bf = P.sbuf("ones_bfA", [128, 128], BF16)
    onesb = Buf("onesA")
    P.op(P.dve, [], [onesb], lambda: nc.vector.memset(ones_bf[:], 1.0))

    def sb(name, shape, dt=F32):
        return P.sbuf(name, shape, dt), Buf(name)

    pa, pabuf = sb("pa_sb", [128, NPAR])
    ptab, ptabbuf = sb("ptab_sb", [128, 4, 16])
    cst, cstbuf = sb("cst", [128, 8])
    w1, w1buf = sb("w1_sb", [128, 16, 768], BF16)
    w2, w2buf = sb("w2_sb", [128, 16, 448], BF16)
    wrg, wrgbuf = sb("wrg_sb", [128, 4, 128], BF16)
    wpl, wplbuf = sb("wpl_sb", [128, 2, 256], BF16)
    hT = P.sbuf("hT", [128, 2, KD, TT], BF16)
    hbufs = [[Buf("h%d_%d" % (j, k)) for k in range(KD)] for j in range(2)]
    NTMP = 8
    tmp = P.sbuf("tmp", [128, NTMP, TT], F32)
    tb = [Buf("t%d" % i) for i in range(NTMP)]
    tmpb = P.sbuf("tmpb", [128, 4, TT], BF16)
    tbb = [Buf("tb%d" % i) for i in range(4)]
    xb = P.sbuf("xb", [128, 2, TT + 3], F32)
    xbb = [Buf("xb0"), Buf("xb1")]
    hcar = P.sbuf("hcar", [128, 2], F32)
    hcb = [Buf("hc0"), Buf("hc1")]
    ub = P.sbuf("ub", [128, 2, TT + 15], F32)
    ubb = [Buf("ub0"), Buf("ub1")]
    sw = P.sbuf("sw", [128, 4, TT + 15], F32)
    swb = [Buf("sw%d" % i) for i in range(4)]
    t16 = P.sbuf("t16", [128, 2, 16], F32)
    t16b = [Buf("t16a"), Buf("t16b")]
    KT, ktbuf_all = sb("KT", [128, S], BF16)
    ktb = [Buf("kt%d" % i) for i in range(NT)]
    V = P.sbuf("V", [128, NKB, 128], BF16)
    vb = [Buf("v%d" % i) for i in range(NT)]
    negF = P.sbuf("negF", [128, NKB], F32)
    nfb = [Buf("nf%d" % i) for i in range(NT)]
    bm, bmbuf = sb("bm", [128, NKB])
    ct, ctbuf = sb("ct", [128, 1])
    fc = P.sbuf("fc", [64, 2], F32)
    fcb = [Buf("fc0"), Buf("fc1")]
    one32, one32b = sb("one32", [128, 128], F32)
    onesT, onesTb = sb("onesT", [64, TT], F32)
    Z, zbuf = sb("Z", [128, TT], BF16)
    selT, selb = sb("selT", [128, 128], BF16)
    tri, trib = sb("tri", [128, 128], BF16)
    pT = P.sbuf("pT", [128, 3, TT], BF16)
    ptb = [Buf("pt%d" % i) for i in range(3)]
    state = {"bank": 0, "pt": 0, "yo": 0, "tmp": 0}

    P.dma(P.sp, "c0a", [], [pabuf], lambda: nc.sync.dma_start(out=pa[:], in_=pad))
    P.dma(P.sp, "c0b", [], [ptabbuf], lambda: nc.sync.dma_start(out=ptab[:], in_=ptabd))
    for k in range(KD):
        P.dma(P.pool, "cw1", [], [w1buf],
              lambda k=k: nc.gpsimd.dma_start(out=w1[:, k, :], in_=w1d[:, k, :], max_dma_last_dim=4096))
    P.dma(P.pool, "cwr", [], [wrgbuf], lambda: nc.gpsimd.dma_start(out=wrg[:], in_=wrgd, max_dma_last_dim=4096))
    P.dma(P.pool, "cwp", [], [wplbuf], lambda: nc.gpsimd.dma_start(out=wpl[:], in_=wpld, max_dma_last_dim=4096))
    P.op(P.dve, [], [one32b], lambda: nc.vector.memset(one32[:], 1.0))
    P.op(P.dve, [], [one32b], lambda: nc.vector.memset(onesT[:], 1.0))
    P.op(P.dve, [pabuf], [pabuf],
         lambda: nc.vector.tensor_scalar(out=pa[:, 18:20], in0=pa[:, 18:20], scalar1=-1.0, scalar2=None,
                                         op0=ALU.mult))
    P.op(P.dve, [], [xbb[0], xbb[1]], lambda: nc.vector.memset(xb[:], 0.0))
    P.op(P.dve, [], [ubb[0], ubb[1]], lambda: nc.vector.memset(ub[:], 0.0))
    P.op(P.dve, [], [hcb[0], hcb[1]], lambda: nc.vector.memset(hcar[:], 0.0))
    P.op(P.dve, [], swb, lambda: nc.vector.memset(sw[:], 0.0))
    P.op(P.dve, [], [selb], lambda: nc.vector.memset(selT[:], 0.0))
    P.op(P.dve, [], [zbuf], lambda: nc.vector.memset(Z[:], 0.0))
    P.op(P.dve, [], [selb], lambda: nc.vector.memset(selT[0:1, :], 1.0))
    P.op(P.dve, [], [selb], lambda: nc.vector.memset(selT[32:33, :], 1.0))
    P.op(P.dve, [], [trib], lambda: nc.vector.memset(tri[:], 1.0))
    P.op(P.pool, [trib], [trib],
         lambda: nc.gpsimd.affine_select(out=tri[:], in_=tri[:], pattern=[[1, 128]], compare_op=ALU.is_ge,
                                         fill=0.0, base=0, channel_multiplier=-1))
    P.op(P.act, [pabuf], [cstbuf],
         lambda: nc.scalar.activation(out=cst[:, 0:2], in_=pa[:, 14:16], func=AF.Exp, scale=-1.0))
    P.op(P.act, [cstbuf], [cstbuf],
         lambda: nc.scalar.activation(out=cst[:, 0:2], in_=cst[:, 0:2], func=AF.Ln, bias=one32[:, 0:1]))
    P.op(P.dve, [cstbuf], [cstbuf],
         lambda: nc.vector.tensor_scalar(out=cst[:, 2:4], in0=cst[:, 0:2], scalar1=-16.0, scalar2=None,
                                         op0=ALU.mult))
    P.op(P.dve, [cstbuf], [cstbuf],
         lambda: nc.vector.tensor_scalar(out=cst[:, 0:2], in0=cst[:, 0:2], scalar1=-8.0, scalar2=None,
                                         op0=ALU.mult))

    def load_w2(hd):
        for k in range(KD):
            P.dma(P.pool, "cw2", [], [w2buf],
                  lambda k=k: nc.gpsimd.dma_start(out=w2[:, k, :], in_=w2d[hd * 128:(hd + 1) * 128, k, :],
                                                  max_dma_last_dim=4096))
    load_w2(0)

    def bank():
        b = state["bank"]
        state["bank"] = (b + 1) % 6
        return b

    def T():
        i = state["tmp"]
        state["tmp"] = (i + 1) % NTMP
        return i

    def gemm_chunk(wt, wbuf, c0, hj, bk):
        def emit():
            ins = None
            for k in range(KD):
                ins = nc.tensor.matmul(ps[:, bk, :], lhsT=wt[:, k, c0:c0 + 128], rhs=hT[:, hj, k, :],
                                       start=(k == 0), stop=(k == KD - 1))
            return ins
        P.op(P.pe, [wbuf] + hbufs[hj], [pb[bk]], emit)

    def store_y(which, row0, src_ap, src_buf, t0):
        i = state["yo"]
        state["yo"] = (i + 1) % 8
        P.dma(P.sp, "yo%d" % i, [src_buf], [yob[i]],
              lambda: nc.sync.dma_start(
                  out=yo[(t0 // NTOK) * 768 + which * 256 + row0:(t0 // NTOK) * 768 + which * 256 + row0 + 128,
                         (t0 % NTOK):(t0 % NTOK) + TT], in_=src_ap))

    def load_h(it, hj):
        r, off = divmod(it * TT, NTOK)
        for k in range(KD):
            P.dma(P.sp, "hl%d" % hj, [hfbuf], [hbufs[hj][k]],
                  lambda k=k: nc.sync.dma_start(out=hT[:, hj, k, :],
                                                in_=hfull[k * 512 + r * 128:k * 512 + (r + 1) * 128, off:off + TT]))

    for it in range(NT):
        t0 = it * TT
        hj = it % 2
        load_h(it, hj)

        for blk in range(2):
            bk = bank()
            gemm_chunk(w1, w1buf, blk * 128, hj, bk)
            P.op(P.act, [pb[bk]], [xbb[blk]],
                 lambda bk=bk: nc.scalar.activation(out=xb[:, blk, 3:TT + 3], in_=ps[:, bk, :], func=AF.Copy))
            u = T()
            P.op(P.act, [xbb[blk], pabuf], [tb[u]],
                 lambda: nc.scalar.activation(out=tmp[:, u, :], in_=xb[:, blk, 3:TT + 3], func=AF.Identity,
                                              scale=pa[:, blk * 4 + 3:blk * 4 + 4], bias=pa[:, 8 + blk:9 + blk]))
            for tap in (2, 1, 0):
                P.op(P.dve, [xbb[blk], pabuf, tb[u]], [tb[u]],
                     lambda tap=tap: nc.vector.scalar_tensor_tensor(
                         out=tmp[:, u, :], in0=xb[:, blk, tap:tap + TT], scalar=pa[:, blk * 4 + tap:blk * 4 + tap + 1],
                         in1=tmp[:, u, :], op0=ALU.mult, op1=ALU.add))
            P.op(P.dve, [xbb[blk]], [xbb[blk]],
                 lambda: nc.vector.tensor_copy(out=xb[:, blk, 0:3], in_=xb[:, blk, TT:TT + 3]))
            ubf = 0
            P.op(P.dve, [tb[u]], [tbb[ubf]], lambda: nc.vector.tensor_copy(out=tmpb[:, ubf, :], in_=tmp[:, u, :]))
            bkr, bki = bank(), bank()
            P.op(P.pe, [tbb[ubf], wrgbuf], [pb[bkr]],
                 lambda: nc.tensor.matmul(ps[:, bkr, :], lhsT=wrg[:, blk * 2, :], rhs=tmpb[:, ubf, :],
                                          start=True, stop=True))
            P.op(P.pe, [tbb[ubf], wrgbuf], [pb[bki]],
                 lambda: nc.tensor.matmul(ps[:, bki, :], lhsT=wrg[:, blk * 2 + 1, :], rhs=tmpb[:, ubf, :],
                                          start=True, stop=True))
            r, ig = T(), T()
            P.op(P.act, [pb[bkr], pabuf], [tb[r]],
                 lambda: nc.scalar.activation(out=tmp[:, r, :], in_=ps[:, bkr, :], func=AF.Sigmoid,
                                              bias=pa[:, 10 + blk:11 + blk]))
            P.op(P.act, [pb[bki], pabuf], [tb[ig]],
                 lambda: nc.scalar.activation(out=tmp[:, ig, :], in_=ps[:, bki, :], func=AF.Sigmoid,
                                              bias=pa[:, 12 + blk:13 + blk]))
            a, a2 = T(), T()
            P.op(P.act, [tb[r], cstbuf], [tb[a]],
                 lambda: nc.scalar.activation(out=tmp[:, a, :], in_=tmp[:, r, :], func=AF.Exp,
                                              scale=cst[:, blk:blk + 1]))
            P.op(P.act, [tb[r], cstbuf], [tb[a2]],
                 lambda: nc.scalar.activation(out=tmp[:, a2, :], in_=tmp[:, r, :], func=AF.Exp,
                                              scale=cst[:, 2 + blk:3 + blk]))
            P.op(P.dve, [tb[a2]], [tb[a2]],
                 lambda: nc.vector.tensor_scalar(out=tmp[:, a2, :], in0=tmp[:, a2, :], scalar1=-1.0, scalar2=1.0,
                                                 op0=ALU.mult, op1=ALU.add))
            P.op(P.act, [tb[a2]], [tb[a2]],
                 lambda: nc.scalar.activation(out=tmp[:, a2, :], in_=tmp[:, a2, :], func=AF.Sqrt))
            P.op(P.dve, [tb[ig], tb[u]], [tb[ig]],
                 lambda: nc.vector.tensor_tensor(out=tmp[:, ig, :], in0=tmp[:, ig, :], in1=tmp[:, u, :], op=ALU.mult))
            P.op(P.dve, [tb[ig], tb[a2]], [tb[ig]],
                 lambda: nc.vector.tensor_tensor(out=tmp[:, ig, :], in0=tmp[:, ig, :], in1=tmp[:, a2, :],
                                                 op=ALU.mult))
            P.op(P.dve, [tb[a], tb[ig], hcb[blk]], [tb[r]],
                 lambda: nc.vector.tensor_tensor_scan(out=tmp[:, r, :], data0=tmp[:, a, :], data1=tmp[:, ig, :],
                                                      initial=hcar[:, blk:blk + 1], op0=ALU.mult, op1=ALU.add))
            P.op(P.dve, [tb[r]], [hcb[blk]],
                 lambda: nc.vector.tensor_copy(out=hcar[:, blk:blk + 1], in_=tmp[:, r, TT - 1:TT]))
            bky = bank()
            gemm_chunk(w1, w1buf, 256 + blk * 128, hj, bky)
            z = T()
            P.op(P.act, [pb[bky]], [tb[z]],
                 lambda: nc.scalar.activation(out=tmp[:, z, :], in_=ps[:, bky, :], func=AF.Square))
            P.op(P.dve, [tb[z]], [tb[z]],
                 lambda: nc.vector.tensor_scalar(out=tmp[:, z, :], in0=tmp[:, z, :], scalar1=0.044715, scalar2=1.0,
                                                 op0=ALU.mult, op1=ALU.add))
            P.op(P.dve, [tb[z], pb[bky]], [tb[z]],
                 lambda: nc.vector.tensor_tensor(out=tmp[:, z, :], in0=ps[:, bky, :], in1=tmp[:, z, :], op=ALU.mult))
            P.op(P.act, [tb[z]], [tb[z]],
                 lambda: nc.scalar.activation(out=tmp[:, z, :], in_=tmp[:, z, :], func=AF.Sigmoid, scale=GELU_C))
            P.op(P.dve, [tb[z], pb[bky]], [tb[z]],
                 lambda: nc.vector.tensor_tensor(out=tmp[:, z, :], in0=ps[:, bky, :], in1=tmp[:, z, :], op=ALU.mult))
            yb_ = 1 + blk
            P.op(P.dve, [tb[z], tb[r]], [tbb[yb_]],
                 lambda: nc.vector.tensor_tensor(out=tmpb[:, yb_, :], in0=tmp[:, z, :], in1=tmp[:, r, :], op=ALU.mult))
            store_y(0, blk * 128, tmpb[:, yb_, :], tbb[yb_], t0)

        plb = []
        for cc in range(2):
            bk = bank()
            gemm_chunk(w1, w1buf, 512 + cc * 128, hj, bk)
            P.op(P.act, [pb[bk]], [ubb[cc]],
                 lambda bk=bk: nc.scalar.activation(out=ub[:, cc, 15:TT + 15], in_=ps[:, bk, :], func=AF.Copy))
            W_ = TT + 15
            P.op(P.dve, [ubb[cc]], [swb[0]],
                 lambda: nc.vector.tensor_tensor(out=sw[:, 0, 1:W_], in0=ub[:, cc, 1:W_], in1=ub[:, cc, 0:W_ - 1],
                                                 op=ALU.add))
            P.op(P.dve, [swb[0]], [swb[1]],
                 lambda: nc.vector.tensor_tensor(out=sw[:, 1, 3:W_], in0=sw[:, 0, 3:W_], in1=sw[:, 0, 1:W_ - 2],
                                                 op=ALU.add))
            P.op(P.dve, [swb[1]], [swb[2]],
                 lambda: nc.vector.tensor_tensor(out=sw[:, 2, 7:W_], in0=sw[:, 1, 7:W_], in1=sw[:, 1, 3:W_ - 4],
                                                 op=ALU.add))
            P.op(P.dve, [swb[2]], [swb[3]],
                 lambda: nc.vector.tensor_tensor(out=sw[:, 3, 15:W_], in0=sw[:, 2, 15:W_], in1=sw[:, 2, 7:W_ - 8],
                                                 op=ALU.add))
            rv = T()
            P.op(P.dve, [swb[0], ubb[cc], pabuf], [tb[rv]],
                 lambda: nc.vector.scalar_tensor_tensor(out=tmp[:, rv, :], in0=sw[:, 0, 15:W_], scalar=pa[:, 20:21],
                                                        in1=ub[:, cc, 15:W_], op0=ALU.mult, op1=ALU.subtract))
            for wi in (1, 2):
                P.op(P.dve, [swb[wi], tb[rv], pabuf], [tb[rv]],
                     lambda wi=wi: nc.vector.scalar_tensor_tensor(out=tmp[:, rv, :], in0=sw[:, wi, 15:W_],
                                                                  scalar=pa[:, 20 + wi:21 + wi], in1=tmp[:, rv, :],
                                                                  op0=ALU.mult, op1=ALU.add))
            pl = 2 + cc
            P.op(P.dve, [swb[3], tb[rv], pabuf], [tbb[pl]],
                 lambda: nc.vector.scalar_tensor_tensor(out=tmpb[:, pl, :], in0=sw[:, 3, 15:W_], scalar=pa[:, 23:24],
                                                        in1=tmp[:, rv, :], op0=ALU.mult, op1=ALU.add))
            if it == 0:
                P.op(P.dve, [swb[0], ptabbuf], [t16b[0]],
                     lambda: nc.vector.tensor_tensor(out=t16[:, 0, :], in0=sw[:, 0, 15:31], in1=ptab[:, 0, :],
                                                     op=ALU.mult))
                for wi in (1, 2, 3):
                    P.op(P.dve, [swb[wi], ptabbuf], [t16b[1]],
                         lambda wi=wi: nc.vector.tensor_tensor(out=t16[:, 1, :], in0=sw[:, wi, 15:31],
                                                               in1=ptab[:, wi, :], op=ALU.mult))
                    P.op(P.dve, [t16b[0], t16b[1]], [t16b[0]],
                         lambda: nc.vector.tensor_tensor(out=t16[:, 0, :], in0=t16[:, 0, :], in1=t16[:, 1, :],
                                                         op=ALU.add))
                P.op(P.dve, [t16b[0], ubb[cc]], [tbb[pl]],
                     lambda: nc.vector.tensor_tensor(out=tmpb[:, pl, 0:16], in0=t16[:, 0, :], in1=ub[:, cc, 15:31],
                                                     op=ALU.subtract))
            P.op(P.dve, [ubb[cc]], [ubb[cc]],
                 lambda: nc.vector.tensor_copy(out=ub[:, cc, 0:15], in_=ub[:, cc, TT:TT + 15]))
            plb.append(pl)
        for dc in range(2):
            bk = bank()
            def emit(bk=bk, dc=dc):
                ins = None
                for cc in range(2):
                    ins = nc.tensor.matmul(ps[:, bk, :], lhsT=wpl[:, cc, dc * 128:(dc + 1) * 128],
                                           rhs=tmpb[:, 2 + cc, :], start=(cc == 0), stop=(cc == 1))
                return ins
            P.op(P.pe, [tbb[2], tbb[3], wplbuf], [pb[bk]], emit)
            yi = dc
            pti = state["pt"]
            state["pt"] = (pti + 1) % 3
            P.op(P.act, [pb[bk], pabuf], [ptb[pti]],
                 lambda bk=bk, pti=pti, dc=dc: nc.scalar.activation(out=pT[:, pti, :], in_=ps[:, bk, :],
                                                                    func=AF.Identity,
                                                                    scale=pa[:, 16 + dc:17 + dc]))
            store_y(2, dc * 128, pT[:, pti, :], ptb[pti], t0)

    d["after_part"]((0, 1, 4, 5))
    SB0, SB1, OB, SMB, QKB, VB, FB, SMALL = 0, 1, 2, 3, 4, 5, 6, 7
    scale = 1.0 / SQRT_DH
    for hd in range(2):
        if hd == 1:
            load_w2(1)
            d["after_part"]((2,))
        P.op(P.dve, [], [fcb[0]], lambda: nc.vector.memset(fc[:, 0:1], 0.0))
        for it in range(NT):
            t0 = it * TT
            hj = it % 2
            load_h(it, hj)
            def emit_f():
                ins = None
                for k in range(KD):
                    ins = nc.tensor.matmul(ps[0:64, FB, :], lhsT=w2[:, k, 384:448], rhs=hT[:, hj, k, :],
                                           start=(k == 0), stop=(k == KD - 1))
                return ins
            P.op(P.pe, [w2buf] + hbufs[hj], [pb[FB]], emit_f)
            e, fa, fr = T(), T(), T()
            P.op(P.act, [pb[FB], pabuf], [tb[e]],
                 lambda: nc.scalar.activation(out=tmp[0:64, e, :], in_=ps[0:64, FB, :], func=AF.Exp, scale=-1.0,
                                              bias=pa[0:64, 18 + hd:19 + hd]))
            P.op(P.act, [tb[e], one32b], [tb[e]],
                 lambda: nc.scalar.activation(out=tmp[0:64, e, :], in_=tmp[0:64, e, :], func=AF.Ln,
                                              bias=one32[0:64, 0:1]))
            fi, fo = it % 2, (it + 1) % 2
            P.op(P.dve, [tb[e], one32b, fcb[fi]], [tb[fa]],
                 lambda: nc.vector.tensor_tensor_scan(out=tmp[0:64, fa, :], data0=onesT[:, :],
                                                      data1=tmp[0:64, e, :], initial=fc[:, fi:fi + 1],
                                                      op0=ALU.mult, op1=ALU.subtract))
            P.op(P.dve, [tb[fa]], [fcb[fo]],
                 lambda: nc.vector.tensor_copy(out=fc[:, fo:fo + 1], in_=tmp[0:64, fa, TT - 1:TT]))
            P.op(P.dve, [tb[fa], fcb[fi]], [tb[fr]],
                 lambda: nc.vector.tensor_scalar(out=tmp[0:64, fr, :], in0=tmp[0:64, fa, :], scalar1=fc[:, fi:fi + 1],
                                                 scalar2=SQRT_DH, op0=ALU.subtract, op1=ALU.mult))
            P.op(P.dve, [tb[fr]], [zbuf], lambda: nc.vector.tensor_copy(out=Z[0:64, :], in_=tmp[0:64, fr, :]))
            P.op(P.dve, [tb[fr], zbuf], [tb[fr]],
                 lambda: nc.vector.tensor_tensor(out=tmp[32:64, fr, :], in0=tmp[32:64, fr, :], in1=Z[32:64, :],
                                                 op=ALU.subtract))
            P.op(P.dve, [tb[fr]], [zbuf], lambda: nc.vector.tensor_copy(out=Z[32:64, :], in_=tmp[32:64, fr, :]))
            qi = 0
            gemm_chunk(w2, w2buf, 0, hj, QKB)
            P.op(P.act, [pb[QKB]], [tbb[qi]],
                 lambda: nc.scalar.activation(out=tmpb[:, qi, :], in_=ps[:, QKB, :], func=AF.Copy))
            gemm_chunk(w2, w2buf, 128, hj, QKB)
            P.op(P.act, [pb[QKB]], [ktb[it]],
                 lambda: nc.scalar.activation(out=KT[:, t0:t0 + TT], in_=ps[:, QKB, :], func=AF.Copy))
            def emit_v():
                ins = None
                for tbk in range(4):
                    for k in range(KD):
                        ins = nc.tensor.matmul(ps[:, VB, tbk * 128:(tbk + 1) * 128],
                                               lhsT=hT[:, hj, k, tbk * 128:(tbk + 1) * 128], rhs=w2[:, k, 256:384],
                                               start=(k == 0), stop=(k == KD - 1))
                return ins
            P.op(P.pe, [w2buf] + hbufs[hj], [pb[VB]], emit_v)
            P.op(P.act, [pb[VB]], [vb[it]],
                 lambda: nc.scalar.activation(out=V[:, it * 4:(it + 1) * 4, :],
                                              in_=ps[:, VB, :].rearrange("p (a b) -> p a b", a=4), func=AF.Copy))
            def emit_small():
                ins = None
                for tbk in range(4):
                    ins = nc.tensor.matmul(ps[:, SMALL, tbk:tbk + 1], lhsT=tmp[0:1, fa, tbk * 128:(tbk + 1) * 128],
                                           rhs=one32[0:1, 0:1], start=True, stop=True)
                ins = nc.tensor.matmul(ps[:, SMALL, 4:5], lhsT=one32[0:1, :], rhs=fc[0:1, fi:fi + 1],
                                       start=True, stop=True)
                return ins
            P.op(P.pe, [tb[fa], one32b, fcb[fi]], [pb[SMALL]], emit_small)
            P.op(P.dve, [pb[SMALL]], [nfb[it]],
                 lambda: nc.vector.tensor_scalar(out=negF[:, it * 4:(it + 1) * 4], in0=ps[:, SMALL, 0:4],
                                                 scalar1=-1.0, scalar2=None, op0=ALU.mult))
            P.op(P.dve, [pb[SMALL]], [ctbuf], lambda: nc.vector.tensor_copy(out=ct[:, :], in_=ps[:, SMALL, 4:5]))
            nkb = (it + 1) * 4
            P.op(P.dve, nfb[0:it + 1] + [ctbuf], [bmbuf],
                 lambda: nc.vector.tensor_scalar(out=bm[:, 0:nkb], in0=negF[:, 0:nkb], scalar1=ct[:, 0:1],
                                                 scalar2=None, op0=ALU.add))
            SBANKS = [SB0, SB1, QKB, VB]
            PRE = 3

            def issue_s(kb):
                j = kb - it * 4
                c0 = max(j, 0) * 128
                sbk = SBANKS[kb % 4]

                def emit_s():
                    nc.tensor.matmul(ps[:, sbk, c0:TT], lhsT=KT[:, kb * 128:(kb + 1) * 128], rhs=tmpb[:, qi, c0:TT],
                                     start=True, stop=False)
                    return nc.tensor.matmul(ps[:, sbk, c0:TT], lhsT=selT[:, :], rhs=Z[:, c0:TT],
                                            start=False, stop=True)
                P.op(P.pe, [ktb[kb // 4], tbb[qi], selb, zbuf], [pb[sbk]], emit_s)

            for kb in range(min(PRE, nkb)):
                issue_s(kb)
            for kb in range(nkb):
                if kb + PRE < nkb:
                    issue_s(kb + PRE)
                j = kb - it * 4
                c0 = max(j, 0) * 128
                sbk = SBANKS[kb % 4]
                pti = state["pt"]
                state["pt"] = (pti + 1) % 3
                P.op(P.act, [pb[sbk], bmbuf], [ptb[pti]],
                     lambda: nc.scalar.activation(out=pT[:, pti, c0:TT], in_=ps[:, sbk, c0:TT], func=AF.Exp,
                                                  scale=scale, bias=bm[:, kb:kb + 1]))
                if j >= 0:
                    P.op(P.dve, [ptb[pti], trib], [ptb[pti]],
                         lambda: nc.vector.tensor_tensor(out=pT[:, pti, c0:c0 + 128], in0=pT[:, pti, c0:c0 + 128],
                                                         in1=tri[:, :], op=ALU.mult))

                def emit_o():
                    nc.tensor.matmul(ps[:, OB, c0:TT], lhsT=V[:, kb, :], rhs=pT[:, pti, c0:TT],
                                     start=(kb == 0), stop=(kb == nkb - 1))
                    return nc.tensor.matmul(ps[:, SMB, c0:TT], lhsT=ones_bf[:, :], rhs=pT[:, pti, c0:TT],
                                            start=(kb == 0), stop=(kb == nkb - 1))
                P.op(P.pe, [vb[kb // 4], ptb[pti], onesb], [pb[OB], pb[SMB]], emit_o)
            rc = T()
            P.op(P.dve, [pb[SMB]], [tb[rc]], lambda: nc.vector.reciprocal(out=tmp[:, rc, :], in_=ps[:, SMB, :]))
            yi = 1 + it % 2
            P.op(P.dve, [pb[OB], tb[rc]], [tbb[yi]],
                 lambda: nc.vector.tensor_tensor(out=tmpb[:, yi, :], in0=ps[:, OB, :], in1=tmp[:, rc, :], op=ALU.mult))
            store_y(1, hd * 128, tmpb[:, yi, :], tbb[yi], t0)
    d["after_part"]((3,))
    P.pop_scope()


POOL_WINDOWS = (2, 4, 8, 16)


def prep_A_inputs(l, b, j, xT_b, g_mix, w_in, b_forget, conv_w, conv_b, w_rg, b_rg, w_ig, b_ig, lru_lambda,
                  w_pool, pool_scale):
    W = w_in[l]

    def arr(cols):
        return np.ascontiguousarray(cols.reshape(16, 128, cols.shape[1]).transpose(1, 0, 2))
    w1 = np.concatenate([W[:, 2 * j * 128:(2 * j + 2) * 128], W[:, 1024 + 2 * j * 128:1024 + (2 * j + 2) * 128],
                         W[:, 5128 + j * 256:5128 + (j + 1) * 256]], axis=1)
    w2 = []
    for hd in range(2):
        h = 2 * j + hd
        w2.append(arr(np.concatenate([W[:, 2048 + h * 128:2048 + (h + 1) * 128],
                                      W[:, 3072 + h * 128:3072 + (h + 1) * 128],
                                      W[:, 4096 + h * 128:4096 + (h + 1) * 128],
                                      np.repeat(W[:, 5120 + h:5121 + h], 64, axis=1)], axis=1)))
    wrg = np.stack([w_rg[l][2 * j], w_ig[l][2 * j], w_rg[l][2 * j + 1], w_ig[l][2 * j + 1]], axis=1)
    wpl = np.ascontiguousarray(w_pool[l][j].reshape(2, 128, 256).transpose(1, 0, 2))
    pa = np.zeros((128, NPAR), np.float32)
    for blk in range(2):
        ch = slice((2 * j + blk) * 128, (2 * j + blk + 1) * 128)
        for tap in range(4):
            pa[:, blk * 4 + tap] = conv_w[l][tap, ch]
        pa[:, 8 + blk] = conv_b[l][ch]
        pa[:, 10 + blk] = b_rg[l][ch]
        pa[:, 12 + blk] = b_ig[l][ch]
        pa[:, 14 + blk] = lru_lambda[l][ch]
        pa[:, 16 + blk] = pool_scale[l][j * 256 + blk * 128:j * 256 + (blk + 1) * 128]
        pa[:, 18 + blk] = b_forget[l][2 * j + blk]
    ptab = np.zeros((128, 4, 16), np.float32)
    for wi, w in enumerate(POOL_WINDOWS):
        if wi == j:
            pa[:, 20 + wi] = 1.0 / w
            ptab[:, wi, :] = 1.0 / np.minimum(np.arange(1, 17), w)
    return dict(w1=arr(w1), w2=np.stack(w2),
                wrg=np.ascontiguousarray(wrg), wpl=wpl, pa=pa, ptab=ptab)


GROUPS = [[0, 1, 2, 3], [4, 5, 6, 7]]


def build_fused(S, B_TT):
    P = Prog()
    nc = P.nc
    NTOK = S // 4
    ntt = NTOK // B_TT
    ps = P.psum("ps", [128, 8, 512], F32)
    pb = [Buf("p%d" % i) for i in range(8)]
    xT = P.dram("xT", [D, NTOK], F32, "ExternalInput")
    gm0 = P.dram("gm0", [128, 16], F32, "ExternalInput")
    ptab = P.dram("ptab", [128, 4, 16], F32, "ExternalInput")
    oT = P.dram("oT", [D, NTOK], F32, "ExternalOutput")
    A_in, B_in = [], []
    for l in range(2):
        A_in.append(dict(
            w1=P.dram("w1_%d" % l, [128, 16, 768], F32, "ExternalInput"),
            w2=P.dram("w2_%d" % l, [256, 16, 448], F32, "ExternalInput"),
            wrg=P.dram("wrg_%d" % l, [128, 4, 128], F32, "ExternalInput"),
            wpl=P.dram("wpl_%d" % l, [128, 2, 256], F32, "ExternalInput"),
            pa=P.dram("pa_%d" % l, [128, NPAR], F32, "ExternalInput"),
            ptab=ptab))
        B_in.append(dict(
            wg=P.dram("wg_%d" % l, [48, 128, 16, 128], F32, "ExternalInput"),
            wbr=P.dram("wbr_%d" % l, [48, 128, 8, 128], F32, "ExternalInput"),
            wo=P.dram("wo_%d" % l, [16, 128, 16, 128], F32, "ExternalInput"),
            wf1=P.dram("wf1_%d" % l, [88, 128, 16, 128], F32, "ExternalInput"),
            wf2=P.dram("wf2_%d" % l, [16, 128, 44, 128], F32, "ExternalInput"),
            gv=P.dram("gv_%d" % l, [128, 3, 16], F32, "ExternalInput")))
    hloc = [P.dram("hloc%d" % l, [D, NTOK], BF16) for l in range(2)]
    hfull = [P.dram("hfull%d" % l, [4 * D, NTOK], BF16) for l in range(2)]
    yo = [P.dram("yo%d" % l, [4 * 768, NTOK], BF16) for l in range(2)]
    yfull = [P.dram("yfull%d" % l, [4 * 4 * 768, NTOK], BF16) for l in range(2)]
    x1T = [P.dram("x1T%d" % l, [D, NTOK], F32) for l in range(2)]
    x2T = [P.dram("x2T%d" % l, [D, NTOK], F32) for l in range(2)]
    ysl = [P.dram("ysl%d" % l, [4 * 768, NTOK], BF16) for l in range(2)]
    qreg = nc.sync.partition_id() % 4

    def mk():
        return [[Buf("b") for _ in range(16)] for _ in range(ntt)]

    emit_N(P, ps, pb, NTOK, B_TT, xT, gm0, hloc[0], mk())
    x2prev = None
    outb = None
    for l in range(2):
        hfbuf = Buf("hf%d" % l)
        for k in range(KD):
            P.cc([], [hfbuf], lambda k=k: nc.gpsimd.collective_compute(
                "AllGather", ALU.bypass, replica_groups=GROUPS, ins=[hloc[l][k * 128:(k + 1) * 128, :]],
                outs=[hfull[l][k * 512:(k + 1) * 512, :]]))
        ybuf = Buf("yf%d" % l)

        def after_part(rbs, l=l, ybuf=ybuf):
            P._wait(P.pool, {S_.sem: S_.n for nm, S_ in P.dmaq.items() if nm.startswith("yo") and S_.n > 0})
            for q in range(4):
                for rb in rbs:
                    qb = q * 6 + rb
                    P.cc([], [ybuf], lambda qb=qb: nc.gpsimd.collective_compute(
                        "AllGather", ALU.bypass, replica_groups=GROUPS, ins=[yo[l][qb * 128:(qb + 1) * 128, :]],
                        outs=[yfull[l][qb * 512:(qb + 1) * 512, :]]))
        emit_A(P, ps, pb, S, dict(A_in[l], hfull=hfull[l], hfbuf=hfbuf, yo=yo[l], after_part=after_part))
        yslb = Buf("ysl%d" % l)
        for r in range(4):
            P.dma(P.sp, "ysl", [ybuf], [yslb],
                  lambda r=r: nc.sync.dma_start(out=ysl[l][r * 768:(r + 1) * 768, :],
                                                in_=yfull[l][ds(qreg * 3072 + r * 768, 768), :]))
        x2b = mk()
        outb = mk()
        dd = dict(B_in[l], xsrc=(xT if l == 0 else x2T[0]),
                  xsrc_buf=((lambda tt, k: None) if l == 0 else (lambda tt, k, xp=x2prev: xp[tt][k])),
                  yfull=ysl[l], ybuf=yslb, x1T=x1T[l], x2T=x2T[l], dst=(hloc[1] if l == 0 else oT),
                  x2bufs=x2b, outbufs=outb)
        emit_B(P, ps, pb, NTOK, B_TT, dd, last=(l == 1))
        x2prev = x2b
    P.finish([b for row in outb for b in row])
    return P


_CACHE = {}
B_TT_FULL = 1024


def prep_inputs(x, g_mix, w_in, b_forget, conv_w, conv_b, w_rg, b_rg, w_ig, b_ig, lru_lambda, w_pool, pool_scale,
                w_branch_rnn, w_branch_attn, w_branch_pool, w_out, g_ffn, w_ffn_in, w_ffn_out, g_final):
    nb, S, _ = x.shape
    NTOK = S // 4
    BW = []
    for l in range(2):
        g_next = g_mix[1] if l == 0 else g_final
        W = prep_B_weights(l, w_in, w_branch_rnn, w_branch_attn, w_branch_pool, w_out, w_ffn_in, w_ffn_out,
                           g_mix, g_ffn, g_next)
        BW.append({"%s_%d" % (k, l): v for k, v in W.items()})
    in_maps = []
    for c in range(NCORES):
        b, j = c // 4, c % 4
        m = {"xT": np.ascontiguousarray(x[b, j * NTOK:(j + 1) * NTOK, :].T),
             "gm0": np.ascontiguousarray(g_mix[0].reshape(16, 128).T)}
        for l in range(2):
            a = prep_A_inputs(l, b, j, None, g_mix, w_in, b_forget, conv_w, conv_b, w_rg, b_rg, w_ig, b_ig,
                              lru_lambda, w_pool, pool_scale)
            m["ptab"] = a["ptab"]
            m["w1_%d" % l] = a["w1"]
            m["w2_%d" % l] = np.ascontiguousarray(a["w2"].reshape(256, 16, 448))
            m["wrg_%d" % l] = a["wrg"]
            m["wpl_%d" % l] = a["wpl"]
            m["pa_%d" % l] = a["pa"]
            m.update(BW[l])
        in_maps.append(m)
    return in_maps


def kernel(x, g_mix, w_in, b_forget, conv_w, conv_b, w_rg, b_rg, w_ig, b_ig, lru_lambda, w_pool, pool_scale,
           w_branch_rnn, w_branch_attn, w_branch_pool, w_out, g_ffn, w_ffn_in, w_ffn_out, g_final):
    args = [np.asarray(a, dtype=np.float32) for a in (
        x, g_mix, w_in, b_forget, conv_w, conv_b, w_rg, b_rg, w_ig, b_ig, lru_lambda, w_pool, pool_scale,
        w_branch_rnn, w_branch_attn, w_branch_pool, w_out, g_ffn, w_ffn_in, w_ffn_out, g_final)]
    x = args[0]
    nb, S, _ = x.shape
    NTOK = S // 4
    tt = B_TT_FULL if NTOK % B_TT_FULL == 0 else 512
    key = (S, tt)
    if key not in _CACHE:
        _CACHE[key] = build_fused(S, tt)
    P = _CACHE[key]
    in_maps = prep_inputs(*args)
    res = run_bass_kernel_spmd(P.nc, in_maps, core_ids=list(range(NCORES)))
    out = np.empty((nb, S, D), np.float32)
    for c in range(NCORES):
        b, j = c // 4, c % 4
        out[b, j * NTOK:(j + 1) * NTOK, :] = res.results[c]["oT"].T
    return out
```

```python
import contextlib
import numpy as np
import ml_dtypes
import concourse.bass as bass
import concourse.mybir as mybir
from concourse.bass import ds
from concourse.bass_utils import run_bass_kernel_spmd

F32 = mybir.dt.float32
BF16 = mybir.dt.bfloat16
AF = mybir.ActivationFunctionType
ALU = mybir.AluOpType

D = 2048
KD = D // 128
DFF = 5632
KFF = DFF // 128
EPS = 1e-6
NCORES = 8
SEQ = 16384
SAME_ENGINE_SYNC = True
SEM_LIMIT = 30000


class Eng:
    def __init__(self, P, name, h, step):
        self.P, self.name, self.h, self.step = P, name, h, step
        self.n = 0
        self.sem = P.new_sem(name)
        self.nsem = 0
        self.waited = {}

    def tick(self):
        if self.n >= SEM_LIMIT * self.step:
            self.nsem += 1
            self.sem = self.P.new_sem("%s_%d" % (self.name, self.nsem))
            self.n = 0
        self.n += self.step
        return (self.sem, self.n)


class Buf:
    __slots__ = ("name", "writer", "readers")

    def __init__(self, name):
        self.name = name
        self.writer = None
        self.readers = {}


class Prog:
    def __init__(self):
        self.nc = bass.Bass("TRN2", target_bir_lowering=False)
        self.stack = contextlib.ExitStack()
        self.nsem_total = 0
        nc = self.nc
        self.pe = Eng(self, "pe", nc.tensor, 1)
        self.act = Eng(self, "act", nc.scalar, 1)
        self.dve = Eng(self, "dve", nc.vector, 1)
        self.pool = Eng(self, "pool", nc.gpsimd, 1)
        self.sp = Eng(self, "sp", nc.sync, 1)
        self.dmaq = {}
        self.scopes = []
        self.nname = 0
        self.xr = 0

    def new_sem(self, name):
        self.nsem_total += 1
        return self.stack.enter_context(self.nc.semaphore("s_%s_%d" % (name, self.nsem_total)))

    def sbuf(self, name, shape, dt):
        st = self.scopes[-1] if self.scopes else self.stack
        self.nname += 1
        return st.enter_context(self.nc.sbuf_tensor("%s_%d" % (name, self.nname), shape, dt))

    def push_scope(self):
        self.scopes.append(contextlib.ExitStack())

    def pop_scope(self):
        self.barrier()
        self.scopes.pop().close()

    def barrier(self):
        engs = [self.pe, self.act, self.dve, self.pool, self.sp]
        targets = {}
        for E in engs + list(self.dmaq.values()):
            if E.n > 0:
                targets[E.sem] = E.n
        for E in engs:
            self._wait(E, {s: n for s, n in targets.items() if s is not E.sem})

    def cc(self, reads, writes, emit):
        if "cc" not in self.dmaq:
            self.dmaq["cc"] = Eng(self, "cc", None, 1)
        S = self.dmaq["cc"]
        self._wait(self.pool, self._deps(reads, writes))
        ins = emit()
        tok = S.tick()
        ins.then_inc(tok[0])
        for b in reads:
            b.readers[id(S)] = tok
        for b in writes:
            b.writer = tok
            b.readers = {}
        return tok

    def psum(self, name, shape, dt=F32):
        return self.stack.enter_context(self.nc.psum_tensor(name, shape, dt))

    def dram(self, name, shape, dt, kind="Internal"):
        return self.nc.dram_tensor(name, shape, dt, kind=kind).ap()

    def _deps(self, reads, writes):
        deps = {}
        for b in reads:
            if b.writer is not None:
                s, n = b.writer
                deps[s] = max(deps.get(s, 0), n)
        for b in writes:
            if b.writer is not None:
                s, n = b.writer
                deps[s] = max(deps.get(s, 0), n)
            for (s, n) in b.readers.values():
                deps[s] = max(deps.get(s, 0), n)
        return deps

    def _wait(self, E, deps, skip_sem=None):
        for s, n in deps.items():
            if s is skip_sem:
                continue
            if E.waited.get(s, 0) < n:
                E.h.wait_ge(s, n)
                E.waited[s] = n

    def op(self, E, reads, writes, emit):
        deps = self._deps(reads, writes)
        skip = None
        if E is self.pe or not SAME_ENGINE_SYNC:
            skip = E.sem
        self._wait(E, deps, skip)
        ins = emit()
        tok = E.tick()
        ins.then_inc(tok[0], 1)
        for b in reads:
            b.readers[id(E)] = tok
        for b in writes:
            b.writer = tok
            b.readers = {}
        return tok

    def dma(self, Q, slot, reads, writes, emit):
        if slot not in self.dmaq:
            self.dmaq[slot] = Eng(self, "d" + slot, None, 16)
        S = self.dmaq[slot]
        deps = self._deps(reads, writes)
        self._wait(Q, deps)
        ins = emit()
        tok = S.tick()
        ins.then_inc(tok[0], 16)
        for b in reads:
            b.readers[id(S) + len(b.readers) * 0] = tok
        for b in writes:
            b.writer = tok
            b.readers = {}
        return tok

    def finish(self, bufs):
        deps = {}
        for b in bufs:
            if b.writer is not None:
                s, n = b.writer
                deps[s] = max(deps.get(s, 0), n)
        self._wait(self.sp, deps)
        self.stack.close()


class WRing:
    def __init__(self, P, nslots):
        self.P = P
        self.n = nslots
        self.t = P.sbuf("wring", [128, nslots, 16, 128], BF16)
        self.bufs = [Buf("w%d" % i) for i in range(nslots)]
        P.wring_gen = getattr(P, "wring_gen", 0) + 1
        self.gen = P.wring_gen
        self.i = 0

    def load(self, src, kc):
        P = self.P
        i = self.i
        self.i = (self.i + 1) % self.n
        dst = self.t[:, i, 0:kc, :]
        b = self.bufs[i]
        P.dma(P.pool, "w%d_%d" % (self.gen, i), [], [b],
              lambda: P.nc.gpsimd.dma_start(out=dst, in_=src, max_dma_last_dim=4096))
        return self.t[:, i], b


def emit_norm(P, C, src_chunk, src_buf, gcol, out_fn, TT):
    nc = P.nc
    nsub = TT // 512
    xring, xbufs, sq, sqbufs, ps, pb, rstd, rstdbuf, ones_bf, onesb = (
        C["xring"], C["xbufs"], C["sq"], C["sqbufs"], C["ps"], C["pb"], C["rstd"], C["rstdbuf"], C["ones_bf"],
        C["onesb"])
    nx = len(xbufs)
    for k in range(KD):
        xi = P.xr
        P.xr = (P.xr + 1) % nx
        sb = src_buf(k)
        P.dma(P.sp, "x%d" % xi, [sb] if sb else [], [xbufs[xi]],
              lambda xi=xi, k=k: nc.sync.dma_start(out=xring[:, xi, :], in_=src_chunk(k)))
        si = k % 2
        P.op(P.act, [xbufs[xi]], [sqbufs[si]],
             lambda xi=xi, si=si: nc.scalar.activation(out=sq[:, si, :], in_=xring[:, xi, :], func=AF.Square))
        for s in range(nsub):
            P.op(P.pe, [sqbufs[si], onesb], [pb[6 + s]],
                 lambda s=s, si=si, k=k: nc.tensor.matmul(ps[:, 6 + s, :], lhsT=ones_bf[:, :],
                                                          rhs=sq[:, si, s * 512:(s + 1) * 512],
                                                          start=(k == 0), stop=(k == KD - 1)))
    for s in range(nsub):
        P.op(P.act, [pb[6 + s]], [rstdbuf],
             lambda s=s: nc.scalar.activation(out=rstd[:, s * 512:(s + 1) * 512], in_=ps[:, 6 + s, :],
                                              func=AF.Sqrt, scale=1.0 / D, bias=C["eps"][:, 0:1]))
    P.op(P.dve, [rstdbuf], [rstdbuf], lambda: nc.vector.reciprocal(out=rstd[:, :], in_=rstd[:, :]))
    for k in range(KD):
        xi = P.xr
        P.xr = (P.xr + 1) % nx
        sb = src_buf(k)
        P.dma(P.sp, "x%d" % xi, [sb] if sb else [], [xbufs[xi]],
              lambda xi=xi, k=k: nc.sync.dma_start(out=xring[:, xi, :], in_=src_chunk(k)))
        out_fn(k, xring[:, xi, :], gcol[:, k:k + 1], rstd[:, :], [xbufs[xi], rstdbuf])


def common_setup(P, TT, ps, pb, NX=4):
    nc = P.nc
    C = {}
    C["ones_bf"] = P.sbuf("ones_bf", [128, 128], BF16)
    C["onesb"] = Buf("ones")
    C["eps"] = P.sbuf("eps", [128, 1], F32)
    C["xring"] = P.sbuf("xring", [128, NX, TT], F32)
    C["xbufs"] = [Buf("x%d" % i) for i in range(NX)]
    C["sq"] = P.sbuf("sq", [128, 2, TT], BF16)
    C["sqbufs"] = [Buf("sq0"), Buf("sq1")]
    C["rstd"] = P.sbuf("rstd", [128, TT], F32)
    C["rstdbuf"] = Buf("rstd")
    C["ps"] = ps
    C["pb"] = pb
    P.xr = 0
    P.op(P.dve, [], [C["onesb"]], lambda: nc.vector.memset(C["ones_bf"][:], 1.0))
    P.op(P.dve, [], [C["rstdbuf"]], lambda: nc.vector.memset(C["eps"][:], EPS))
    return C


def emit_B(P, ps_, pb_, NTOK, TT, d, last):
    nc = P.nc
    nsub = TT // 512
    xT, xsrc_buf = d["xsrc"], d["xsrc_buf"]
    yfull, ybuf = d["yfull"], d["ybuf"]
    wg, wbr, wo, wf1, wf2, gv = d["wg"], d["wbr"], d["wo"], d["wf1"], d["wf2"], d["gv"]
    x1T, x2T, dst = d["x1T"], d["x2T"], d["dst"]
    ntt = NTOK // TT
    x1bufs = [[Buf("x1_%d_%d" % (t, n)) for n in range(16)] for t in range(ntt)]
    x2bufs = d["x2bufs"]
    outbufs = d["outbufs"]
    final_norm = True

    P.push_scope()
    C = common_setup(P, TT, ps_, pb_)
    ps, pb, xring, xbufs = C["ps"], C["pb"], C["xring"], C["xbufs"]
    NX = len(xbufs)
    g_sb = P.sbuf("g_sb", [128, 3, 16], F32)
    gbuf = Buf("g")
    hT = P.sbuf("hT", [128, KD, TT], BF16)
    hbufs = [Buf("h%d" % k) for k in range(KD)]
    R = P.sbuf("R", [128, KFF, TT], BF16)
    rb = [Buf("r%d" % k) for k in range(KFF)]
    tmp = P.sbuf("tmp", [128, 3, TT], F32)
    tb = [Buf("t%d" % i) for i in range(3)]
    ost = P.sbuf("ost", [128, 2, TT], F32)
    ob = [Buf("o%d" % i) for i in range(2)]
    hst = P.sbuf("hst", [128, 2, TT], BF16)
    hsb_ = [Buf("hst%d" % i) for i in range(2)]
    wr = WRing(P, 8)

    P.dma(P.sp, "g", [], [gbuf], lambda: nc.sync.dma_start(out=g_sb[:], in_=gv))

    state = {"pset": 0, "oi": 0}

    def v3(ap2d):
        return ap2d.rearrange("p (s t) -> p s t", s=nsub)

    def pview(pset):
        return ps[:, pset * 2:pset * 2 + nsub, :]

    def pbufs(pset):
        return pb[pset * 2:pset * 2 + nsub]

    def gemm(wsrcs, kc_list, rhs_fn, rhs_bufs):
        pset = state["pset"]
        state["pset"] ^= 1
        slots = [wr.load(src, kc) for src, kc in zip(wsrcs, kc_list)]
        ktot = sum(kc_list)
        for s in range(nsub):
            def emit(s=s):
                kk = 0
                ins = None
                for (wt, wbuf), kc in zip(slots, kc_list):
                    for k in range(kc):
                        ins = nc.tensor.matmul(ps[:, pset * 2 + s, :], lhsT=wt[:, k, :], rhs=rhs_fn(kk, s),
                                               start=(kk == 0), stop=(kk == ktot - 1))
                        kk += 1
                return ins
            P.op(P.pe, [wb_ for (_, wb_) in slots] + rhs_bufs, [pb[pset * 2 + s]], emit)
        return pset

    def h_rhs(kk, s):
        return hT[:, kk, s * 512:(s + 1) * 512]

    def R_rhs(kk, s):
        return R[:, kk, s * 512:(s + 1) * 512]

    def norm_to_h(k, x_ap, g_ap, rstd_ap, reads):
        P.op(P.dve, reads + [gbuf], [hbufs[k]],
             lambda: nc.vector.scalar_tensor_tensor(out=hT[:, k, :], in0=x_ap, scalar=g_ap, in1=rstd_ap,
                                                    op0=ALU.mult, op1=ALU.mult))

    for tt in range(ntt):
        t0 = tt * TT
        emit_norm(P, C, lambda k: xT[k * 128:(k + 1) * 128, t0:t0 + TT], lambda k: xsrc_buf(tt, k), g_sb[:, 0, :],
                  norm_to_h, TT)
        for i in range(3):
            for k in range(8):
                r = 16 + i * 8 + k
                row0 = ((i * 2 + k % 2) * 4 + k // 2) * 128
                P.dma(P.sp, "y%d" % (i * 8 + k), [ybuf], [rb[r]],
                      lambda r=r, row0=row0: nc.sync.dma_start(
                          out=R[:, r, :], in_=yfull[row0:row0 + 128, t0:t0 + TT]))
        for m in range(16):
            for i in range(3):
                pset = gemm([wg[i * 16 + m]], [16], h_rhs, hbufs)
                P.op(P.act, pbufs(pset), [tb[0]],
                     lambda pset=pset: nc.scalar.activation(out=v3(tmp[:, 0, :]), in_=pview(pset), func=AF.Sigmoid))
                pset = gemm([wbr[i * 16 + m]], [8],
                            lambda kk, s, i=i: R[:, 16 + i * 8 + kk, s * 512:(s + 1) * 512],
                            rb[16 + i * 8:16 + i * 8 + 8])
                dsti = 1 if i == 0 else 2
                P.op(P.dve, pbufs(pset) + [tb[0]], [tb[dsti]],
                     lambda pset=pset, dsti=dsti: nc.vector.tensor_tensor(
                         out=v3(tmp[:, dsti, :]), in0=pview(pset), in1=v3(tmp[:, 0, :]), op=ALU.mult))
                if i == 1:
                    P.op(P.dve, [tb[1], tb[2]], [tb[1]],
                         lambda: nc.vector.tensor_tensor(out=tmp[:, 1, :], in0=tmp[:, 1, :], in1=tmp[:, 2, :],
                                                         op=ALU.add))
                elif i == 2:
                    P.op(P.dve, [tb[1], tb[2]], [rb[m]],
                         lambda m=m: nc.vector.tensor_tensor(out=R[:, m, :], in0=tmp[:, 1, :], in1=tmp[:, 2, :],
                                                             op=ALU.add))
        for n in range(16):
            pset = gemm([wo[n]], [16], R_rhs, rb[0:16])
            xi = P.xr
            P.xr = (P.xr + 1) % NX
            xsb = xsrc_buf(tt, n)
            P.dma(P.sp, "x%d" % xi, [xsb] if xsb else [], [xbufs[xi]],
                  lambda xi=xi, n=n: nc.sync.dma_start(out=xring[:, xi, :],
                                                       in_=xT[n * 128:(n + 1) * 128, t0:t0 + TT]))
            oi = state["oi"]
            state["oi"] ^= 1
            P.op(P.dve, pbufs(pset) + [xbufs[xi]], [ob[oi]],
                 lambda pset=pset, xi=xi, oi=oi: nc.vector.tensor_tensor(
                     out=v3(ost[:, oi, :]), in0=pview(pset), in1=v3(xring[:, xi, :]), op=ALU.add))
            P.dma(P.sp, "x1w%d" % oi, [ob[oi]], [x1bufs[tt][n]],
                  lambda oi=oi, n=n: nc.sync.dma_start(out=x1T[n * 128:(n + 1) * 128, t0:t0 + TT],
                                                       in_=ost[:, oi, :]))
        emit_norm(P, C, lambda k: x1T[k * 128:(k + 1) * 128, t0:t0 + TT], lambda k: x1bufs[tt][k],
                  g_sb[:, 1, :], norm_to_h, TT)
        for j in range(KFF):
            pg = gemm([wf1[2 * j]], [16], h_rhs, hbufs)
            ti = j % 2
            P.op(P.act, pbufs(pg), [tb[ti]],
                 lambda pg=pg, ti=ti: nc.scalar.activation(out=v3(tmp[:, ti, :]), in_=pview(pg), func=AF.Silu))
            pu = gemm([wf1[2 * j + 1]], [16], h_rhs, hbufs)
            P.op(P.dve, pbufs(pu) + [tb[ti]], [rb[j]],
                 lambda pu=pu, ti=ti, j=j: nc.vector.tensor_tensor(
                     out=v3(R[:, j, :]), in0=pview(pu), in1=v3(tmp[:, ti, :]), op=ALU.mult))
        for n in range(16):
            pset = gemm([wf2[n, :, 0:16, :], wf2[n, :, 16:32, :], wf2[n, :, 32:44, :]], [16, 16, 12], R_rhs, rb)
            xi = P.xr
            P.xr = (P.xr + 1) % NX
            P.dma(P.sp, "x%d" % xi, [x1bufs[tt][n]], [xbufs[xi]],
                  lambda xi=xi, n=n: nc.sync.dma_start(out=xring[:, xi, :],
                                                       in_=x1T[n * 128:(n + 1) * 128, t0:t0 + TT]))
            oi = state["oi"]
            state["oi"] ^= 1
            P.op(P.dve, pbufs(pset) + [xbufs[xi]], [ob[oi]],
                 lambda pset=pset, xi=xi, oi=oi: nc.vector.tensor_tensor(
                     out=v3(ost[:, oi, :]), in0=pview(pset), in1=v3(xring[:, xi, :]), op=ALU.add))
            if final_norm:
                P.dma(P.sp, "x2w%d" % oi, [ob[oi]], [x2bufs[tt][n]],
                      lambda oi=oi, n=n: nc.sync.dma_start(out=x2T[n * 128:(n + 1) * 128, t0:t0 + TT],
                                                           in_=ost[:, oi, :]))
            else:
                P.dma(P.sp, "ow", [ob[oi]], [outbufs[tt][n]],
                      lambda oi=oi, n=n: nc.sync.dma_start(out=oT[n * 128:(n + 1) * 128, t0:t0 + TT],
                                                           in_=ost[:, oi, :]))
        if final_norm:
            def norm_to_out(k, x_ap, g_ap, rstd_ap, reads):
                oi = state["oi"]
                state["oi"] ^= 1
                P.op(P.dve, reads + [gbuf], [ob[oi]],
                     lambda: nc.vector.scalar_tensor_tensor(out=ost[:, oi, :], in0=x_ap, scalar=g_ap, in1=rstd_ap,
                                                            op0=ALU.mult, op1=ALU.mult))
                P.dma(P.sp, "ow", [ob[oi]], [outbufs[tt][k]],
                      lambda: nc.sync.dma_start(out=dst[k * 128:(k + 1) * 128, t0:t0 + TT], in_=ost[:, oi, :]))

            def norm_to_hloc(k, x_ap, g_ap, rstd_ap, reads):
                oi = state["oi"]
                state["oi"] ^= 1
                P.op(P.dve, reads + [gbuf], [hsb_[oi]],
                     lambda: nc.vector.scalar_tensor_tensor(out=hst[:, oi, :], in0=x_ap, scalar=g_ap, in1=rstd_ap,
                                                            op0=ALU.mult, op1=ALU.mult))
                P.dma(P.sp, "ow", [hsb_[oi]], [outbufs[tt][k]],
                      lambda: nc.sync.dma_start(out=dst[k * 128:(k + 1) * 128, t0:t0 + TT], in_=hst[:, oi, :]))
            emit_norm(P, C, lambda k: x2T[k * 128:(k + 1) * 128, t0:t0 + TT], lambda k: x2bufs[tt][k],
                      g_sb[:, 2, :], norm_to_out if last else norm_to_hloc, TT)
    P.pop_scope()


def emit_N(P, ps_, pb_, NTOK, TT, xT, gm, hloc, outbufs):
    nc = P.nc
    P.push_scope()
    C = common_setup(P, TT, ps_, pb_)
    g_sb = P.sbuf("gN", [128, 16], F32)
    gbuf = Buf("gN")
    hst = P.sbuf("hstN", [128, 2, TT], BF16)
    hsb_ = [Buf("hstN%d" % i) for i in range(2)]
    P.dma(P.sp, "g", [], [gbuf], lambda: nc.sync.dma_start(out=g_sb[:], in_=gm))
    st = {"oi": 0}
    for tt in range(NTOK // TT):
        t0 = tt * TT

        def out_fn(k, x_ap, g_ap, rstd_ap, reads):
            oi = st["oi"]
            st["oi"] ^= 1
            P.op(P.dve, reads + [gbuf], [hsb_[oi]],
                 lambda: nc.vector.scalar_tensor_tensor(out=hst[:, oi, :], in0=x_ap, scalar=g_ap, in1=rstd_ap,
                                                        op0=ALU.mult, op1=ALU.mult))
            P.dma(P.sp, "ow", [hsb_[oi]], [outbufs[tt][k]],
                  lambda: nc.sync.dma_start(out=hloc[k * 128:(k + 1) * 128, t0:t0 + TT], in_=hst[:, oi, :]))
        emit_norm(P, C, lambda k: xT[k * 128:(k + 1) * 128, t0:t0 + TT], lambda k: None, g_sb, out_fn, TT)
    P.pop_scope()


def arrange_w(w, kc):
    K, N = w.shape
    return np.ascontiguousarray(w.reshape(K // 128, 128, N // 128, 128).transpose(2, 1, 0, 3))


def prep_B_weights(l, w_in, w_branch_rnn, w_branch_attn, w_branch_pool, w_out, w_ffn_in, w_ffn_out,
                   g_mix, g_ffn, g_final):
    off = D_IN_GATES
    wg = arrange_w(w_in[l][:, off:off + 3 * D], 16)
    wbr = np.concatenate([arrange_w(w_branch_rnn[l], 8), arrange_w(w_branch_attn[l], 8),
                          arrange_w(w_branch_pool[l], 8)], axis=0)
    wo = arrange_w(w_out[l], 16)
    f1 = w_ffn_in[l]
    f1i = np.stack([f1[:, :DFF].reshape(D, KFF, 128), f1[:, DFF:].reshape(D, KFF, 128)], axis=2).reshape(D, 2 * DFF)
    wf1 = arrange_w(f1i, 16)
    wf2 = arrange_w(w_ffn_out[l], 44)
    gv = np.ascontiguousarray(np.stack([g_mix[l].reshape(16, 128).T, g_ffn[l].reshape(16, 128).T,
                                        g_final.reshape(16, 128).T], axis=1)).astype(np.float32)
    return dict(wg=wg, wbr=wbr, wo=wo, wf1=wf1, wf2=wf2, gv=gv)


D_IN_GATES = 1024 * 2 + 1024 * 3 + 8 + 1024


NPAR = 24
SQRT_DH = 11.313708498984761
GELU_C = 1.5957691216057308


def emit_A(P, ps, pb, S, d):
    nc = P.nc
    TT = 512
    NT = S // TT
    NKB = S // 128
    NTOK = S // 4
    hfull, hfbuf = d["hfull"], d["hfbuf"]
    w1d, w2d, wrgd, wpld, pad, ptabd, yo = d["w1"], d["w2"], d["wrg"], d["wpl"], d["pa"], d["ptab"], d["yo"]
    yob = [Buf("yo%d" % i) for i in range(8)]
    P.push_scope()
    ones_bf = P.sbuf("ones_bfA", [128, 128], BF16)
    onesb = Buf("onesA")
    P.op(P.dve, [], [onesb], lambda: nc.vector.memset(ones_bf[:], 1.0))

    def sb(name, shape, dt=F32):
        return P.sbuf(name, shape, dt), Buf(name)

    pa, pabuf = sb("pa_sb", [128, NPAR])
    ptab, ptabbuf = sb("ptab_sb", [128, 4, 16])
    cst, cstbuf = sb("cst", [128, 8])
    w1, w1buf = sb("w1_sb", [128, 16, 768], BF16)
    w2, w2buf = sb("w2_sb", [128, 16, 448], BF16)
    wrg, wrgbuf = sb("wrg_sb", [128, 4, 128], BF16)
    wpl, wplbuf = sb("wpl_sb", [128, 2, 256], BF16)
    hT = P.sbuf("hT", [128, 2, KD, TT], BF16)
    hbufs = [[Buf("h%d_%d" % (j, k)) for k in range(KD)] for j in range(2)]
    NTMP = 14
    tmp = P.sbuf("tmp", [128, NTMP, TT], F32)
    tb = [Buf("t%d" % i) for i in range(NTMP)]
    tmpb = P.sbuf("tmpb", [128, 6, TT], BF16)
    tbb = [Buf("tb%d" % i) for i in range(6)]
    xb = P.sbuf("xb", [128, 2, TT + 3], F32)
    xbb = [Buf("xb0"), Buf("xb1")]
    hcar = P.sbuf("hcar", [128, 2], F32)
    hcb = [Buf("hc0"), Buf("hc1")]
    ub = P.sbuf("ub", [128, 2, TT + 15], F32)
    ubb = [Buf("ub0"), Buf("ub1")]
    sw = P.sbuf("sw", [128, 4, TT + 15], F32)
    swb = [Buf("sw%d" % i) for i in range(4)]
    t16 = P.sbuf("t16", [128, 2, 16], F32)
    t16b = [Buf("t16a"), Buf("t16b")]
    KT, ktbuf_all = sb("KT", [128, S], BF16)
    ktb = [Buf("kt%d" % i) for i in range(NT)]
    V = P.sbuf("V", [128, NKB, 128], BF16)
    vb = [Buf("v%d" % i) for i in range(NT)]
    negF = P.sbuf("negF", [128, NKB], F32)
    nfb = [Buf("nf%d" % i) for i in range(NT)]
    bm, bmbuf = sb("bm", [128, NKB])
    ct, ctbuf = sb("ct", [128, 1])
    fc = P.sbuf("fc", [64, 2], F32)
    fcb = [Buf("fc0"), Buf("fc1")]
    one32, one32b = sb("one32", [128, 128], F32)
    onesT, onesTb = sb("onesT", [64, TT], F32)
    Z, zbuf = sb("Z", [128, TT], BF16)
    selT, selb = sb("selT", [128, 128], BF16)
    tri, trib = sb("tri", [128, 128], BF16)
    pT = P.sbuf("pT", [128, 3, TT], BF16)
    ptb = [Buf("pt%d" % i) for i in range(3)]
    state = {"bank": 0, "pt": 0, "yo": 0, "tmp": 0}

    P.dma(P.sp, "c0a", [], [pabuf], lambda: nc.sync.dma_start(out=pa[:], in_=pad))
    P.dma(P.sp, "c0b", [], [ptabbuf], lambda: nc.sync.dma_start(out=ptab[:], in_=ptabd))
    for k in range(KD):
        P.dma(P.pool, "cw1", [], [w1buf],
              lambda k=k: nc.gpsimd.dma_start(out=w1[:, k, :], in_=w1d[:, k, :], max_dma_last_dim=4096))
    P.dma(P.pool, "cwr", [], [wrgbuf], lambda: nc.gpsimd.dma_start(out=wrg[:], in_=wrgd, max_dma_last_dim=4096))
    P.dma(P.pool, "cwp", [], [wplbuf], lambda: nc.gpsimd.dma_start(out=wpl[:], in_=wpld, max_dma_last_dim=4096))
    P.op(P.dve, [], [one32b], lambda: nc.vector.memset(one32[:], 1.0))
    P.op(P.dve, [], [one32b], lambda: nc.vector.memset(onesT[:], 1.0))
    P.op(P.dve, [pabuf], [pabuf],
         lambda: nc.vector.tensor_scalar(out=pa[:, 18:20], in0=pa[:, 18:20], scalar1=-1.0, scalar2=None,
                                         op0=ALU.mult))
    P.op(P.dve, [], [xbb[0], xbb[1]], lambda: nc.vector.memset(xb[:], 0.0))
    P.op(P.dve, [], [ubb[0], ubb[1]], lambda: nc.vector.memset(ub[:], 0.0))
    P.op(P.dve, [], [hcb[0], hcb[1]], lambda: nc.vector.memset(hcar[:], 0.0))
    P.op(P.dve, [], swb, lambda: nc.vector.memset(sw[:], 0.0))
    P.op(P.dve, [], [selb], lambda: nc.vector.memset(selT[:], 0.0))
    P.op(P.dve, [], [zbuf], lambda: nc.vector.memset(Z[:], 0.0))
    P.op(P.dve, [], [selb], lambda: nc.vector.memset(selT[0:1, :], 1.0))
    P.op(P.dve, [], [selb], lambda: nc.vector.memset(selT[32:33, :], 1.0))
    P.op(P.dve, [], [trib], lambda: nc.vector.memset(tri[:], 1.0))
    P.op(P.pool, [trib], [trib],
         lambda: nc.gpsimd.affine_select(out=tri[:], in_=tri[:], pattern=[[1, 128]], compare_op=ALU.is_ge,
                                         fill=0.0, base=0, channel_multiplier=-1))
    P.op(P.act, [pabuf], [cstbuf],
         lambda: nc.scalar.activation(out=cst[:, 0:2], in_=pa[:, 14:16], func=AF.Exp, scale=-1.0))
    P.op(P.act, [cstbuf], [cstbuf],
         lambda: nc.scalar.activation(out=cst[:, 0:2], in_=cst[:, 0:2], func=AF.Ln, bias=one32[:, 0:1]))
    P.op(P.dve, [cstbuf], [cstbuf],
         lambda: nc.vector.tensor_scalar(out=cst[:, 2:4], in0=cst[:, 0:2], scalar1=-16.0, scalar2=None,
                                         op0=ALU.mult))
    P.op(P.dve, [cstbuf], [cstbuf],
         lambda: nc.vector.tensor_scalar(out=cst[:, 0:2], in0=cst[:, 0:2], scalar1=-8.0, scalar2=None,
                                         op0=ALU.mult))

    def load_w2(hd):
        for k in range(KD):
            P.dma(P.pool, "cw2", [], [w2buf],
                  lambda k=k: nc.gpsimd.dma_start(out=w2[:, k, :], in_=w2d[hd * 128:(hd + 1) * 128, k, :],
                                                  max_dma_last_dim=4096))
    load_w2(0)

    def bank():
        b = state["bank"]
        state["bank"] = (b + 1) % 8
        return b

    def T():
        i = state["tmp"]
        state["tmp"] = (i + 1) % NTMP
        return i

    def gemm_chunk(wt, wbuf, c0, hj, bk):
        def emit():
            ins = None
            for k in range(KD):
                ins = nc.tensor.matmul(ps[:, bk, :], lhsT=wt[:, k, c0:c0 + 128], rhs=hT[:, hj, k, :],
                                       start=(k == 0), stop=(k == KD - 1))
            return ins
        P.op(P.pe, [wbuf] + hbufs[hj], [pb[bk]], emit)

    def store_y(which, row0, src_ap, src_buf, t0):
        i = state["yo"]
        state["yo"] = (i + 1) % 8
        P.dma(P.sp, "yo%d" % i, [src_buf], [yob[i]],
              lambda: nc.sync.dma_start(
                  out=yo[(t0 // NTOK) * 768 + which * 256 + row0:(t0 // NTOK) * 768 + which * 256 + row0 + 128,
                         (t0 % NTOK):(t0 % NTOK) + TT], in_=src_ap))

    def load_h(it, hj):
        r, off = divmod(it * TT, NTOK)
        for k in range(KD):
            P.dma(P.sp, "hl%d" % hj, [hfbuf], [hbufs[hj][k]],
                  lambda k=k: nc.sync.dma_start(out=hT[:, hj, k, :],
                                                in_=hfull[k * 512 + r * 128:k * 512 + (r + 1) * 128, off:off + TT]))

    for it in range(NT):
        t0 = it * TT
        hj = it % 2
        if it == 0:
            load_h(0, 0)
        if it + 1 < NT:
            load_h(it + 1, (it + 1) % 2)

        def lru_gen(blk):
            bk = 3 * blk
            gemm_chunk(w1, w1buf, blk * 128, hj, bk)
            yield
            P.op(P.act, [pb[bk]], [xbb[blk]],
                 lambda bk=bk: nc.scalar.activation(out=xb[:, blk, 3:TT + 3], in_=ps[:, bk, :], func=AF.Copy))
            yield
            u, r, ig, a, a2, z = [blk * 6 + i_ for i_ in range(6)]
            P.op(P.act, [xbb[blk], pabuf], [tb[u]],
                 lambda: nc.scalar.activation(out=tmp[:, u, :], in_=xb[:, blk, 3:TT + 3], func=AF.Identity,
                                              scale=pa[:, blk * 4 + 3:blk * 4 + 4], bias=pa[:, 8 + blk:9 + blk]))
            yield
            for tap in (2, 1, 0):
                P.op(P.dve, [xbb[blk], pabuf, tb[u]], [tb[u]],
                     lambda tap=tap: nc.vector.scalar_tensor_tensor(
                         out=tmp[:, u, :], in0=xb[:, blk, tap:tap + TT], scalar=pa[:, blk * 4 + tap:blk * 4 + tap + 1],
                         in1=tmp[:, u, :], op0=ALU.mult, op1=ALU.add))
                yield
            P.op(P.dve, [xbb[blk]], [xbb[blk]],
                 lambda: nc.vector.tensor_copy(out=xb[:, blk, 0:3], in_=xb[:, blk, TT:TT + 3]))
            yield
            ubf = blk
            P.op(P.dve, [tb[u]], [tbb[ubf]], lambda: nc.vector.tensor_copy(out=tmpb[:, ubf, :], in_=tmp[:, u, :]))
            yield
            bkr, bki = 3 * blk + 1, 3 * blk + 2
            P.op(P.pe, [tbb[ubf], wrgbuf], [pb[bkr]],
                 lambda: nc.tensor.matmul(ps[:, bkr, :], lhsT=wrg[:, blk * 2, :], rhs=tmpb[:, ubf, :],
                                          start=True, stop=True))
            yield
            P.op(P.pe, [tbb[ubf], wrgbuf], [pb[bki]],
                 lambda: nc.tensor.matmul(ps[:, bki, :], lhsT=wrg[:, blk * 2 + 1, :], rhs=tmpb[:, ubf, :],
                                          start=True, stop=True))
            yield
            P.op(P.act, [pb[bkr], pabuf], [tb[r]],
                 lambda: nc.scalar.activation(out=tmp[:, r, :], in_=ps[:, bkr, :], func=AF.Sigmoid,
                                              bias=pa[:, 10 + blk:11 + blk]))
            yield
            P.op(P.act, [pb[bki], pabuf], [tb[ig]],
                 lambda: nc.scalar.activation(out=tmp[:, ig, :], in_=ps[:, bki, :], func=AF.Sigmoid,
                                              bias=pa[:, 12 + blk:13 + blk]))
            yield
            P.op(P.act, [tb[r], cstbuf], [tb[a]],
                 lambda: nc.scalar.activation(out=tmp[:, a, :], in_=tmp[:, r, :], func=AF.Exp,
                                              scale=cst[:, blk:blk + 1]))
            yield
            P.op(P.act, [tb[r], cstbuf], [tb[a2]],
                 lambda: nc.scalar.activation(out=tmp[:, a2, :], in_=tmp[:, r, :], func=AF.Exp,
                                              scale=cst[:, 2 + blk:3 + blk]))
            yield
            P.op(P.dve, [tb[a2]], [tb[a2]],
                 lambda: nc.vector.tensor_scalar(out=tmp[:, a2, :], in0=tmp[:, a2, :], scalar1=-1.0, scalar2=1.0,
                                                 op0=ALU.mult, op1=ALU.add))
            yield
            P.op(P.act, [tb[a2]], [tb[a2]],
                 lambda: nc.scalar.activation(out=tmp[:, a2, :], in_=tmp[:, a2, :], func=AF.Sqrt))
            yield
            P.op(P.dve, [tb[ig], tb[u]], [tb[ig]],
                 lambda: nc.vector.tensor_tensor(out=tmp[:, ig, :], in0=tmp[:, ig, :], in1=tmp[:, u, :], op=ALU.mult))
            yield
            P.op(P.dve, [tb[ig], tb[a2]], [tb[ig]],
                 lambda: nc.vector.tensor_tensor(out=tmp[:, ig, :], in0=tmp[:, ig, :], in1=tmp[:, a2, :],
                                                 op=ALU.mult))
            yield
            P.op(P.dve, [tb[a], tb[ig], hcb[blk]], [tb[r]],
                 lambda: nc.vector.tensor_tensor_scan(out=tmp[:, r, :], data0=tmp[:, a, :], data1=tmp[:, ig, :],
                                                      initial=hcar[:, blk:blk + 1], op0=ALU.mult, op1=ALU.add))
            yield
            P.op(P.dve, [tb[r]], [hcb[blk]],
                 lambda: nc.vector.tensor_copy(out=hcar[:, blk:blk + 1], in_=tmp[:, r, TT - 1:TT]))
            yield
            bky = 3 * blk
            gemm_chunk(w1, w1buf, 256 + blk * 128, hj, bky)
            yield
            P.op(P.act, [pb[bky]], [tb[z]],
                 lambda: nc.scalar.activation(out=tmp[:, z, :], in_=ps[:, bky, :], func=AF.Square))
            yield
            P.op(P.dve, [tb[z]], [tb[z]],
                 lambda: nc.vector.tensor_scalar(out=tmp[:, z, :], in0=tmp[:, z, :], scalar1=0.044715, scalar2=1.0,
                                                 op0=ALU.mult, op1=ALU.add))
            yield
            P.op(P.dve, [tb[z], pb[bky]], [tb[z]],
                 lambda: nc.vector.tensor_tensor(out=tmp[:, z, :], in0=ps[:, bky, :], in1=tmp[:, z, :], op=ALU.mult))
            yield
            P.op(P.act, [tb[z]], [tb[z]],
                 lambda: nc.scalar.activation(out=tmp[:, z, :], in_=tmp[:, z, :], func=AF.Sigmoid, scale=GELU_C))
            yield
            P.op(P.dve, [tb[z], pb[bky]], [tb[z]],
                 lambda: nc.vector.tensor_tensor(out=tmp[:, z, :], in0=ps[:, bky, :], in1=tmp[:, z, :], op=ALU.mult))
            yield
            yb_ = 2 + blk
            P.op(P.dve, [tb[z], tb[r]], [tbb[yb_]],
                 lambda: nc.vector.tensor_tensor(out=tmpb[:, yb_, :], in0=tmp[:, z, :], in1=tmp[:, r, :], op=ALU.mult))
            yield
            store_y(0, blk * 128, tmpb[:, yb_, :], tbb[yb_], t0)
            yield

        def pool_gen(cc):
            bk = 6
            gemm_chunk(w1, w1buf, 512 + cc * 128, hj, bk)
            yield
            P.op(P.act, [pb[bk]], [ubb[cc]],
                 lambda bk=bk: nc.scalar.activation(out=ub[:, cc, 15:TT + 15], in_=ps[:, bk, :], func=AF.Copy))
            yield
            W_ = TT + 15
            P.op(P.dve, [ubb[cc]], [swb[0]],
                 lambda: nc.vector.tensor_tensor(out=sw[:, 0, 1:W_], in0=ub[:, cc, 1:W_], in1=ub[:, cc, 0:W_ - 1],
                                                 op=ALU.add))
            yield
            P.op(P.dve, [swb[0]], [swb[1]],
                 lambda: nc.vector.tensor_tensor(out=sw[:, 1, 3:W_], in0=sw[:, 0, 3:W_], in1=sw[:, 0, 1:W_ - 2],
                                                 op=ALU.add))
            yield
            P.op(P.dve, [swb[1]], [swb[2]],
                 lambda: nc.vector.tensor_tensor(out=sw[:, 2, 7:W_], in0=sw[:, 1, 7:W_], in1=sw[:, 1, 3:W_ - 4],
                                                 op=ALU.add))
            yield
            P.op(P.dve, [swb[2]], [swb[3]],
                 lambda: nc.vector.tensor_tensor(out=sw[:, 3, 15:W_], in0=sw[:, 2, 15:W_], in1=sw[:, 2, 7:W_ - 8],
                                                 op=ALU.add))
            yield
            rv = 12 + cc
            P.op(P.dve, [swb[0], ubb[cc], pabuf], [tb[rv]],
                 lambda: nc.vector.scalar_tensor_tensor(out=tmp[:, rv, :], in0=sw[:, 0, 15:W_], scalar=pa[:, 20:21],
                                                        in1=ub[:, cc, 15:W_], op0=ALU.mult, op1=ALU.subtract))
            yield
            for wi in (1, 2):
                P.op(P.dve, [swb[wi], tb[rv], pabuf], [tb[rv]],
                     lambda wi=wi: nc.vector.scalar_tensor_tensor(out=tmp[:, rv, :], in0=sw[:, wi, 15:W_],
                                                                  scalar=pa[:, 20 + wi:21 + wi], in1=tmp[:, rv, :],
                                                                  op0=ALU.mult, op1=ALU.add))
                yield
            pl = 4 + cc
            P.op(P.dve, [swb[3], tb[rv], pabuf], [tbb[pl]],
                 lambda: nc.vector.scalar_tensor_tensor(out=tmpb[:, pl, :], in0=sw[:, 3, 15:W_], scalar=pa[:, 23:24],
                                                        in1=tmp[:, rv, :], op0=ALU.mult, op1=ALU.add))
            yield
            if it == 0:
                P.op(P.dve, [swb[0], ptabbuf], [t16b[0]],
                     lambda: nc.vector.tensor_tensor(out=t16[:, 0, :], in0=sw[:, 0, 15:31], in1=ptab[:, 0, :],
                                                     op=ALU.mult))
                yield
                for wi in (1, 2, 3):
                    P.op(P.dve, [swb[wi], ptabbuf], [t16b[1]],
                         lambda wi=wi: nc.vector.tensor_tensor(out=t16[:, 1, :], in0=sw[:, wi, 15:31],
                                                               in1=ptab[:, wi, :], op=ALU.mult))
                    yield
                    P.op(P.dve, [t16b[0], t16b[1]], [t16b[0]],
                         lambda: nc.vector.tensor_tensor(out=t16[:, 0, :], in0=t16[:, 0, :], in1=t16[:, 1, :],
                                                         op=ALU.add))
                    yield
                P.op(P.dve, [t16b[0], ubb[cc]], [tbb[pl]],
                     lambda: nc.vector.tensor_tensor(out=tmpb[:, pl, 0:16], in0=t16[:, 0, :], in1=ub[:, cc, 15:31],
                                                     op=ALU.subtract))
                yield
            P.op(P.dve, [ubb[cc]], [ubb[cc]],
                 lambda: nc.vector.tensor_copy(out=ub[:, cc, 0:15], in_=ub[:, cc, TT:TT + 15]))
            yield
        def pool_both():
            yield from pool_gen(0)
            yield from pool_gen(1)
        gens = [lru_gen(0), lru_gen(1), pool_both()]
        while gens:
            for g_ in list(gens):
                try:
                    next(g_)
                except StopIteration:
                    gens.remove(g_)
        for dc in range(2):
            bk = 6 + dc
            def emit(bk=bk, dc=dc):
                ins = None
                for cc in range(2):
                    ins = nc.tensor.matmul(ps[:, bk, :], lhsT=wpl[:, cc, dc * 128:(dc + 1) * 128],
                                           rhs=tmpb[:, 4 + cc, :], start=(cc == 0), stop=(cc == 1))
                return ins
            P.op(P.pe, [tbb[4], tbb[5], wplbuf], [pb[bk]], emit)
            yi = dc
            pti = state["pt"]
            state["pt"] = (pti + 1) % 3
            P.op(P.act, [pb[bk], pabuf], [ptb[pti]],
                 lambda bk=bk, pti=pti, dc=dc: nc.scalar.activation(out=pT[:, pti, :], in_=ps[:, bk, :],
                                                                    func=AF.Identity,
                                                                    scale=pa[:, 16 + dc:17 + dc]))
            store_y(2, dc * 128, pT[:, pti, :], ptb[pti], t0)

    d["after_part"]((0, 1, 4, 5))
    SB0, SB1, OB, SMB, QKB, VB, FB, SMALL = 0, 1, 2, 3, 4, 5, 6, 7
    scale = 1.0 / SQRT_DH
    for hd in range(2):
        if hd == 1:
            load_w2(1)
            d["after_part"]((2,))
        P.op(P.dve, [], [fcb[0]], lambda: nc.vector.memset(fc[:, 0:1], 0.0))
        for it in range(NT):
            t0 = it * TT
            hj = it % 2
            if it == 0:
                load_h(0, 0)
            if it + 1 < NT:
                load_h(it + 1, (it + 1) % 2)
            def emit_f():
                ins = None
                for k in range(KD):
                    ins = nc.tensor.matmul(ps[0:64, FB, :], lhsT=w2[:, k, 384:448], rhs=hT[:, hj, k, :],
                                           start=(k == 0), stop=(k == KD - 1))
                return ins
            P.op(P.pe, [w2buf] + hbufs[hj], [pb[FB]], emit_f)
            e, fa, fr = T(), T(), T()
            P.op(P.act, [pb[FB], pabuf], [tb[e]],
                 lambda: nc.scalar.activation(out=tmp[0:64, e, :], in_=ps[0:64, FB, :], func=AF.Exp, scale=-1.0,
                                              bias=pa[0:64, 18 + hd:19 + hd]))
            P.op(P.act, [tb[e], one32b], [tb[e]],
                 lambda: nc.scalar.activation(out=tmp[0:64, e, :], in_=tmp[0:64, e, :], func=AF.Ln,
                                              bias=one32[0:64, 0:1]))
            fi, fo = it % 2, (it + 1) % 2
            P.op(P.dve, [tb[e], one32b, fcb[fi]], [tb[fa]],
                 lambda: nc.vector.tensor_tensor_scan(out=tmp[0:64, fa, :], data0=onesT[:, :],
                                                      data1=tmp[0:64, e, :], initial=fc[:, fi:fi + 1],
                                                      op0=ALU.mult, op1=ALU.subtract))
            P.op(P.dve, [tb[fa]], [fcb[fo]],
                 lambda: nc.vector.tensor_copy(out=fc[:, fo:fo + 1], in_=tmp[0:64, fa, TT - 1:TT]))
            P.op(P.dve, [tb[fa], fcb[fi]], [tb[fr]],
                 lambda: nc.vector.tensor_scalar(out=tmp[0:64, fr, :], in0=tmp[0:64, fa, :], scalar1=fc[:, fi:fi + 1],
                                                 scalar2=SQRT_DH, op0=ALU.subtract, op1=ALU.mult))
            P.op(P.dve, [tb[fr]], [zbuf], lambda: nc.vector.tensor_copy(out=Z[0:64, :], in_=tmp[0:64, fr, :]))
            P.op(P.dve, [tb[fr], zbuf], [tb[fr]],
                 lambda: nc.vector.tensor_tensor(out=tmp[32:64, fr, :], in0=tmp[32:64, fr, :], in1=Z[32:64, :],
                                                 op=ALU.subtract))
            P.op(P.dve, [tb[fr]], [zbuf], lambda: nc.vector.tensor_copy(out=Z[32:64, :], in_=tmp[32:64, fr, :]))
            qi = 0
            gemm_chunk(w2, w2buf, 0, hj, QKB)
            P.op(P.act, [pb[QKB]], [tbb[qi]],
                 lambda: nc.scalar.activation(out=tmpb[:, qi, :], in_=ps[:, QKB, :], func=AF.Copy))
            gemm_chunk(w2, w2buf, 128, hj, QKB)
            P.op(P.act, [pb[QKB]], [ktb[it]],
                 lambda: nc.scalar.activation(out=KT[:, t0:t0 + TT], in_=ps[:, QKB, :], func=AF.Copy))
            def emit_v():
                ins = None
                for tbk in range(4):
                    for k in range(KD):
                        ins = nc.tensor.matmul(ps[:, VB, tbk * 128:(tbk + 1) * 128],
                                               lhsT=hT[:, hj, k, tbk * 128:(tbk + 1) * 128], rhs=w2[:, k, 256:384],
                                               start=(k == 0), stop=(k == KD - 1))
                return ins
            P.op(P.pe, [w2buf] + hbufs[hj], [pb[VB]], emit_v)
            P.op(P.act, [pb[VB]], [vb[it]],
                 lambda: nc.scalar.activation(out=V[:, it * 4:(it + 1) * 4, :],
                                              in_=ps[:, VB, :].rearrange("p (a b) -> p a b", a=4), func=AF.Copy))
            def emit_small():
                ins = None
                for tbk in range(4):
                    ins = nc.tensor.matmul(ps[:, SMALL, tbk:tbk + 1], lhsT=tmp[0:1, fa, tbk * 128:(tbk + 1) * 128],
                                           rhs=one32[0:1, 0:1], start=True, stop=True)
                ins = nc.tensor.matmul(ps[:, SMALL, 4:5], lhsT=one32[0:1, :], rhs=fc[0:1, fi:fi + 1],
                                       start=True, stop=True)
                return ins
            P.op(P.pe, [tb[fa], one32b, fcb[fi]], [pb[SMALL]], emit_small)
            P.op(P.dve, [pb[SMALL]], [nfb[it]],
                 lambda: nc.vector.tensor_scalar(out=negF[:, it * 4:(it + 1) * 4], in0=ps[:, SMALL, 0:4],
                                                 scalar1=-1.0, scalar2=None, op0=ALU.mult))
            P.op(P.dve, [pb[SMALL]], [ctbuf], lambda: nc.vector.tensor_copy(out=ct[:, :], in_=ps[:, SMALL, 4:5]))
            nkb = (it + 1) * 4
            P.op(P.dve, nfb[0:it + 1] + [ctbuf], [bmbuf],
                 lambda: nc.vector.tensor_scalar(out=bm[:, 0:nkb], in0=negF[:, 0:nkb], scalar1=ct[:, 0:1],
                                                 scalar2=None, op0=ALU.add))
            SBANKS = [SB0, SB1, QKB, VB]
            PRE = 3

            def issue_s(kb):
                j = kb - it * 4
                c0 = max(j, 0) * 128
                sbk = SBANKS[kb % 4]

                def emit_s():
                    nc.tensor.matmul(ps[:, sbk, c0:TT], lhsT=KT[:, kb * 128:(kb + 1) * 128], rhs=tmpb[:, qi, c0:TT],
                                     start=True, stop=False)
                    return nc.tensor.matmul(ps[:, sbk, c0:TT], lhsT=selT[:, :], rhs=Z[:, c0:TT],
                                            start=False, stop=True)
                P.op(P.pe, [ktb[kb // 4], tbb[qi], selb, zbuf], [pb[sbk]], emit_s)

            for kb in range(min(PRE, nkb)):
                issue_s(kb)
            for kb in range(nkb):
                if kb + PRE < nkb:
                    issue_s(kb + PRE)
                j = kb - it * 4
                c0 = max(j, 0) * 128
                sbk = SBANKS[kb % 4]
                pti = state["pt"]
                state["pt"] = (pti + 1) % 3
                P.op(P.act, [pb[sbk], bmbuf], [ptb[pti]],
                     lambda: nc.scalar.activation(out=pT[:, pti, c0:TT], in_=ps[:, sbk, c0:TT], func=AF.Exp,
                                                  scale=scale, bias=bm[:, kb:kb + 1]))
                if j >= 0:
                    P.op(P.dve, [ptb[pti], trib], [ptb[pti]],
                         lambda: nc.vector.tensor_tensor(out=pT[:, pti, c0:c0 + 128], in0=pT[:, pti, c0:c0 + 128],
                                                         in1=tri[:, :], op=ALU.mult))

                def emit_o():
                    nc.tensor.matmul(ps[:, OB, c0:TT], lhsT=V[:, kb, :], rhs=pT[:, pti, c0:TT],
                                     start=(kb == 0), stop=(kb == nkb - 1))
                    return nc.tensor.matmul(ps[:, SMB, c0:TT], lhsT=ones_bf[:, :], rhs=pT[:, pti, c0:TT],
                                            start=(kb == 0), stop=(kb == nkb - 1))
                P.op(P.pe, [vb[kb // 4], ptb[pti], onesb], [pb[OB], pb[SMB]], emit_o)
            rc = T()
            P.op(P.dve, [pb[SMB]], [tb[rc]], lambda: nc.vector.reciprocal(out=tmp[:, rc, :], in_=ps[:, SMB, :]))
            yi = 1 + it % 2
            P.op(P.dve, [pb[OB], tb[rc]], [tbb[yi]],
                 lambda: nc.vector.tensor_tensor(out=tmpb[:, yi, :], in0=ps[:, OB, :], in1=tmp[:, rc, :], op=ALU.mult))
            store_y(1, hd * 128, tmpb[:, yi, :], tbb[yi], t0)
    d["after_part"]((3,))
    P.pop_scope()


POOL_WINDOWS = (2, 4, 8, 16)


def prep_A_inputs(l, b, j, xT_b, g_mix, w_in, b_forget, conv_w, conv_b, w_rg, b_rg, w_ig, b_ig, lru_lambda,
                  w_pool, pool_scale):
    W = w_in[l]

    def arr(cols):
        return np.ascontiguousarray(cols.reshape(16, 128, cols.shape[1]).transpose(1, 0, 2))
    w1 = np.concatenate([W[:, 2 * j * 128:(2 * j + 2) * 128], W[:, 1024 + 2 * j * 128:1024 + (2 * j + 2) * 128],
                         W[:, 5128 + j * 256:5128 + (j + 1) * 256]], axis=1)
    w2 = []
    for hd in range(2):
        h = 2 * j + hd
        w2.append(arr(np.concatenate([W[:, 2048 + h * 128:2048 + (h + 1) * 128],
                                      W[:, 3072 + h * 128:3072 + (h + 1) * 128],
                                      W[:, 4096 + h * 128:4096 + (h + 1) * 128],
                                      np.repeat(W[:, 5120 + h:5121 + h], 64, axis=1)], axis=1)))
    wrg = np.stack([w_rg[l][2 * j], w_ig[l][2 * j], w_rg[l][2 * j + 1], w_ig[l][2 * j + 1]], axis=1)
    wpl = np.ascontiguousarray(w_pool[l][j].reshape(2, 128, 256).transpose(1, 0, 2))
    pa = np.zeros((128, NPAR), np.float32)
    for blk in range(2):
        ch = slice((2 * j + blk) * 128, (2 * j + blk + 1) * 128)
        for tap in range(4):
            pa[:, blk * 4 + tap] = conv_w[l][tap, ch]
        pa[:, 8 + blk] = conv_b[l][ch]
        pa[:, 10 + blk] = b_rg[l][ch]
        pa[:, 12 + blk] = b_ig[l][ch]
        pa[:, 14 + blk] = lru_lambda[l][ch]
        pa[:, 16 + blk] = pool_scale[l][j * 256 + blk * 128:j * 256 + (blk + 1) * 128]
        pa[:, 18 + blk] = b_forget[l][2 * j + blk]
    ptab = np.zeros((128, 4, 16), np.float32)
    for wi, w in enumerate(POOL_WINDOWS):
        if wi == j:
            pa[:, 20 + wi] = 1.0 / w
            ptab[:, wi, :] = 1.0 / np.minimum(np.arange(1, 17), w)
    return dict(w1=arr(w1), w2=np.stack(w2),
                wrg=np.ascontiguousarray(wrg), wpl=wpl, pa=pa, ptab=ptab)


GROUPS = [[0, 1, 2, 3], [4, 5, 6, 7]]


def build_fused(S, B_TT):
    P = Prog()
    nc = P.nc
    NTOK = S // 4
    ntt = NTOK // B_TT
    ps = P.psum("ps", [128, 8, 512], F32)
    pb = [Buf("p%d" % i) for i in range(8)]
    xT = P.dram("xT", [D, NTOK], F32, "ExternalInput")
    gm0 = P.dram("gm0", [128, 16], F32, "ExternalInput")
    ptab = P.dram("ptab", [128, 4, 16], F32, "ExternalInput")
    oT = P.dram("oT", [D, NTOK], F32, "ExternalOutput")
    A_in, B_in = [], []
    for l in range(2):
        A_in.append(dict(
            w1=P.dram("w1_%d" % l, [128, 16, 768], F32, "ExternalInput"),
            w2=P.dram("w2_%d" % l, [256, 16, 448], F32, "ExternalInput"),
            wrg=P.dram("wrg_%d" % l, [128, 4, 128], F32, "ExternalInput"),
            wpl=P.dram("wpl_%d" % l, [128, 2, 256], F32, "ExternalInput"),
            pa=P.dram("pa_%d" % l, [128, NPAR], F32, "ExternalInput"),
            ptab=ptab))
        B_in.append(dict(
            wg=P.dram("wg_%d" % l, [48, 128, 16, 128], F32, "ExternalInput"),
            wbr=P.dram("wbr_%d" % l, [48, 128, 8, 128], F32, "ExternalInput"),
            wo=P.dram("wo_%d" % l, [16, 128, 16, 128], F32, "ExternalInput"),
            wf1=P.dram("wf1_%d" % l, [88, 128, 16, 128], F32, "ExternalInput"),
            wf2=P.dram("wf2_%d" % l, [16, 128, 44, 128], F32, "ExternalInput"),
            gv=P.dram("gv_%d" % l, [128, 3, 16], F32, "ExternalInput")))
    hloc = [P.dram("hloc%d" % l, [D, NTOK], BF16) for l in range(2)]
    hfull = [P.dram("hfull%d" % l, [4 * D, NTOK], BF16) for l in range(2)]
    yo = [P.dram("yo%d" % l, [4 * 768, NTOK], BF16) for l in range(2)]
    yfull = [P.dram("yfull%d" % l, [4 * 4 * 768, NTOK], BF16) for l in range(2)]
    x1T = [P.dram("x1T%d" % l, [D, NTOK], F32) for l in range(2)]
    x2T = [P.dram("x2T%d" % l, [D, NTOK], F32) for l in range(2)]
    ysl = [P.dram("ysl%d" % l, [4 * 768, NTOK], BF16) for l in range(2)]
    qreg = nc.sync.partition_id() % 4

    def mk():
        return [[Buf("b") for _ in range(16)] for _ in range(ntt)]

    emit_N(P, ps, pb, NTOK, B_TT, xT, gm0, hloc[0], mk())
    x2prev = None
    outb = None
    for l in range(2):
        hfbuf = Buf("hf%d" % l)
        for k in range(KD):
            P.cc([], [hfbuf], lambda k=k: nc.gpsimd.collective_compute(
                "AllGather", ALU.bypass, replica_groups=GROUPS, ins=[hloc[l][k * 128:(k + 1) * 128, :]],
                outs=[hfull[l][k * 512:(k + 1) * 512, :]]))
        ybuf = Buf("yf%d" % l)

        def after_part(rbs, l=l, ybuf=ybuf):
            P._wait(P.pool, {S_.sem: S_.n for nm, S_ in P.dmaq.items() if nm.startswith("yo") and S_.n > 0})
            for q in range(4):
                for rb in rbs:
                    qb = q * 6 + rb
                    P.cc([], [ybuf], lambda qb=qb: nc.gpsimd.collective_compute(
                        "AllGather", ALU.bypass, replica_groups=GROUPS, ins=[yo[l][qb * 128:(qb + 1) * 128, :]],
                        outs=[yfull[l][qb * 512:(qb + 1) * 512, :]]))
        emit_A(P, ps, pb, S, dict(A_in[l], hfull=hfull[l], hfbuf=hfbuf, yo=yo[l], after_part=after_part))
        yslb = Buf("ysl%d" % l)
        for r in range(4):
            P.dma(P.sp, "ysl", [ybuf], [yslb],
                  lambda r=r: nc.sync.dma_start(out=ysl[l][r * 768:(r + 1) * 768, :],
                                                in_=yfull[l][ds(qreg * 3072 + r * 768, 768), :]))
        x2b = mk()
        outb = mk()
        dd = dict(B_in[l], xsrc=(xT if l == 0 else x2T[0]),
                  xsrc_buf=((lambda tt, k: None) if l == 0 else (lambda tt, k, xp=x2prev: xp[tt][k])),
                  yfull=ysl[l], ybuf=yslb, x1T=x1T[l], x2T=x2T[l], dst=(hloc[1] if l == 0 else oT),
                  x2bufs=x2b, outbufs=outb)
        emit_B(P, ps, pb, NTOK, B_TT, dd, last=(l == 1))
        x2prev = x2b
    P.finish([b for row in outb for b in row])
    return P


_CACHE = {}
B_TT_FULL = 1024


def prep_inputs(x, g_mix, w_in, b_forget, conv_w, conv_b, w_rg, b_rg, w_ig, b_ig, lru_lambda, w_pool, pool_scale,
                w_branch_rnn, w_branch_attn, w_branch_pool, w_out, g_ffn, w_ffn_in, w_ffn_out, g_final):
    nb, S, _ = x.shape
    NTOK = S // 4
    BW = []
    for l in range(2):
        g_next = g_mix[1] if l == 0 else g_final
        W = prep_B_weights(l, w_in, w_branch_rnn, w_branch_attn, w_branch_pool, w_out, w_ffn_in, w_ffn_out,
                           g_mix, g_ffn, g_next)
        BW.append({"%s_%d" % (k, l): v for k, v in W.items()})
    in_maps = []
    for c in range(NCORES):
        b, j = c // 4, c % 4
        m = {"xT": np.ascontiguousarray(x[b, j * NTOK:(j + 1) * NTOK, :].T),
             "gm0": np.ascontiguousarray(g_mix[0].reshape(16, 128).T)}
        for l in range(2):
            a = prep_A_inputs(l, b, j, None, g_mix, w_in, b_forget, conv_w, conv_b, w_rg, b_rg, w_ig, b_ig,
                              lru_lambda, w_pool, pool_scale)
            m["ptab"] = a["ptab"]
            m["w1_%d" % l] = a["w1"]
            m["w2_%d" % l] = np.ascontiguousarray(a["w2"].reshape(256, 16, 448))
            m["wrg_%d" % l] = a["wrg"]
            m["wpl_%d" % l] = a["wpl"]
            m["pa_%d" % l] = a["pa"]
            m.update(BW[l])
        in_maps.append(m)
    return in_maps


def kernel(x, g_mix, w_in, b_forget, conv_w, conv_b, w_rg, b_rg, w_ig, b_ig, lru_lambda, w_pool, pool_scale,
           w_branch_rnn, w_branch_attn, w_branch_pool, w_out, g_ffn, w_ffn_in, w_ffn_out, g_final):
    args = [np.asarray(a, dtype=np.float32) for a in (
        x, g_mix, w_in, b_forget, conv_w, conv_b, w_rg, b_rg, w_ig, b_ig, lru_lambda, w_pool, pool_scale,
        w_branch_rnn, w_branch_attn, w_branch_pool, w_out, g_ffn, w_ffn_in, w_ffn_out, g_final)]
    x = args[0]
    nb, S, _ = x.shape
    NTOK = S // 4
    tt = B_TT_FULL if NTOK % B_TT_FULL == 0 else 512
    key = (S, tt)
    if key not in _CACHE:
        _CACHE[key] = build_fused(S, tt)
    P = _CACHE[key]
    in_maps = prep_inputs(*args)
    res = run_bass_kernel_spmd(P.nc, in_maps, core_ids=list(range(NCORES)))
    out = np.empty((nb, S, D), np.float32)
    for c in range(NCORES):
        b, j = c // 4, c % 4
        out[b, j * NTOK:(j + 1) * NTOK, :] = res.results[c]["oT"].T
    return out
```

```python
import contextlib
import numpy as np
import ml_dtypes
import concourse.bass as bass
import concourse.mybir as mybir
from concourse.bass import ds
from concourse.bass_utils import run_bass_kernel_spmd

F32 = mybir.dt.float32
BF16 = mybir.dt.bfloat16
AF = mybir.ActivationFunctionType
ALU = mybir.AluOpType

D = 2048
KD = D // 128
DFF = 5632
KFF = DFF // 128
EPS = 1e-6
NCORES = 8
SEQ = 16384
SAME_ENGINE_SYNC = True
SEM_LIMIT = 30000


class Eng:
    def __init__(self, P, name, h, step):
        self.P, self.name, self.h, self.step = P, name, h, step
        self.n = 0
        self.sem = P.new_sem(name)
        self.nsem = 0
        self.waited = {}

    def tick(self):
        if self.n >= SEM_LIMIT * self.step:
            self.nsem += 1
            self.sem = self.P.new_sem("%s_%d" % (self.name, self.nsem))
            self.n = 0
        self.n += self.step
        return (self.sem, self.n)


class Buf:
    __slots__ = ("name", "writer", "readers")

    def __init__(self, name):
        self.name = name
        self.writer = None
        self.readers = {}


class Prog:
    def __init__(self):
        self.nc = bass.Bass("TRN2", target_bir_lowering=False)
        self.stack = contextlib.ExitStack()
        self.nsem_total = 0
        nc = self.nc
        self.pe = Eng(self, "pe", nc.tensor, 1)
        self.act = Eng(self, "act", nc.scalar, 1)
        self.dve = Eng(self, "dve", nc.vector, 1)
        self.pool = Eng(self, "pool", nc.gpsimd, 1)
        self.sp = Eng(self, "sp", nc.sync, 1)
        self.dmaq = {}
        self.scopes = []
        self.nname = 0
        self.xr = 0

    def new_sem(self, name):
        self.nsem_total += 1
        return self.stack.enter_context(self.nc.semaphore("s_%s_%d" % (name, self.nsem_total)))

    def sbuf(self, name, shape, dt):
        st = self.scopes[-1] if self.scopes else self.stack
        self.nname += 1
        return st.enter_context(self.nc.sbuf_tensor("%s_%d" % (name, self.nname), shape, dt))

    def push_scope(self):
        self.scopes.append(contextlib.ExitStack())

    def pop_scope(self):
        self.barrier()
        self.scopes.pop().close()

    def barrier(self):
        engs = [self.pe, self.act, self.dve, self.pool, self.sp]
        targets = {}
        for E in engs + list(self.dmaq.values()):
            if E.n > 0:
                targets[E.sem] = E.n
        for E in engs:
            self._wait(E, {s: n for s, n in targets.items() if s is not E.sem})

    def cc(self, reads, writes, emit):
        if "cc" not in self.dmaq:
            self.dmaq["cc"] = Eng(self, "cc", None, 1)
        S = self.dmaq["cc"]
        self._wait(self.pool, self._deps(reads, writes))
        ins = emit()
        tok = S.tick()
        ins.then_inc(tok[0])
        for b in reads:
            b.readers[id(S)] = tok
        for b in writes:
            b.writer = tok
            b.readers = {}
        return tok

    def psum(self, name, shape, dt=F32):
        return self.stack.enter_context(self.nc.psum_tensor(name, shape, dt))

    def dram(self, name, shape, dt, kind="Internal"):
        return self.nc.dram_tensor(name, shape, dt, kind=kind).ap()

    def _deps(self, reads, writes):
        deps = {}
        for b in reads:
            if b.writer is not None:
                s, n = b.writer
                deps[s] = max(deps.get(s, 0), n)
        for b in writes:
            if b.writer is not None:
                s, n = b.writer
                deps[s] = max(deps.get(s, 0), n)
            for (s, n) in b.readers.values():
                deps[s] = max(deps.get(s, 0), n)
        return deps

    def _wait(self, E, deps, skip_sem=None):
        for s, n in deps.items():
            if s is skip_sem:
                continue
            if E.waited.get(s, 0) < n:
                E.h.wait_ge(s, n)
                E.waited[s] = n

    def op(self, E, reads, writes, emit):
        deps = self._deps(reads, writes)
        skip = None
        if E is self.pe or not SAME_ENGINE_SYNC:
            skip = E.sem
        self._wait(E, deps, skip)
        ins = emit()
        tok = E.tick()
        ins.then_inc(tok[0], 1)
        for b in reads:
            b.readers[id(E)] = tok
        for b in writes:
            b.writer = tok
            b.readers = {}
        return tok

    def dma(self, Q, slot, reads, writes, emit):
        if slot not in self.dmaq:
            self.dmaq[slot] = Eng(self, "d" + slot, None, 16)
        S = self.dmaq[slot]
        deps = self._deps(reads, writes)
        self._wait(Q, deps)
        ins = emit()
        tok = S.tick()
        ins.then_inc(tok[0], 16)
        for b in reads:
            b.readers[id(S) + len(b.readers) * 0] = tok
        for b in writes:
            b.writer = tok
            b.readers = {}
        return tok

    def finish(self, bufs):
        deps = {}
        for b in bufs:
            if b.writer is not None:
                s, n = b.writer
                deps[s] = max(deps.get(s, 0), n)
        self._wait(self.sp, deps)
        self.stack.close()


class WRing:
    def __init__(self, P, nslots):
        self.P = P
        self.n = nslots
        self.t = P.sbuf("wring", [128, nslots, 16, 128], BF16)
        self.bufs = [Buf("w%d" % i) for i in range(nslots)]
        P.wring_gen = getattr(P, "wring_gen", 0) + 1
        self.gen = P.wring_gen
        self.i = 0

    def load(self, src, kc):
        P = self.P
        i = self.i
        self.i = (self.i + 1) % self.n
        dst = self.t[:, i, 0:kc, :]
        b = self.bufs[i]
        P.dma(P.pool, "w%d_%d" % (self.gen, i), [], [b],
              lambda: P.nc.gpsimd.dma_start(out=dst, in_=src, max_dma_last_dim=4096))
        return self.t[:, i], b


def emit_norm(P, C, src_chunk, src_buf, gcol, out_fn, TT):
    nc = P.nc
    nsub = TT // 512
    xring, xbufs, sq, sqbufs, ps, pb, rstd, rstdbuf, ones_bf, onesb = (
        C["xring"], C["xbufs"], C["sq"], C["sqbufs"], C["ps"], C["pb"], C["rstd"], C["rstdbuf"], C["ones_bf"],
        C["onesb"])
    nx = len(xbufs)
    for k in range(KD):
        xi = P.xr
        P.xr = (P.xr + 1) % nx
        sb = src_buf(k)
        P.dma(P.sp, "x%d" % xi, [sb] if sb else [], [xbufs[xi]],
              lambda xi=xi, k=k: nc.sync.dma_start(out=xring[:, xi, :], in_=src_chunk(k)))
        si = k % 2
        P.op(P.act, [xbufs[xi]], [sqbufs[si]],
             lambda xi=xi, si=si: nc.scalar.activation(out=sq[:, si, :], in_=xring[:, xi, :], func=AF.Square))
        for s in range(nsub):
            P.op(P.pe, [sqbufs[si], onesb], [pb[6 + s]],
                 lambda s=s, si=si, k=k: nc.tensor.matmul(ps[:, 6 + s, :], lhsT=ones_bf[:, :],
                                                          rhs=sq[:, si, s * 512:(s + 1) * 512],
                                                          start=(k == 0), stop=(k == KD - 1)))
    for s in range(nsub):
        P.op(P.act, [pb[6 + s]], [rstdbuf],
             lambda s=s: nc.scalar.activation(out=rstd[:, s * 512:(s + 1) * 512], in_=ps[:, 6 + s, :],
                                              func=AF.Sqrt, scale=1.0 / D, bias=C["eps"][:, 0:1]))
    P.op(P.dve, [rstdbuf], [rstdbuf], lambda: nc.vector.reciprocal(out=rstd[:, :], in_=rstd[:, :]))
    for k in range(KD):
        xi = P.xr
        P.xr = (P.xr + 1) % nx
        sb = src_buf(k)
        P.dma(P.sp, "x%d" % xi, [sb] if sb else [], [xbufs[xi]],
              lambda xi=xi, k=k: nc.sync.dma_start(out=xring[:, xi, :], in_=src_chunk(k)))
        out_fn(k, xring[:, xi, :], gcol[:, k:k + 1], rstd[:, :], [xbufs[xi], rstdbuf])


def common_setup(P, TT, ps, pb, NX=4):
    nc = P.nc
    C = {}
    C["ones_bf"] = P.sbuf("ones_bf", [128, 128], BF16)
    C["onesb"] = Buf("ones")
    C["eps"] = P.sbuf("eps", [128, 1], F32)
    C["xring"] = P.sbuf("xring", [128, NX, TT], F32)
    C["xbufs"] = [Buf("x%d" % i) for i in range(NX)]
    C["sq"] = P.sbuf("sq", [128, 2, TT], BF16)
    C["sqbufs"] = [Buf("sq0"), Buf("sq1")]
    C["rstd"] = P.sbuf("rstd", [128, TT], F32)
    C["rstdbuf"] = Buf("rstd")
    C["ps"] = ps
    C["pb"] = pb
    P.xr = 0
    P.op(P.dve, [], [C["onesb"]], lambda: nc.vector.memset(C["ones_bf"][:], 1.0))
    P.op(P.dve, [], [C["rstdbuf"]], lambda: nc.vector.memset(C["eps"][:], EPS))
    return C


def emit_B(P, ps_, pb_, NTOK, TT, d, last):
    nc = P.nc
    nsub = TT // 512
    xT, xsrc_buf = d["xsrc"], d["xsrc_buf"]
    yfull, ybuf = d["yfull"], d["ybuf"]
    wg, wbr, wo, wf1, wf2, gv = d["wg"], d["wbr"], d["wo"], d["wf1"], d["wf2"], d["gv"]
    x1T, x2T, dst = d["x1T"], d["x2T"], d["dst"]
    ntt = NTOK // TT
    x1bufs = [[Buf("x1_%d_%d" % (t, n)) for n in range(16)] for t in range(ntt)]
    x2bufs = d["x2bufs"]
    outbufs = d["outbufs"]
    final_norm = True

    P.push_scope()
    C = common_setup(P, TT, ps_, pb_)
    ps, pb, xring, xbufs = C["ps"], C["pb"], C["xring"], C["xbufs"]
    NX = len(xbufs)
    g_sb = P.sbuf("g_sb", [128, 3, 16], F32)
    gbuf = Buf("g")
    hT = P.sbuf("hT", [128, KD, TT], BF16)
    hbufs = [Buf("h%d" % k) for k in range(KD)]
    R = P.sbuf("R", [128, KFF, TT], BF16)
    rb = [Buf("r%d" % k) for k in range(KFF)]
    tmp = P.sbuf("tmp", [128, 3, TT], F32)
    tb = [Buf("t%d" % i) for i in range(3)]
    ost = P.sbuf("ost", [128, 2, TT], F32)
    ob = [Buf("o%d" % i) for i in range(2)]
    hst = P.sbuf("hst", [128, 2, TT], BF16)
    hsb_ = [Buf("hst%d" % i) for i in range(2)]
    wr = WRing(P, 8)

    P.dma(P.sp, "g", [], [gbuf], lambda: nc.sync.dma_start(out=g_sb[:], in_=gv))

    state = {"pset": 0, "oi": 0}

    def v3(ap2d):
        return ap2d.rearrange("p (s t) -> p s t", s=nsub)

    def pview(pset):
        return ps[:, pset * 2:pset * 2 + nsub, :]

    def pbufs(pset):
        return pb[pset * 2:pset * 2 + nsub]

    def gemm(wsrcs, kc_list, rhs_fn, rhs_bufs):
        pset = state["pset"]
        state["pset"] ^= 1
        slots = [wr.load(src, kc) for src, kc in zip(wsrcs, kc_list)]
        ktot = sum(kc_list)
        for s in range(nsub):
            def emit(s=s):
                kk = 0
                ins = None
                for (wt, wbuf), kc in zip(slots, kc_list):
                    for k in range(kc):
                        ins = nc.tensor.matmul(ps[:, pset * 2 + s, :], lhsT=wt[:, k, :], rhs=rhs_fn(kk, s),
                                               start=(kk == 0), stop=(kk == ktot - 1))
                        kk += 1
                return ins
            P.op(P.pe, [wb_ for (_, wb_) in slots] + rhs_bufs, [pb[pset * 2 + s]], emit)
        return pset

    def h_rhs(kk, s):
        return hT[:, kk, s * 512:(s + 1) * 512]

    def R_rhs(kk, s):
        return R[:, kk, s * 512:(s + 1) * 512]

    def norm_to_h(k, x_ap, g_ap, rstd_ap, reads):
        P.op(P.dve, reads + [gbuf], [hbufs[k]],
             lambda: nc.vector.scalar_tensor_tensor(out=hT[:, k, :], in0=x_ap, scalar=g_ap, in1=rstd_ap,
                                                    op0=ALU.mult, op1=ALU.mult))

    for tt in range(ntt):
        t0 = tt * TT
        emit_norm(P, C, lambda k: xT[k * 128:(k + 1) * 128, t0:t0 + TT], lambda k: xsrc_buf(tt, k), g_sb[:, 0, :],
                  norm_to_h, TT)
        for i in range(3):
            for k in range(8):
                r = 16 + i * 8 + k
                row0 = ((i * 2 + k % 2) * 4 + k // 2) * 128
                P.dma(P.sp, "y%d" % (i * 8 + k), [ybuf], [rb[r]],
                      lambda r=r, row0=row0: nc.sync.dma_start(
                          out=R[:, r, :], in_=yfull[row0:row0 + 128, t0:t0 + TT]))
        for m in range(16):
            for i in range(3):
                pset = gemm([wg[i * 16 + m]], [16], h_rhs, hbufs)
                P.op(P.act, pbufs(pset), [tb[0]],
                     lambda pset=pset: nc.scalar.activation(out=v3(tmp[:, 0, :]), in_=pview(pset), func=AF.Sigmoid))
                pset = gemm([wbr[i * 16 + m]], [8],
                            lambda kk, s, i=i: R[:, 16 + i * 8 + kk, s * 512:(s + 1) * 512],
                            rb[16 + i * 8:16 + i * 8 + 8])
                dsti = 1 if i == 0 else 2
                P.op(P.dve, pbufs(pset) + [tb[0]], [tb[dsti]],
                     lambda pset=pset, dsti=dsti: nc.vector.tensor_tensor(
                         out=v3(tmp[:, dsti, :]), in0=pview(pset), in1=v3(tmp[:, 0, :]), op=ALU.mult))
                if i == 1:
                    P.op(P.dve, [tb[1], tb[2]], [tb[1]],
                         lambda: nc.vector.tensor_tensor(out=tmp[:, 1, :], in0=tmp[:, 1, :], in1=tmp[:, 2, :],
                                                         op=ALU.add))
                elif i == 2:
                    P.op(P.dve, [tb[1], tb[2]], [rb[m]],
                         lambda m=m: nc.vector.tensor_tensor(out=R[:, m, :], in0=tmp[:, 1, :], in1=tmp[:, 2, :],
                                                             op=ALU.add))
        for n in range(16):
            pset = gemm([wo[n]], [16], R_rhs, rb[0:16])
            xi = P.xr
            P.xr = (P.xr + 1) % NX
            xsb = xsrc_buf(tt, n)
            P.dma(P.sp, "x%d" % xi, [xsb] if xsb else [], [xbufs[xi]],
                  lambda xi=xi, n=n: nc.sync.dma_start(out=xring[:, xi, :],
                                                       in_=xT[n * 128:(n + 1) * 128, t0:t0 + TT]))
            oi = state["oi"]
            state["oi"] ^= 1
            P.op(P.dve, pbufs(pset) + [xbufs[xi]], [ob[oi]],
                 lambda pset=pset, xi=xi, oi=oi: nc.vector.tensor_tensor(
                     out=v3(ost[:, oi, :]), in0=pview(pset), in1=v3(xring[:, xi, :]), op=ALU.add))
            P.dma(P.sp, "x1w%d" % oi, [ob[oi]], [x1bufs[tt][n]],
                  lambda oi=oi, n=n: nc.sync.dma_start(out=x1T[n * 128:(n + 1) * 128, t0:t0 + TT],
                                                       in_=ost[:, oi, :]))
        emit_norm(P, C, lambda k: x1T[k * 128:(k + 1) * 128, t0:t0 + TT], lambda k: x1bufs[tt][k],
                  g_sb[:, 1, :], norm_to_h, TT)
        for j in range(KFF):
            pg = gemm([wf1[2 * j]], [16], h_rhs, hbufs)
            ti = j % 2
            P.op(P.act, pbufs(pg), [tb[ti]],
                 lambda pg=pg, ti=ti: nc.scalar.activation(out=v3(tmp[:, ti, :]), in_=pview(pg), func=AF.Silu))
            pu = gemm([wf1[2 * j + 1]], [16], h_rhs, hbufs)
            P.op(P.dve, pbufs(pu) + [tb[ti]], [rb[j]],
                 lambda pu=pu, ti=ti, j=j: nc.vector.tensor_tensor(
                     out=v3(R[:, j, :]), in0=pview(pu), in1=v3(tmp[:, ti, :]), op=ALU.mult))
        for n in range(16):
            pset = gemm([wf2[n, :, 0:16, :], wf2[n, :, 16:32, :], wf2[n, :, 32:44, :]], [16, 16, 12], R_rhs, rb)
            xi = P.xr
            P.xr = (P.xr + 1) % NX
            P.dma(P.sp, "x%d" % xi, [x1bufs[tt][n]], [xbufs[xi]],
                  lambda xi=xi, n=n: nc.sync.dma_start(out=xring[:, xi, :],
                                                       in_=x1T[n * 128:(n + 1) * 128, t0:t0 + TT]))
            oi = state["oi"]
            state["oi"] ^= 1
            P.op(P.dve, pbufs(pset) + [xbufs[xi]], [ob[oi]],
                 lambda pset=pset, xi=xi, oi=oi: nc.vector.tensor_tensor(
                     out=v3(ost[:, oi, :]), in0=pview(pset), in1=v3(xring[:, xi, :]), op=ALU.add))
            if final_norm:
                P.dma(P.sp, "x2w%d" % oi, [ob[oi]], [x2bufs[tt][n]],
                      lambda oi=oi, n=n: nc.sync.dma_start(out=x2T[n * 128:(n + 1) * 128, t0:t0 + TT],
                                                           in_=ost[:, oi, :]))
            else:
                P.dma(P.sp, "ow", [ob[oi]], [outbufs[tt][n]],
                      lambda oi=oi, n=n: nc.sync.dma_start(out=oT[n * 128:(n + 1) * 128, t0:t0 + TT],
                                                           in_=ost[:, oi, :]))
        if final_norm:
            def norm_to_out(k, x_ap, g_ap, rstd_ap, reads):
                oi = state["oi"]
                state["oi"] ^= 1
                P.op(P.dve, reads + [gbuf], [ob[oi]],
                     lambda: nc.vector.scalar_tensor_tensor(out=ost[:, oi, :], in0=x_ap, scalar=g_ap, in1=rstd_ap,
                                                            op0=ALU.mult, op1=ALU.mult))
                P.dma(P.sp, "ow", [ob[oi]], [outbufs[tt][k]],
                      lambda: nc.sync.dma_start(out=dst[k * 128:(k + 1) * 128, t0:t0 + TT], in_=ost[:, oi, :]))

            def norm_to_hloc(k, x_ap, g_ap, rstd_ap, reads):
                oi = state["oi"]
                state["oi"] ^= 1
                P.op(P.dve, reads + [gbuf], [hsb_[oi]],
                     lambda: nc.vector.scalar_tensor_tensor(out=hst[:, oi, :], in0=x_ap, scalar=g_ap, in1=rstd_ap,
                                                            op0=ALU.mult, op1=ALU.mult))
                P.dma(P.sp, "ow", [hsb_[oi]], [outbufs[tt][k]],
                      lambda: nc.sync.dma_start(out=dst[k * 128:(k + 1) * 128, t0:t0 + TT], in_=hst[:, oi, :]))
            emit_norm(P, C, lambda k: x2T[k * 128:(k + 1) * 128, t0:t0 + TT], lambda k: x2bufs[tt][k],
                      g_sb[:, 2, :], norm_to_out if last else norm_to_hloc, TT)
    P.pop_scope()


def emit_N(P, ps_, pb_, NTOK, TT, xT, gm, hloc, outbufs):
    nc = P.nc
    P.push_scope()
    C = common_setup(P, TT, ps_, pb_)
    g_sb = P.sbuf("gN", [128, 16], F32)
    gbuf = Buf("gN")
    hst = P.sbuf("hstN", [128, 2, TT], BF16)
    hsb_ = [Buf("hstN%d" % i) for i in range(2)]
    P.dma(P.sp, "g", [], [gbuf], lambda: nc.sync.dma_start(out=g_sb[:], in_=gm))
    st = {"oi": 0}
    for tt in range(NTOK // TT):
        t0 = tt * TT

        def out_fn(k, x_ap, g_ap, rstd_ap, reads):
            oi = st["oi"]
            st["oi"] ^= 1
            P.op(P.dve, reads + [gbuf], [hsb_[oi]],
                 lambda: nc.vector.scalar_tensor_tensor(out=hst[:, oi, :], in0=x_ap, scalar=g_ap, in1=rstd_ap,
                                                        op0=ALU.mult, op1=ALU.mult))
            P.dma(P.sp, "ow", [hsb_[oi]], [outbufs[tt][k]],
                  lambda: nc.sync.dma_start(out=hloc[k * 128:(k + 1) * 128, t0:t0 + TT], in_=hst[:, oi, :]))
        emit_norm(P, C, lambda k: xT[k * 128:(k + 1) * 128, t0:t0 + TT], lambda k: None, g_sb, out_fn, TT)
    P.pop_scope()


def arrange_w(w, kc):
    K, N = w.shape
    return np.ascontiguousarray(w.reshape(K // 128, 128, N // 128, 128).transpose(2, 1, 0, 3))


def prep_B_weights(l, w_in, w_branch_rnn, w_branch_attn, w_branch_pool, w_out, w_ffn_in, w_ffn_out,
                   g_mix, g_ffn, g_final):
    off = D_IN_GATES
    wg = arrange_w(w_in[l][:, off:off + 3 * D], 16)
    wbr = np.concatenate([arrange_w(w_branch_rnn[l], 8), arrange_w(w_branch_attn[l], 8),
                          arrange_w(w_branch_pool[l], 8)], axis=0)
    wo = arrange_w(w_out[l], 16)
    f1 = w_ffn_in[l]
    f1i = np.stack([f1[:, :DFF].reshape(D, KFF, 128), f1[:, DFF:].reshape(D, KFF, 128)], axis=2).reshape(D, 2 * DFF)
    wf1 = arrange_w(f1i, 16)
    wf2 = arrange_w(w_ffn_out[l], 44)
    gv = np.ascontiguousarray(np.stack([g_mix[l].reshape(16, 128).T, g_ffn[l].reshape(16, 128).T,
                                        g_final.reshape(16, 128).T], axis=1)).astype(np.float32)
    return dict(wg=wg, wbr=wbr, wo=wo, wf1=wf1, wf2=wf2, gv=gv)


D_IN_GATES = 1024 * 2 + 1024 * 3 + 8 + 1024


NPAR = 24
SQRT_DH = 11.313708498984761
GELU_C = 1.5957691216057308


def emit_A(P, ps, pb, S, d):
    nc = P.nc
    TT = 512
    NT = S // TT
    NKB = S // 128
    NTOK = S // 4
    hfull, hfbuf = d["hfull"], d["hfbuf"]
    w1d, w2d, wrgd, wpld, pad, ptabd, yo = d["w1"], d["w2"], d["wrg"], d["wpl"], d["pa"], d["ptab"], d["yo"]
    yob = [Buf("yo%d" % i) for i in range(8)]
    P.push_scope()
    ones_bf = P.sbuf("ones_bfA", [128, 128], BF16)
    onesb = Buf("onesA")
    P.op(P.dve, [], [onesb], lambda: nc.vector.memset(ones_bf[:], 1.0))

    def sb(name, shape, dt=F32):
        return P.sbuf(name, shape, dt), Buf(name)

    pa, pabuf = sb("pa_sb", [128, NPAR])
    ptab, ptabbuf = sb("ptab_sb", [128, 4, 16])
    cst, cstbuf = sb("cst", [128, 8])
    w1, w1buf = sb("w1_sb", [128, 16, 768], BF16)
    w2, w2buf = sb("w2_sb", [128, 16, 512], BF16)
    wrg, wrgbuf = sb("wrg_sb", [128, 4, 128], BF16)
    wpl, wplbuf = sb("wpl_sb", [128, 2, 256], BF16)
    hT = P.sbuf("hT", [128, 2, KD, TT], BF16)
    hbufs = [[Buf("h%d_%d" % (j, k)) for k in range(KD)] for j in range(2)]
    NTMP = 14
    tmp = P.sbuf("tmp", [128, NTMP, TT], F32)
    tb = [Buf("t%d" % i) for i in range(NTMP)]
    tmpb = P.sbuf("tmpb", [128, 6, TT], BF16)
    tbb = [Buf("tb%d" % i) for i in range(6)]
    xb = P.sbuf("xb", [128, 2, TT + 3], F32)
    xbb = [Buf("xb0"), Buf("xb1")]
    hcar = P.sbuf("hcar", [128, 2], F32)
    hcb = [Buf("hc0"), Buf("hc1")]
    ub = P.sbuf("ub", [128, 2, TT + 15], F32)
    ubb = [Buf("ub0"), Buf("ub1")]
    sw = P.sbuf("sw", [128, 4, TT + 15], F32)
    swb = [Buf("sw%d" % i) for i in range(4)]
    t16 = P.sbuf("t16", [128, 2, 16], F32)
    t16b = [Buf("t16a"), Buf("t16b")]
    KT, ktbuf_all = sb("KT", [128, S], BF16)
    ktb = [Buf("kt%d" % i) for i in range(NT)]
    V = P.sbuf("V", [128, NKB, 128], BF16)
    vb = [Buf("v%d" % i) for i in range(NT)]
    negF = P.sbuf("negF", [128, NKB], F32)
    nfb = [Buf("nf%d" % i) for i in range(NT)]
    bm, bmbuf = sb("bm", [128, NKB])
    ct, ctbuf = sb("ct", [128, 1])
    fc = P.sbuf("fc", [128, 2], F32)
    fcb = [Buf("fc0"), Buf("fc1")]
    one32, one32b = sb("one32", [128, 128], F32)
    onesT, onesTb = sb("onesT", [128, TT], F32)
    Et = P.sbuf("Et", [128, 5, TT], BF16)
    etb = [Buf("et%d" % i) for i in range(5)]
    negcol, ncbuf = sb("negcol", [128, 4])
    tri, trib = sb("tri", [128, 128], BF16)
    pT = P.sbuf("pT", [128, 3, TT], BF16)
    ptb = [Buf("pt%d" % i) for i in range(3)]
    state = {"bank": 0, "pt": 0, "yo": 0, "tmp": 0}

    P.dma(P.sp, "c0a", [], [pabuf], lambda: nc.sync.dma_start(out=pa[:], in_=pad))
    P.dma(P.sp, "c0b", [], [ptabbuf], lambda: nc.sync.dma_start(out=ptab[:], in_=ptabd))
    for k in range(KD):
        P.dma(P.pool, "cw1", [], [w1buf],
              lambda k=k: nc.gpsimd.dma_start(out=w1[:, k, :], in_=w1d[:, k, :], max_dma_last_dim=4096))
    P.dma(P.pool, "cwr", [], [wrgbuf], lambda: nc.gpsimd.dma_start(out=wrg[:], in_=wrgd, max_dma_last_dim=4096))
    P.dma(P.pool, "cwp", [], [wplbuf], lambda: nc.gpsimd.dma_start(out=wpl[:], in_=wpld, max_dma_last_dim=4096))
    P.op(P.dve, [], [one32b], lambda: nc.vector.memset(one32[:], 1.0))
    P.op(P.dve, [], [one32b], lambda: nc.vector.memset(onesT[:], 1.0))
    P.op(P.dve, [pabuf], [pabuf],
         lambda: nc.vector.tensor_scalar(out=pa[:, 18:20], in0=pa[:, 18:20], scalar1=-1.0, scalar2=None,
                                         op0=ALU.mult))
    P.op(P.dve, [], [xbb[0], xbb[1]], lambda: nc.vector.memset(xb[:], 0.0))
    P.op(P.dve, [], [ubb[0], ubb[1]], lambda: nc.vector.memset(ub[:], 0.0))
    P.op(P.dve, [], [hcb[0], hcb[1]], lambda: nc.vector.memset(hcar[:], 0.0))
    P.op(P.dve, [], swb, lambda: nc.vector.memset(sw[:], 0.0))
    P.op(P.dve, [], [ncbuf], lambda: nc.vector.memset(negcol[:], 0.0))
    P.op(P.dve, [], [trib], lambda: nc.vector.memset(tri[:], 1.0))
    P.op(P.pool, [trib], [trib],
         lambda: nc.gpsimd.affine_select(out=tri[:], in_=tri[:], pattern=[[1, 128]], compare_op=ALU.is_ge,
                                         fill=0.0, base=0, channel_multiplier=-1))
    P.op(P.act, [pabuf], [cstbuf],
         lambda: nc.scalar.activation(out=cst[:, 0:2], in_=pa[:, 14:16], func=AF.Exp, scale=-1.0))
    P.op(P.act, [cstbuf], [cstbuf],
         lambda: nc.scalar.activation(out=cst[:, 0:2], in_=cst[:, 0:2], func=AF.Ln, bias=one32[:, 0:1]))
    P.op(P.dve, [cstbuf], [cstbuf],
         lambda: nc.vector.tensor_scalar(out=cst[:, 2:4], in0=cst[:, 0:2], scalar1=-16.0, scalar2=None,
                                         op0=ALU.mult))
    P.op(P.dve, [cstbuf], [cstbuf],
         lambda: nc.vector.tensor_scalar(out=cst[:, 0:2], in0=cst[:, 0:2], scalar1=-8.0, scalar2=None,
                                         op0=ALU.mult))

    def load_w2(hd):
        for k in range(KD):
            P.dma(P.pool, "cw2", [], [w2buf],
                  lambda k=k: nc.gpsimd.dma_start(out=w2[:, k, :], in_=w2d[hd * 128:(hd + 1) * 128, k, :],
                                                  max_dma_last_dim=4096))
    load_w2(0)

    def bank():
        b = state["bank"]
        state["bank"] = (b + 1) % 8
        return b

    def T():
        i = state["tmp"]
        state["tmp"] = (i + 1) % NTMP
        return i

    def gemm_chunk(wt, wbuf, c0, hj, bk):
        def emit():
            ins = None
            for k in range(KD):
                ins = nc.tensor.matmul(ps[:, bk, :], lhsT=wt[:, k, c0:c0 + 128], rhs=hT[:, hj, k, :],
                                       start=(k == 0), stop=(k == KD - 1))
            return ins
        P.op(P.pe, [wbuf] + hbufs[hj], [pb[bk]], emit)

    def store_y(which, row0, src_ap, src_buf, t0):
        i = state["yo"]
        state["yo"] = (i + 1) % 8
        P.dma(P.sp, "yo%d" % i, [src_buf], [yob[i]],
              lambda: nc.sync.dma_start(
                  out=yo[(t0 // NTOK) * 768 + which * 256 + row0:(t0 // NTOK) * 768 + which * 256 + row0 + 128,
                         (t0 % NTOK):(t0 % NTOK) + TT], in_=src_ap))

    def load_h(it, hj):
        r, off = divmod(it * TT, NTOK)
        for k in range(KD):
            P.dma(P.sp, "hl%d" % hj, [hfbuf], [hbufs[hj][k]],
                  lambda k=k: nc.sync.dma_start(out=hT[:, hj, k, :],
                                                in_=hfull[k * 512 + r * 128:k * 512 + (r + 1) * 128, off:off + TT]))

    for it in range(NT):
        t0 = it * TT
        hj = it % 2
        if it == 0:
            load_h(0, 0)
        if it + 1 < NT:
            load_h(it + 1, (it + 1) % 2)

        def lru_gen(blk):
            bk = 3 * blk
            gemm_chunk(w1, w1buf, blk * 128, hj, bk)
            yield
            P.op(P.act, [pb[bk]], [xbb[blk]],
                 lambda bk=bk: nc.scalar.activation(out=xb[:, blk, 3:TT + 3], in_=ps[:, bk, :], func=AF.Copy))
            yield
            u, r, ig, a, a2, z = [blk * 6 + i_ for i_ in range(6)]
            P.op(P.act, [xbb[blk], pabuf], [tb[u]],
                 lambda: nc.scalar.activation(out=tmp[:, u, :], in_=xb[:, blk, 3:TT + 3], func=AF.Identity,
                                              scale=pa[:, blk * 4 + 3:blk * 4 + 4], bias=pa[:, 8 + blk:9 + blk]))
            yield
            for tap in (2, 1, 0):
                P.op(P.dve, [xbb[blk], pabuf, tb[u]], [tb[u]],
                     lambda tap=tap: nc.vector.scalar_tensor_tensor(
                         out=tmp[:, u, :], in0=xb[:, blk, tap:tap + TT], scalar=pa[:, blk * 4 + tap:blk * 4 + tap + 1],
                         in1=tmp[:, u, :], op0=ALU.mult, op1=ALU.add))
                yield
            P.op(P.dve, [xbb[blk]], [xbb[blk]],
                 lambda: nc.vector.tensor_copy(out=xb[:, blk, 0:3], in_=xb[:, blk, TT:TT + 3]))
            yield
            ubf = blk
            P.op(P.dve, [tb[u]], [tbb[ubf]], lambda: nc.vector.tensor_copy(out=tmpb[:, ubf, :], in_=tmp[:, u, :]))
            yield
            bkr, bki = 3 * blk + 1, 3 * blk + 2
            P.op(P.pe, [tbb[ubf], wrgbuf], [pb[bkr]],
                 lambda: nc.tensor.matmul(ps[:, bkr, :], lhsT=wrg[:, blk * 2, :], rhs=tmpb[:, ubf, :],
                                          start=True, stop=True))
            yield
            P.op(P.pe, [tbb[ubf], wrgbuf], [pb[bki]],
                 lambda: nc.tensor.matmul(ps[:, bki, :], lhsT=wrg[:, blk * 2 + 1, :], rhs=tmpb[:, ubf, :],
                                          start=True, stop=True))
            yield
            P.op(P.act, [pb[bkr], pabuf], [tb[r]],
                 lambda: nc.scalar.activation(out=tmp[:, r, :], in_=ps[:, bkr, :], func=AF.Sigmoid,
                                              bias=pa[:, 10 + blk:11 + blk]))
            yield
            P.op(P.act, [pb[bki], pabuf], [tb[ig]],
                 lambda: nc.scalar.activation(out=tmp[:, ig, :], in_=ps[:, bki, :], func=AF.Sigmoid,
                                              bias=pa[:, 12 + blk:13 + blk]))
            yield
            P.op(P.act, [tb[r], cstbuf], [tb[a]],
                 lambda: nc.scalar.activation(out=tmp[:, a, :], in_=tmp[:, r, :], func=AF.Exp,
                                              scale=cst[:, blk:blk + 1]))
            yield
            P.op(P.act, [tb[r], cstbuf], [tb[a2]],
                 lambda: nc.scalar.activation(out=tmp[:, a2, :], in_=tmp[:, r, :], func=AF.Exp,
                                              scale=cst[:, 2 + blk:3 + blk]))
            yield
            P.op(P.dve, [tb[a2]], [tb[a2]],
                 lambda: nc.vector.tensor_scalar(out=tmp[:, a2, :], in0=tmp[:, a2, :], scalar1=-1.0, scalar2=1.0,
                                                 op0=ALU.mult, op1=ALU.add))
            yield
            P.op(P.act, [tb[a2]], [tb[a2]],
                 lambda: nc.scalar.activation(out=tmp[:, a2, :], in_=tmp[:, a2, :], func=AF.Sqrt))
            yield
            P.op(P.dve, [tb[ig], tb[u]], [tb[ig]],
                 lambda: nc.vector.tensor_tensor(out=tmp[:, ig, :], in0=tmp[:, ig, :], in1=tmp[:, u, :], op=ALU.mult))
            yield
            P.op(P.dve, [tb[ig], tb[a2]], [tb[ig]],
                 lambda: nc.vector.tensor_tensor(out=tmp[:, ig, :], in0=tmp[:, ig, :], in1=tmp[:, a2, :],
                                                 op=ALU.mult))
            yield
            P.op(P.dve, [tb[a], tb[ig], hcb[blk]], [tb[r]],
                 lambda: nc.vector.tensor_tensor_scan(out=tmp[:, r, :], data0=tmp[:, a, :], data1=tmp[:, ig, :],
                                                      initial=hcar[:, blk:blk + 1], op0=ALU.mult, op1=ALU.add))
            yield
            P.op(P.dve, [tb[r]], [hcb[blk]],
                 lambda: nc.vector.tensor_copy(out=hcar[:, blk:blk + 1], in_=tmp[:, r, TT - 1:TT]))
            yield
            bky = 3 * blk
            gemm_chunk(w1, w1buf, 256 + blk * 128, hj, bky)
            yield
            P.op(P.act, [pb[bky]], [tb[z]],
                 lambda: nc.scalar.activation(out=tmp[:, z, :], in_=ps[:, bky, :], func=AF.Square))
            yield
            P.op(P.dve, [tb[z]], [tb[z]],
                 lambda: nc.vector.tensor_scalar(out=tmp[:, z, :], in0=tmp[:, z, :], scalar1=0.044715, scalar2=1.0,
                                                 op0=ALU.mult, op1=ALU.add))
            yield
            P.op(P.dve, [tb[z], pb[bky]], [tb[z]],
                 lambda: nc.vector.tensor_tensor(out=tmp[:, z, :], in0=ps[:, bky, :], in1=tmp[:, z, :], op=ALU.mult))
            yield
            P.op(P.act, [tb[z]], [tb[z]],
                 lambda: nc.scalar.activation(out=tmp[:, z, :], in_=tmp[:, z, :], func=AF.Sigmoid, scale=GELU_C))
            yield
            P.op(P.dve, [tb[z], pb[bky]], [tb[z]],
                 lambda: nc.vector.tensor_tensor(out=tmp[:, z, :], in0=ps[:, bky, :], in1=tmp[:, z, :], op=ALU.mult))
            yield
            yb_ = 2 + blk
            P.op(P.dve, [tb[z], tb[r]], [tbb[yb_]],
                 lambda: nc.vector.tensor_tensor(out=tmpb[:, yb_, :], in0=tmp[:, z, :], in1=tmp[:, r, :], op=ALU.mult))
            yield
            store_y(0, blk * 128, tmpb[:, yb_, :], tbb[yb_], t0)
            yield

        def pool_gen(cc):
            bk = 6
            gemm_chunk(w1, w1buf, 512 + cc * 128, hj, bk)
            yield
            P.op(P.act, [pb[bk]], [ubb[cc]],
                 lambda bk=bk: nc.scalar.activation(out=ub[:, cc, 15:TT + 15], in_=ps[:, bk, :], func=AF.Copy))
            yield
            W_ = TT + 15
            P.op(P.dve, [ubb[cc]], [swb[0]],
                 lambda: nc.vector.tensor_tensor(out=sw[:, 0, 1:W_], in0=ub[:, cc, 1:W_], in1=ub[:, cc, 0:W_ - 1],
                                                 op=ALU.add))
            yield
            P.op(P.dve, [swb[0]], [swb[1]],
                 lambda: nc.vector.tensor_tensor(out=sw[:, 1, 3:W_], in0=sw[:, 0, 3:W_], in1=sw[:, 0, 1:W_ - 2],
                                                 op=ALU.add))
            yield
            P.op(P.dve, [swb[1]], [swb[2]],
                 lambda: nc.vector.tensor_tensor(out=sw[:, 2, 7:W_], in0=sw[:, 1, 7:W_], in1=sw[:, 1, 3:W_ - 4],
                                                 op=ALU.add))
            yield
            P.op(P.dve, [swb[2]], [swb[3]],
                 lambda: nc.vector.tensor_tensor(out=sw[:, 3, 15:W_], in0=sw[:, 2, 15:W_], in1=sw[:, 2, 7:W_ - 8],
                                                 op=ALU.add))
            yield
            rv = 12 + cc
            P.op(P.dve, [swb[0], ubb[cc], pabuf], [tb[rv]],
                 lambda: nc.vector.scalar_tensor_tensor(out=tmp[:, rv, :], in0=sw[:, 0, 15:W_], scalar=pa[:, 20:21],
                                                        in1=ub[:, cc, 15:W_], op0=ALU.mult, op1=ALU.subtract))
            yield
            for wi in (1, 2):
                P.op(P.dve, [swb[wi], tb[rv], pabuf], [tb[rv]],
                     lambda wi=wi: nc.vector.scalar_tensor_tensor(out=tmp[:, rv, :], in0=sw[:, wi, 15:W_],
                                                                  scalar=pa[:, 20 + wi:21 + wi], in1=tmp[:, rv, :],
                                                                  op0=ALU.mult, op1=ALU.add))
                yield
            pl = 4 + cc
            P.op(P.dve, [swb[3], tb[rv], pabuf], [tbb[pl]],
                 lambda: nc.vector.scalar_tensor_tensor(out=tmpb[:, pl, :], in0=sw[:, 3, 15:W_], scalar=pa[:, 23:24],
                                                        in1=tmp[:, rv, :], op0=ALU.mult, op1=ALU.add))
            yield
            if it == 0:
                P.op(P.dve, [swb[0], ptabbuf], [t16b[0]],
                     lambda: nc.vector.tensor_tensor(out=t16[:, 0, :], in0=sw[:, 0, 15:31], in1=ptab[:, 0, :],
                                                     op=ALU.mult))
                yield
                for wi in (1, 2, 3):
                    P.op(P.dve, [swb[wi], ptabbuf], [t16b[1]],
                         lambda wi=wi: nc.vector.tensor_tensor(out=t16[:, 1, :], in0=sw[:, wi, 15:31],
                                                               in1=ptab[:, wi, :], op=ALU.mult))
                    yield
                    P.op(P.dve, [t16b[0], t16b[1]], [t16b[0]],
                         lambda: nc.vector.tensor_tensor(out=t16[:, 0, :], in0=t16[:, 0, :], in1=t16[:, 1, :],
                                                         op=ALU.add))
                    yield
                P.op(P.dve, [t16b[0], ubb[cc]], [tbb[pl]],
                     lambda: nc.vector.tensor_tensor(out=tmpb[:, pl, 0:16], in0=t16[:, 0, :], in1=ub[:, cc, 15:31],
                                                     op=ALU.subtract))
                yield
            P.op(P.dve, [ubb[cc]], [ubb[cc]],
                 lambda: nc.vector.tensor_copy(out=ub[:, cc, 0:15], in_=ub[:, cc, TT:TT + 15]))
            yield
        def pool_both():
            yield from pool_gen(0)
            yield from pool_gen(1)
        gens = [lru_gen(0), lru_gen(1), pool_both()]
        while gens:
            for g_ in list(gens):
                try:
                    next(g_)
                except StopIteration:
                    gens.remove(g_)
        for dc in range(2):
            bk = 6 + dc
            def emit(bk=bk, dc=dc):
                ins = None
                for cc in range(2):
                    ins = nc.tensor.matmul(ps[:, bk, :], lhsT=wpl[:, cc, dc * 128:(dc + 1) * 128],
                                           rhs=tmpb[:, 4 + cc, :], start=(cc == 0), stop=(cc == 1))
                return ins
            P.op(P.pe, [tbb[4], tbb[5], wplbuf], [pb[bk]], emit)
            yi = dc
            pti = state["pt"]
            state["pt"] = (pti + 1) % 3
            P.op(P.act, [pb[bk], pabuf], [ptb[pti]],
                 lambda bk=bk, pti=pti, dc=dc: nc.scalar.activation(out=pT[:, pti, :], in_=ps[:, bk, :],
                                                                    func=AF.Identity,
                                                                    scale=pa[:, 16 + dc:17 + dc]))
            store_y(2, dc * 128, pT[:, pti, :], ptb[pti], t0)

    d["after_part"]((0, 1, 4, 5))
    SB0, SB1, OB, SMB, QKB, VB, FB, SMALL = 0, 1, 2, 3, 4, 5, 6, 7
    scale = 1.0 / SQRT_DH
    for hd in range(2):
        if hd == 1:
            load_w2(1)
            d["after_part"]((2,))
        P.op(P.dve, [], [fcb[0]], lambda: nc.vector.memset(fc[:, 0:1], 0.0))
        for it in range(NT):
            t0 = it * TT
            hj = it % 2
            if it == 0:
                load_h(0, 0)
            if it + 1 < NT:
                load_h(it + 1, (it + 1) % 2)
            def emit_f():
                ins = None
                for k in range(KD):
                    ins = nc.tensor.matmul(ps[:, FB, :], lhsT=w2[:, k, 384:512], rhs=hT[:, hj, k, :],
                                           start=(k == 0), stop=(k == KD - 1))
                return ins
            P.op(P.pe, [w2buf] + hbufs[hj], [pb[FB]], emit_f)
            e, fa, fr = T(), T(), T()
            P.op(P.act, [pb[FB], pabuf], [tb[e]],
                 lambda: nc.scalar.activation(out=tmp[:, e, :], in_=ps[:, FB, :], func=AF.Exp, scale=-1.0,
                                              bias=pa[:, 18 + hd:19 + hd]))
            P.op(P.act, [tb[e], one32b], [tb[e]],
                 lambda: nc.scalar.activation(out=tmp[:, e, :], in_=tmp[:, e, :], func=AF.Ln, bias=one32[:, 0:1]))
            fi, fo = it % 2, (it + 1) % 2
            P.op(P.dve, [tb[e], one32b, fcb[fi]], [tb[fa]],
                 lambda: nc.vector.tensor_tensor_scan(out=tmp[:, fa, :], data0=onesT[:, :], data1=tmp[:, e, :],
                                                      initial=fc[:, fi:fi + 1], op0=ALU.mult, op1=ALU.subtract))
            P.op(P.dve, [tb[fa]], [fcb[fo]],
                 lambda: nc.vector.tensor_copy(out=fc[:, fo:fo + 1], in_=tmp[:, fa, TT - 1:TT]))
            P.op(P.dve, [tb[fa], fcb[fi]], [tb[fr]],
                 lambda: nc.vector.tensor_scalar(out=tmp[:, fr, :], in0=tmp[:, fa, :], scalar1=fc[:, fi:fi + 1],
                                                 scalar2=None, op0=ALU.subtract))
            for j in (1, 2, 3):
                P.op(P.dve, [tb[fr]], [ncbuf],
                     lambda j=j: nc.vector.tensor_scalar(out=negcol[:, j:j + 1], in0=tmp[:, fr, j * 128 - 1:j * 128],
                                                         scalar1=-1.0, scalar2=None, op0=ALU.mult))
            P.op(P.act, [tb[fr]], [etb[0]],
                 lambda: nc.scalar.activation(out=Et[:, 0, :], in_=tmp[:, fr, :], func=AF.Exp))
            for j in range(4):
                P.op(P.act, [tb[fr], ncbuf], [etb[1 + j]],
                     lambda j=j: nc.scalar.activation(out=Et[:, 1 + j, j * 128:TT], in_=tmp[:, fr, j * 128:TT],
                                                      func=AF.Exp, bias=negcol[:, j:j + 1]))
                P.op(P.dve, [etb[1 + j], trib], [etb[1 + j]],
                     lambda j=j: nc.vector.tensor_tensor(out=Et[:, 1 + j, j * 128:(j + 1) * 128],
                                                         in0=Et[:, 1 + j, j * 128:(j + 1) * 128], in1=tri[:, :],
                                                         op=ALU.mult))
            qi = 0
            gemm_chunk(w2, w2buf, 0, hj, QKB)
            P.op(P.act, [pb[QKB]], [tbb[qi]],
                 lambda: nc.scalar.activation(out=tmpb[:, qi, :], in_=ps[:, QKB, :], func=AF.Copy))
            gemm_chunk(w2, w2buf, 128, hj, QKB)
            P.op(P.act, [pb[QKB]], [ktb[it]],
                 lambda: nc.scalar.activation(out=KT[:, t0:t0 + TT], in_=ps[:, QKB, :], func=AF.Copy))
            def emit_v():
                ins = None
                for tbk in range(4):
                    for k in range(KD):
                        ins = nc.tensor.matmul(ps[:, VB, tbk * 128:(tbk + 1) * 128],
                                               lhsT=hT[:, hj, k, tbk * 128:(tbk + 1) * 128], rhs=w2[:, k, 256:384],
                                               start=(k == 0), stop=(k == KD - 1))
                return ins
            P.op(P.pe, [w2buf] + hbufs[hj], [pb[VB]], emit_v)
            P.op(P.act, [pb[VB]], [vb[it]],
                 lambda: nc.scalar.activation(out=V[:, it * 4:(it + 1) * 4, :],
                                              in_=ps[:, VB, :].rearrange("p (a b) -> p a b", a=4), func=AF.Copy))
            def emit_small():
                ins = None
                for tbk in range(4):
                    ins = nc.tensor.matmul(ps[:, SMALL, tbk:tbk + 1], lhsT=tmp[0:1, fa, tbk * 128:(tbk + 1) * 128],
                                           rhs=one32[0:1, 0:1], start=True, stop=True)
                ins = nc.tensor.matmul(ps[:, SMALL, 4:5], lhsT=one32[0:1, :], rhs=fc[0:1, fi:fi + 1],
                                       start=True, stop=True)
                return ins
            P.op(P.pe, [tb[fa], one32b, fcb[fi]], [pb[SMALL]], emit_small)
            P.op(P.dve, [pb[SMALL]], [nfb[it]],
                 lambda: nc.vector.tensor_scalar(out=negF[:, it * 4:(it + 1) * 4], in0=ps[:, SMALL, 0:4],
                                                 scalar1=-1.0, scalar2=None, op0=ALU.mult))
            P.op(P.dve, [pb[SMALL]], [ctbuf], lambda: nc.vector.tensor_copy(out=ct[:, :], in_=ps[:, SMALL, 4:5]))
            nkb = (it + 1) * 4
            P.op(P.dve, nfb[0:it + 1] + [ctbuf], [bmbuf],
                 lambda: nc.vector.tensor_scalar(out=bm[:, 0:nkb], in0=negF[:, 0:nkb], scalar1=ct[:, 0:1],
                                                 scalar2=None, op0=ALU.add))
            P.op(P.dve, [bmbuf, ncbuf], [bmbuf],
                 lambda: nc.vector.tensor_tensor(out=bm[:, it * 4:it * 4 + 4], in0=bm[:, it * 4:it * 4 + 4],
                                                 in1=negcol[:, 0:4], op=ALU.subtract))
            SBANKS = [SB0, SB1, QKB, VB]
            PRE = 3

            def issue_s(kb):
                j = kb - it * 4
                c0 = max(j, 0) * 128
                sbk = SBANKS[kb % 4]

                def emit_s():
                    return nc.tensor.matmul(ps[:, sbk, c0:TT], lhsT=KT[:, kb * 128:(kb + 1) * 128],
                                            rhs=tmpb[:, qi, c0:TT], start=True, stop=True)
                P.op(P.pe, [ktb[kb // 4], tbb[qi]], [pb[sbk]], emit_s)

            for kb in range(min(PRE, nkb)):
                issue_s(kb)
            for kb in range(nkb):
                if kb + PRE < nkb:
                    issue_s(kb + PRE)
                j = kb - it * 4
                c0 = max(j, 0) * 128
                sbk = SBANKS[kb % 4]
                pti = state["pt"]
                state["pt"] = (pti + 1) % 3
                P.op(P.act, [pb[sbk], bmbuf], [ptb[pti]],
                     lambda: nc.scalar.activation(out=pT[:, pti, c0:TT], in_=ps[:, sbk, c0:TT], func=AF.Exp,
                                                  scale=scale, bias=bm[:, kb:kb + 1]))
                ei = 0 if j < 0 else 1 + j
                P.op(P.dve, [ptb[pti], etb[ei]], [ptb[pti]],
                     lambda: nc.vector.tensor_tensor(out=pT[:, pti, c0:TT], in0=pT[:, pti, c0:TT],
                                                     in1=Et[:, ei, c0:TT], op=ALU.mult))

                def emit_o():
                    nc.tensor.matmul(ps[:, OB, c0:TT], lhsT=V[:, kb, :], rhs=pT[:, pti, c0:TT],
                                     start=(kb == 0), stop=(kb == nkb - 1))
                    return nc.tensor.matmul(ps[:, SMB, c0:TT], lhsT=ones_bf[:, :], rhs=pT[:, pti, c0:TT],
                                            start=(kb == 0), stop=(kb == nkb - 1))
                P.op(P.pe, [vb[kb // 4], ptb[pti], onesb], [pb[OB], pb[SMB]], emit_o)
            rc = T()
            P.op(P.dve, [pb[SMB]], [tb[rc]], lambda: nc.vector.reciprocal(out=tmp[:, rc, :], in_=ps[:, SMB, :]))
            yi = 1 + it % 2
            P.op(P.dve, [pb[OB], tb[rc]], [tbb[yi]],
                 lambda: nc.vector.tensor_tensor(out=tmpb[:, yi, :], in0=ps[:, OB, :], in1=tmp[:, rc, :], op=ALU.mult))
            store_y(1, hd * 128, tmpb[:, yi, :], tbb[yi], t0)
    d["after_part"]((3,))
    P.pop_scope()


POOL_WINDOWS = (2, 4, 8, 16)


def prep_A_inputs(l, b, j, xT_b, g_mix, w_in, b_forget, conv_w, conv_b, w_rg, b_rg, w_ig, b_ig, lru_lambda,
                  w_pool, pool_scale):
    W = w_in[l]

    def arr(cols):
        return np.ascontiguousarray(cols.reshape(16, 128, cols.shape[1]).transpose(1, 0, 2))
    w1 = np.concatenate([W[:, 2 * j * 128:(2 * j + 2) * 128], W[:, 1024 + 2 * j * 128:1024 + (2 * j + 2) * 128],
                         W[:, 5128 + j * 256:5128 + (j + 1) * 256]], axis=1)
    w2 = []
    for hd in range(2):
        h = 2 * j + hd
        w2.append(arr(np.concatenate([W[:, 2048 + h * 128:2048 + (h + 1) * 128],
                                      W[:, 3072 + h * 128:3072 + (h + 1) * 128],
                                      W[:, 4096 + h * 128:4096 + (h + 1) * 128],
                                      np.repeat(W[:, 5120 + h:5121 + h], 128, axis=1)], axis=1)))
    wrg = np.stack([w_rg[l][2 * j], w_ig[l][2 * j], w_rg[l][2 * j + 1], w_ig[l][2 * j + 1]], axis=1)
    wpl = np.ascontiguousarray(w_pool[l][j].reshape(2, 128, 256).transpose(1, 0, 2))
    pa = np.zeros((128, NPAR), np.float32)
    for blk in range(2):
        ch = slice((2 * j + blk) * 128, (2 * j + blk + 1) * 128)
        for tap in range(4):
            pa[:, blk * 4 + tap] = conv_w[l][tap, ch]
        pa[:, 8 + blk] = conv_b[l][ch]
        pa[:, 10 + blk] = b_rg[l][ch]
        pa[:, 12 + blk] = b_ig[l][ch]
        pa[:, 14 + blk] = lru_lambda[l][ch]
        pa[:, 16 + blk] = pool_scale[l][j * 256 + blk * 128:j * 256 + (blk + 1) * 128]
        pa[:, 18 + blk] = b_forget[l][2 * j + blk]
    ptab = np.zeros((128, 4, 16), np.float32)
    for wi, w in enumerate(POOL_WINDOWS):
        if wi == j:
            pa[:, 20 + wi] = 1.0 / w
            ptab[:, wi, :] = 1.0 / np.minimum(np.arange(1, 17), w)
    return dict(w1=arr(w1), w2=np.stack(w2),
                wrg=np.ascontiguousarray(wrg), wpl=wpl, pa=pa, ptab=ptab)


GROUPS = [[0, 1, 2, 3], [4, 5, 6, 7]]


def build_fused(S, B_TT):
    P = Prog()
    nc = P.nc
    NTOK = S // 4
    ntt = NTOK // B_TT
    ps = P.psum("ps", [128, 8, 512], F32)
    pb = [Buf("p%d" % i) for i in range(8)]
    xT = P.dram("xT", [D, NTOK], F32, "ExternalInput")
    gm0 = P.dram("gm0", [128, 16], F32, "ExternalInput")
    ptab = P.dram("ptab", [128, 4, 16], F32, "ExternalInput")
    oT = P.dram("oT", [D, NTOK], F32, "ExternalOutput")
    A_in, B_in = [], []
    for l in range(2):
        A_in.append(dict(
            w1=P.dram("w1_%d" % l, [128, 16, 768], F32, "ExternalInput"),
            w2=P.dram("w2_%d" % l, [256, 16, 512], F32, "ExternalInput"),
            wrg=P.dram("wrg_%d" % l, [128, 4, 128], F32, "ExternalInput"),
            wpl=P.dram("wpl_%d" % l, [128, 2, 256], F32, "ExternalInput"),
            pa=P.dram("pa_%d" % l, [128, NPAR], F32, "ExternalInput"),
            ptab=ptab))
        B_in.append(dict(
            wg=P.dram("wg_%d" % l, [48, 128, 16, 128], F32, "ExternalInput"),
            wbr=P.dram("wbr_%d" % l, [48, 128, 8, 128], F32, "ExternalInput"),
            wo=P.dram("wo_%d" % l, [16, 128, 16, 128], F32, "ExternalInput"),
            wf1=P.dram("wf1_%d" % l, [88, 128, 16, 128], F32, "ExternalInput"),
            wf2=P.dram("wf2_%d" % l, [16, 128, 44, 128], F32, "ExternalInput"),
            gv=P.dram("gv_%d" % l, [128, 3, 16], F32, "ExternalInput")))
    hloc = [P.dram("hloc%d" % l, [D, NTOK], BF16) for l in range(2)]
    hfull = [P.dram("hfull%d" % l, [4 * D, NTOK], BF16) for l in range(2)]
    yo = [P.dram("yo%d" % l, [4 * 768, NTOK], BF16) for l in range(2)]
    yfull = [P.dram("yfull%d" % l, [4 * 4 * 768, NTOK], BF16) for l in range(2)]
    x1T = [P.dram("x1T%d" % l, [D, NTOK], F32) for l in range(2)]
    x2T = [P.dram("x2T%d" % l, [D, NTOK], F32) for l in range(2)]
    ysl = [P.dram("ysl%d" % l, [4 * 768, NTOK], BF16) for l in range(2)]
    qreg = nc.sync.partition_id() % 4

    def mk():
        return [[Buf("b") for _ in range(16)] for _ in range(ntt)]

    emit_N(P, ps, pb, NTOK, B_TT, xT, gm0, hloc[0], mk())
    x2prev = None
    outb = None
    for l in range(2):
        hfbuf = Buf("hf%d" % l)
        for k in range(KD):
            P.cc([], [hfbuf], lambda k=k: nc.gpsimd.collective_compute(
                "AllGather", ALU.bypass, replica_groups=GROUPS, ins=[hloc[l][k * 128:(k + 1) * 128, :]],
                outs=[hfull[l][k * 512:(k + 1) * 512, :]]))
        ybuf = Buf("yf%d" % l)

        def after_part(rbs, l=l, ybuf=ybuf):
            P._wait(P.pool, {S_.sem: S_.n for nm, S_ in P.dmaq.items() if nm.startswith("yo") and S_.n > 0})
            for q in range(4):
                for rb in rbs:
                    qb = q * 6 + rb
                    P.cc([], [ybuf], lambda qb=qb: nc.gpsimd.collective_compute(
                        "AllGather", ALU.bypass, replica_groups=GROUPS, ins=[yo[l][qb * 128:(qb + 1) * 128, :]],
                        outs=[yfull[l][qb * 512:(qb + 1) * 512, :]]))
        emit_A(P, ps, pb, S, dict(A_in[l], hfull=hfull[l], hfbuf=hfbuf, yo=yo[l], after_part=after_part))
        yslb = Buf("ysl%d" % l)
        for r in range(4):
            P.dma(P.sp, "ysl", [ybuf], [yslb],
                  lambda r=r: nc.sync.dma_start(out=ysl[l][r * 768:(r + 1) * 768, :],
                                                in_=yfull[l][ds(qreg * 3072 + r * 768, 768), :]))
        x2b = mk()
        outb = mk()
        dd = dict(B_in[l], xsrc=(xT if l == 0 else x2T[0]),
                  xsrc_buf=((lambda tt, k: None) if l == 0 else (lambda tt, k, xp=x2prev: xp[tt][k])),
                  yfull=ysl[l], ybuf=yslb, x1T=x1T[l], x2T=x2T[l], dst=(hloc[1] if l == 0 else oT),
                  x2bufs=x2b, outbufs=outb)
        emit_B(P, ps, pb, NTOK, B_TT, dd, last=(l == 1))
        x2prev = x2b
    P.finish([b for row in outb for b in row])
    return P


_CACHE = {}
B_TT_FULL = 1024


def prep_inputs(x, g_mix, w_in, b_forget, conv_w, conv_b, w_rg, b_rg, w_ig, b_ig, lru_lambda, w_pool, pool_scale,
                w_branch_rnn, w_branch_attn, w_branch_pool, w_out, g_ffn, w_ffn_in, w_ffn_out, g_final):
    nb, S, _ = x.shape
    NTOK = S // 4
    BW = []
    for l in range(2):
        g_next = g_mix[1] if l == 0 else g_final
        W = prep_B_weights(l, w_in, w_branch_rnn, w_branch_attn, w_branch_pool, w_out, w_ffn_in, w_ffn_out,
                           g_mix, g_ffn, g_next)
        BW.append({"%s_%d" % (k, l): v for k, v in W.items()})
    in_maps = []
    for c in range(NCORES):
        b, j = c // 4, c % 4
        m = {"xT": np.ascontiguousarray(x[b, j * NTOK:(j + 1) * NTOK, :].T),
             "gm0": np.ascontiguousarray(g_mix[0].reshape(16, 128).T)}
        for l in range(2):
            a = prep_A_inputs(l, b, j, None, g_mix, w_in, b_forget, conv_w, conv_b, w_rg, b_rg, w_ig, b_ig,
                              lru_lambda, w_pool, pool_scale)
            m["ptab"] = a["ptab"]
            m["w1_%d" % l] = a["w1"]
            m["w2_%d" % l] = np.ascontiguousarray(a["w2"].reshape(256, 16, 512))
            m["wrg_%d" % l] = a["wrg"]
            m["wpl_%d" % l] = a["wpl"]
            m["pa_%d" % l] = a["pa"]
            m.update(BW[l])
        in_maps.append(m)
    return in_maps


def kernel(x, g_mix, w_in, b_forget, conv_w, conv_b, w_rg, b_rg, w_ig, b_ig, lru_lambda, w_pool, pool_scale,
           w_branch_rnn, w_branch_attn, w_branch_pool, w_out, g_ffn, w_ffn_in, w_ffn_out, g_final):
    args = [np.asarray(a, dtype=np.float32) for a in (
        x, g_mix, w_in, b_forget, conv_w, conv_b, w_rg, b_rg, w_ig, b_ig, lru_lambda, w_pool, pool_scale,
        w_branch_rnn, w_branch_attn, w_branch_pool, w_out, g_ffn, w_ffn_in, w_ffn_out, g_final)]
    x = args[0]
    nb, S, _ = x.shape
    NTOK = S // 4
    tt = B_TT_FULL if NTOK % B_TT_FULL == 0 else 512
    key = (S, tt)
    if key not in _CACHE:
        _CACHE[key] = build_fused(S, tt)
    P = _CACHE[key]
    in_maps = prep_inputs(*args)
    res = run_bass_kernel_spmd(P.nc, in_maps, core_ids=list(range(NCORES)))
    out = np.empty((nb, S, D), np.float32)
    for c in range(NCORES):
        b, j = c // 4, c % 4
        out[b, j * NTOK:(j + 1) * NTOK, :] = res.results[c]["oT"].T
    return out
```

```python
import contextlib
import numpy as np
import ml_dtypes
import concourse.bass as bass
import concourse.mybir as mybir
from concourse.bass import ds
from concourse.bass_utils import run_bass_kernel_spmd

F32 = mybir.dt.float32
BF16 = mybir.dt.bfloat16
AF = mybir.ActivationFunctionType
ALU = mybir.AluOpType

D = 2048
KD = D // 128
DFF = 5632
KFF = DFF // 128
EPS = 1e-6
NCORES = 8
SEQ = 16384
SAME_ENGINE_SYNC = True
SEM_LIMIT = 30000


class Eng:
    def __init__(self, P, name, h, step):
        self.P, self.name, self.h, self.step = P, name, h, step
        self.n = 0
        self.sem = P.new_sem(name)
        self.nsem = 0
        self.waited = {}

    def tick(self):
        if self.n >= SEM_LIMIT * self.step:
            self.nsem += 1
            self.sem = self.P.new_sem("%s_%d" % (self.name, self.nsem))
            self.n = 0
        self.n += self.step
        return (self.sem, self.n)


class Buf:
    __slots__ = ("name", "writer", "readers")

    def __init__(self, name):
        self.name = name
        self.writer = None
        self.readers = {}


class Prog:
    def __init__(self):
        self.nc = bass.Bass("TRN2", target_bir_lowering=False)
        self.stack = contextlib.ExitStack()
        self.nsem_total = 0
        nc = self.nc
        self.pe = Eng(self, "pe", nc.tensor, 1)
        self.act = Eng(self, "act", nc.scalar, 1)
        self.dve = Eng(self, "dve", nc.vector, 1)
        self.pool = Eng(self, "pool", nc.gpsimd, 1)
        self.sp = Eng(self, "sp", nc.sync, 1)
        self.dmaq = {}
        self.scopes = []
        self.nname = 0
        self.xr = 0

    def new_sem(self, name):
        self.nsem_total += 1
        return self.stack.enter_context(self.nc.semaphore("s_%s_%d" % (name, self.nsem_total)))

    def sbuf(self, name, shape, dt):
        st = self.scopes[-1] if self.scopes else self.stack
        self.nname += 1
        return st.enter_context(self.nc.sbuf_tensor("%s_%d" % (name, self.nname), shape, dt))

    def push_scope(self):
        self.scopes.append(contextlib.ExitStack())

    def pop_scope(self):
        self.barrier()
        self.scopes.pop().close()

    def barrier(self):
        engs = [self.pe, self.act, self.dve, self.pool, self.sp]
        targets = {}
        for E in engs + list(self.dmaq.values()):
            if E.n > 0:
                targets[E.sem] = E.n
        for E in engs:
            self._wait(E, {s: n for s, n in targets.items() if s is not E.sem})

    def cc(self, reads, writes, emit):
        if "cc" not in self.dmaq:
            self.dmaq["cc"] = Eng(self, "cc", None, 1)
        S = self.dmaq["cc"]
        self._wait(self.pool, self._deps(reads, writes))
        ins = emit()
        tok = S.tick()
        ins.then_inc(tok[0])
        for b in reads:
            b.readers[id(S)] = tok
        for b in writes:
            b.writer = tok
            b.readers = {}
        return tok

    def psum(self, name, shape, dt=F32):
        return self.stack.enter_context(self.nc.psum_tensor(name, shape, dt))

    def dram(self, name, shape, dt, kind="Internal"):
        return self.nc.dram_tensor(name, shape, dt, kind=kind).ap()

    def _deps(self, reads, writes):
        deps = {}
        for b in reads:
            if b.writer is not None:
                s, n = b.writer
                deps[s] = max(deps.get(s, 0), n)
        for b in writes:
            if b.writer is not None:
                s, n = b.writer
                deps[s] = max(deps.get(s, 0), n)
            for (s, n) in b.readers.values():
                deps[s] = max(deps.get(s, 0), n)
        return deps

    def _wait(self, E, deps, skip_sem=None):
        for s, n in deps.items():
            if s is skip_sem:
                continue
            if E.waited.get(s, 0) < n:
                E.h.wait_ge(s, n)
                E.waited[s] = n

    def op(self, E, reads, writes, emit):
        deps = self._deps(reads, writes)
        skip = None
        if E is self.pe or not SAME_ENGINE_SYNC:
            skip = E.sem
        self._wait(E, deps, skip)
        ins = emit()
        tok = E.tick()
        ins.then_inc(tok[0], 1)
        for b in reads:
            b.readers[id(E)] = tok
        for b in writes:
            b.writer = tok
            b.readers = {}
        return tok

    def dma(self, Q, slot, reads, writes, emit):
        if slot not in self.dmaq:
            self.dmaq[slot] = Eng(self, "d" + slot, None, 16)
        S = self.dmaq[slot]
        deps = self._deps(reads, writes)
        self._wait(Q, deps)
        ins = emit()
        tok = S.tick()
        ins.then_inc(tok[0], 16)
        for b in reads:
            b.readers[id(S) + len(b.readers) * 0] = tok
        for b in writes:
            b.writer = tok
            b.readers = {}
        return tok

    def finish(self, bufs):
        deps = {}
        for b in bufs:
            if b.writer is not None:
                s, n = b.writer
                deps[s] = max(deps.get(s, 0), n)
        self._wait(self.sp, deps)
        self.stack.close()


class WRing:
    def __init__(self, P, nslots):
        self.P = P
        self.n = nslots
        self.t = P.sbuf("wring", [128, nslots, 16, 128], BF16)
        self.bufs = [Buf("w%d" % i) for i in range(nslots)]
        P.wring_gen = getattr(P, "wring_gen", 0) + 1
        self.gen = P.wring_gen
        self.i = 0

    def load(self, src, kc):
        P = self.P
        i = self.i
        self.i = (self.i + 1) % self.n
        dst = self.t[:, i, 0:kc, :]
        b = self.bufs[i]
        P.dma(P.pool, "w%d_%d" % (self.gen, i), [], [b],
              lambda: P.nc.gpsimd.dma_start(out=dst, in_=src, max_dma_last_dim=4096))
        return self.t[:, i], b


def emit_norm(P, C, src_chunk, src_buf, gcol, out_fn, TT):
    nc = P.nc
    nsub = TT // 512
    xring, xbufs, sq, sqbufs, ps, pb, rstd, rstdbuf, ones_bf, onesb = (
        C["xring"], C["xbufs"], C["sq"], C["sqbufs"], C["ps"], C["pb"], C["rstd"], C["rstdbuf"], C["ones_bf"],
        C["onesb"])
    nx = len(xbufs)
    for k in range(KD):
        xi = P.xr
        P.xr = (P.xr + 1) % nx
        sb = src_buf(k)
        P.dma(P.sp, "x%d" % xi, [sb] if sb else [], [xbufs[xi]],
              lambda xi=xi, k=k: nc.sync.dma_start(out=xring[:, xi, :], in_=src_chunk(k)))
        si = k % 2
        P.op(P.act, [xbufs[xi]], [sqbufs[si]],
             lambda xi=xi, si=si: nc.scalar.activation(out=sq[:, si, :], in_=xring[:, xi, :], func=AF.Square))
        for s in range(nsub):
            P.op(P.pe, [sqbufs[si], onesb], [pb[6 + s]],
                 lambda s=s, si=si, k=k: nc.tensor.matmul(ps[:, 6 + s, :], lhsT=ones_bf[:, :],
                                                          rhs=sq[:, si, s * 512:(s + 1) * 512],
                                                          start=(k == 0), stop=(k == KD - 1)))
    for s in range(nsub):
        P.op(P.act, [pb[6 + s]], [rstdbuf],
             lambda s=s: nc.scalar.activation(out=rstd[:, s * 512:(s + 1) * 512], in_=ps[:, 6 + s, :],
                                              func=AF.Sqrt, scale=1.0 / D, bias=C["eps"][:, 0:1]))
    P.op(P.dve, [rstdbuf], [rstdbuf], lambda: nc.vector.reciprocal(out=rstd[:, :], in_=rstd[:, :]))
    for k in range(KD):
        xi = P.xr
        P.xr = (P.xr + 1) % nx
        sb = src_buf(k)
        P.dma(P.sp, "x%d" % xi, [sb] if sb else [], [xbufs[xi]],
              lambda xi=xi, k=k: nc.sync.dma_start(out=xring[:, xi, :], in_=src_chunk(k)))
        out_fn(k, xring[:, xi, :], gcol[:, k:k + 1], rstd[:, :], [xbufs[xi], rstdbuf])


def common_setup(P, TT, ps, pb, NX=4):
    nc = P.nc
    C = {}
    C["ones_bf"] = P.sbuf("ones_bf", [128, 128], BF16)
    C["onesb"] = Buf("ones")
    C["eps"] = P.sbuf("eps", [128, 1], F32)
    C["xring"] = P.sbuf("xring", [128, NX, TT], F32)
    C["xbufs"] = [Buf("x%d" % i) for i in range(NX)]
    C["sq"] = P.sbuf("sq", [128, 2, TT], BF16)
    C["sqbufs"] = [Buf("sq0"), Buf("sq1")]
    C["rstd"] = P.sbuf("rstd", [128, TT], F32)
    C["rstdbuf"] = Buf("rstd")
    C["ps"] = ps
    C["pb"] = pb
    P.xr = 0
    P.op(P.dve, [], [C["onesb"]], lambda: nc.vector.memset(C["ones_bf"][:], 1.0))
    P.op(P.dve, [], [C["rstdbuf"]], lambda: nc.vector.memset(C["eps"][:], EPS))
    return C


def emit_B(P, ps_, pb_, NTOK, TT, d, last):
    nc = P.nc
    nsub = TT // 512
    xT, xsrc_buf = d["xsrc"], d["xsrc_buf"]
    yfull, ybuf = d["yfull"], d["ybuf"]
    wg, wbr, wo, wf1, wf2, gv = d["wg"], d["wbr"], d["wo"], d["wf1"], d["wf2"], d["gv"]
    x1T, x2T, dst = d["x1T"], d["x2T"], d["dst"]
    ntt = NTOK // TT
    x1bufs = [[Buf("x1_%d_%d" % (t, n)) for n in range(16)] for t in range(ntt)]
    x2bufs = d["x2bufs"]
    outbufs = d["outbufs"]
    final_norm = True

    P.push_scope()
    C = common_setup(P, TT, ps_, pb_)
    ps, pb, xring, xbufs = C["ps"], C["pb"], C["xring"], C["xbufs"]
    NX = len(xbufs)
    g_sb = P.sbuf("g_sb", [128, 3, 16], F32)
    gbuf = Buf("g")
    hT = P.sbuf("hT", [128, KD, TT], BF16)
    hbufs = [Buf("h%d" % k) for k in range(KD)]
    R = P.sbuf("R", [128, KFF, TT], BF16)
    rb = [Buf("r%d" % k) for k in range(KFF)]
    tmp = P.sbuf("tmp", [128, 3, TT], F32)
    tb = [Buf("t%d" % i) for i in range(3)]
    ost = P.sbuf("ost", [128, 2, TT], F32)
    ob = [Buf("o%d" % i) for i in range(2)]
    hst = P.sbuf("hst", [128, 2, TT], BF16)
    hsb_ = [Buf("hst%d" % i) for i in range(2)]
    wr = WRing(P, 8)

    P.dma(P.sp, "g", [], [gbuf], lambda: nc.sync.dma_start(out=g_sb[:], in_=gv))

    state = {"pset": 0, "oi": 0}

    def v3(ap2d):
        return ap2d.rearrange("p (s t) -> p s t", s=nsub)

    def pview(pset):
        return ps[:, pset * 2:pset * 2 + nsub, :]

    def pbufs(pset):
        return pb[pset * 2:pset * 2 + nsub]

    def gemm(wsrcs, kc_list, rhs_fn, rhs_bufs):
        pset = state["pset"]
        state["pset"] ^= 1
        slots = [wr.load(src, kc) for src, kc in zip(wsrcs, kc_list)]
        ktot = sum(kc_list)
        for s in range(nsub):
            def emit(s=s):
                kk = 0
                ins = None
                for (wt, wbuf), kc in zip(slots, kc_list):
                    for k in range(kc):
                        ins = nc.tensor.matmul(ps[:, pset * 2 + s, :], lhsT=wt[:, k, :], rhs=rhs_fn(kk, s),
                                               start=(kk == 0), stop=(kk == ktot - 1))
                        kk += 1
                return ins
            P.op(P.pe, [wb_ for (_, wb_) in slots] + rhs_bufs, [pb[pset * 2 + s]], emit)
        return pset

    def h_rhs(kk, s):
        return hT[:, kk, s * 512:(s + 1) * 512]

    def R_rhs(kk, s):
        return R[:, kk, s * 512:(s + 1) * 512]

    def norm_to_h(k, x_ap, g_ap, rstd_ap, reads):
        P.op(P.dve, reads + [gbuf], [hbufs[k]],
             lambda: nc.vector.scalar_tensor_tensor(out=hT[:, k, :], in0=x_ap, scalar=g_ap, in1=rstd_ap,
                                                    op0=ALU.mult, op1=ALU.mult))

    for tt in range(ntt):
        t0 = tt * TT
        emit_norm(P, C, lambda k: xT[k * 128:(k + 1) * 128, t0:t0 + TT], lambda k: xsrc_buf(tt, k), g_sb[:, 0, :],
                  norm_to_h, TT)
        for i in range(3):
            for k in range(8):
                r = 16 + i * 8 + k
                row0 = ((i * 2 + k % 2) * 4 + k // 2) * 128
                P.dma(P.sp, "y%d" % (i * 8 + k), [ybuf], [rb[r]],
                      lambda r=r, row0=row0: nc.sync.dma_start(
                          out=R[:, r, :], in_=yfull[row0:row0 + 128, t0:t0 + TT]))
        for m in range(16):
            for i in range(3):
                pset = gemm([wg[i * 16 + m]], [16], h_rhs, hbufs)
                P.op(P.act, pbufs(pset), [tb[0]],
                     lambda pset=pset: nc.scalar.activation(out=v3(tmp[:, 0, :]), in_=pview(pset), func=AF.Sigmoid))
                pset = gemm([wbr[i * 16 + m]], [8],
                            lambda kk, s, i=i: R[:, 16 + i * 8 + kk, s * 512:(s + 1) * 512],
                            rb[16 + i * 8:16 + i * 8 + 8])
                dsti = 1 if i == 0 else 2
                P.op(P.dve, pbufs(pset) + [tb[0]], [tb[dsti]],
                     lambda pset=pset, dsti=dsti: nc.vector.tensor_tensor(
                         out=v3(tmp[:, dsti, :]), in0=pview(pset), in1=v3(tmp[:, 0, :]), op=ALU.mult))
                if i == 1:
                    P.op(P.dve, [tb[1], tb[2]], [tb[1]],
                         lambda: nc.vector.tensor_tensor(out=tmp[:, 1, :], in0=tmp[:, 1, :], in1=tmp[:, 2, :],
                                                         op=ALU.add))
                elif i == 2:
                    P.op(P.dve, [tb[1], tb[2]], [rb[m]],
                         lambda m=m: nc.vector.tensor_tensor(out=R[:, m, :], in0=tmp[:, 1, :], in1=tmp[:, 2, :],
                                                             op=ALU.add))
        for n in range(16):
            pset = gemm([wo[n]], [16], R_rhs, rb[0:16])
            xi = P.xr
            P.xr = (P.xr + 1) % NX
            xsb = xsrc_buf(tt, n)
            P.dma(P.sp, "x%d" % xi, [xsb] if xsb else [], [xbufs[xi]],
                  lambda xi=xi, n=n: nc.sync.dma_start(out=xring[:, xi, :],
                                                       in_=xT[n * 128:(n + 1) * 128, t0:t0 + TT]))
            oi = state["oi"]
            state["oi"] ^= 1
            P.op(P.dve, pbufs(pset) + [xbufs[xi]], [ob[oi]],
                 lambda pset=pset, xi=xi, oi=oi: nc.vector.tensor_tensor(
                     out=v3(ost[:, oi, :]), in0=pview(pset), in1=v3(xring[:, xi, :]), op=ALU.add))
            P.dma(P.sp, "x1w%d" % oi, [ob[oi]], [x1bufs[tt][n]],
                  lambda oi=oi, n=n: nc.sync.dma_start(out=x1T[n * 128:(n + 1) * 128, t0:t0 + TT],
                                                       in_=ost[:, oi, :]))
        emit_norm(P, C, lambda k: x1T[k * 128:(k + 1) * 128, t0:t0 + TT], lambda k: x1bufs[tt][k],
                  g_sb[:, 1, :], norm_to_h, TT)
        for j in range(KFF):
            pg = gemm([wf1[2 * j]], [16], h_rhs, hbufs)
            ti = j % 2
            P.op(P.act, pbufs(pg), [tb[ti]],
                 lambda pg=pg, ti=ti: nc.scalar.activation(out=v3(tmp[:, ti, :]), in_=pview(pg), func=AF.Silu))
            pu = gemm([wf1[2 * j + 1]], [16], h_rhs, hbufs)
            P.op(P.dve, pbufs(pu) + [tb[ti]], [rb[j]],
                 lambda pu=pu, ti=ti, j=j: nc.vector.tensor_tensor(
                     out=v3(R[:, j, :]), in0=pview(pu), in1=v3(tmp[:, ti, :]), op=ALU.mult))
        for n in range(16):
            pset = gemm([wf2[n, :, 0:16, :], wf2[n, :, 16:32, :], wf2[n, :, 32:44, :]], [16, 16, 12], R_rhs, rb)
            xi = P.xr
            P.xr = (P.xr + 1) % NX
            P.dma(P.sp, "x%d" % xi, [x1bufs[tt][n]], [xbufs[xi]],
                  lambda xi=xi, n=n: nc.sync.dma_start(out=xring[:, xi, :],
                                                       in_=x1T[n * 128:(n + 1) * 128, t0:t0 + TT]))
            oi = state["oi"]
            state["oi"] ^= 1
            P.op(P.dve, pbufs(pset) + [xbufs[xi]], [ob[oi]],
                 lambda pset=pset, xi=xi, oi=oi: nc.vector.tensor_tensor(
                     out=v3(ost[:, oi, :]), in0=pview(pset), in1=v3(xring[:, xi, :]), op=ALU.add))
            if final_norm:
                P.dma(P.sp, "x2w%d" % oi, [ob[oi]], [x2bufs[tt][n]],
                      lambda oi=oi, n=n: nc.sync.dma_start(out=x2T[n * 128:(n + 1) * 128, t0:t0 + TT],
                                                           in_=ost[:, oi, :]))
            else:
                P.dma(P.sp, "ow", [ob[oi]], [outbufs[tt][n]],
                      lambda oi=oi, n=n: nc.sync.dma_start(out=oT[n * 128:(n + 1) * 128, t0:t0 + TT],
                                                           in_=ost[:, oi, :]))
        if final_norm:
            def norm_to_out(k, x_ap, g_ap, rstd_ap, reads):
                oi = state["oi"]
                state["oi"] ^= 1
                P.op(P.dve, reads + [gbuf], [ob[oi]],
                     lambda: nc.vector.scalar_tensor_tensor(out=ost[:, oi, :], in0=x_ap, scalar=g_ap, in1=rstd_ap,
                                                            op0=ALU.mult, op1=ALU.mult))
                P.dma(P.sp, "ow", [ob[oi]], [outbufs[tt][k]],
                      lambda: nc.sync.dma_start(out=dst[k * 128:(k + 1) * 128, t0:t0 + TT], in_=ost[:, oi, :]))

            def norm_to_hloc(k, x_ap, g_ap, rstd_ap, reads):
                oi = state["oi"]
                state["oi"] ^= 1
                P.op(P.dve, reads + [gbuf], [hsb_[oi]],
                     lambda: nc.vector.scalar_tensor_tensor(out=hst[:, oi, :], in0=x_ap, scalar=g_ap, in1=rstd_ap,
                                                            op0=ALU.mult, op1=ALU.mult))
                P.dma(P.sp, "ow", [hsb_[oi]], [outbufs[tt][k]],
                      lambda: nc.sync.dma_start(out=hloc_view(dst, t0, TT, k), in_=hst[:, oi, :].rearrange(
                          "p (c t) -> p c t", c=TT // 256)))
            emit_norm(P, C, lambda k: x2T[k * 128:(k + 1) * 128, t0:t0 + TT], lambda k: x2bufs[tt][k],
                      g_sb[:, 2, :], norm_to_out if last else norm_to_hloc, TT)
    P.pop_scope()


def hloc_view(hloc, t0, TT, k):
    h3 = hloc.rearrange("(c f) t -> c f t", f=D)
    return h3[t0 // 256:(t0 + TT) // 256, k * 128:(k + 1) * 128, :].rearrange("c p t -> p c t")


def emit_N(P, ps_, pb_, NTOK, TT, xT, gm, hloc, outbufs):
    nc = P.nc
    P.push_scope()
    C = common_setup(P, TT, ps_, pb_)
    g_sb = P.sbuf("gN", [128, 16], F32)
    gbuf = Buf("gN")
    hst = P.sbuf("hstN", [128, 2, TT], BF16)
    hsb_ = [Buf("hstN%d" % i) for i in range(2)]
    P.dma(P.sp, "g", [], [gbuf], lambda: nc.sync.dma_start(out=g_sb[:], in_=gm))
    st = {"oi": 0}
    for tt in range(NTOK // TT):
        t0 = tt * TT

        def out_fn(k, x_ap, g_ap, rstd_ap, reads):
            oi = st["oi"]
            st["oi"] ^= 1
            P.op(P.dve, reads + [gbuf], [hsb_[oi]],
                 lambda: nc.vector.scalar_tensor_tensor(out=hst[:, oi, :], in0=x_ap, scalar=g_ap, in1=rstd_ap,
                                                        op0=ALU.mult, op1=ALU.mult))
            P.dma(P.sp, "ow", [hsb_[oi]], [outbufs[tt][k]],
                  lambda: nc.sync.dma_start(out=hloc_view(hloc, t0, TT, k), in_=hst[:, oi, :].rearrange(
                      "p (c t) -> p c t", c=TT // 256)))
        emit_norm(P, C, lambda k: xT[k * 128:(k + 1) * 128, t0:t0 + TT], lambda k: None, g_sb, out_fn, TT)
    P.pop_scope()


def arrange_w(w, kc):
    K, N = w.shape
    return np.ascontiguousarray(w.reshape(K // 128, 128, N // 128, 128).transpose(2, 1, 0, 3))


def prep_B_weights(l, w_in, w_branch_rnn, w_branch_attn, w_branch_pool, w_out, w_ffn_in, w_ffn_out,
                   g_mix, g_ffn, g_final):
    off = D_IN_GATES
    wg = arrange_w(w_in[l][:, off:off + 3 * D], 16)
    wbr = np.concatenate([arrange_w(w_branch_rnn[l], 8), arrange_w(w_branch_attn[l], 8),
                          arrange_w(w_branch_pool[l], 8)], axis=0)
    wo = arrange_w(w_out[l], 16)
    f1 = w_ffn_in[l]
    f1i = np.stack([f1[:, :DFF].reshape(D, KFF, 128), f1[:, DFF:].reshape(D, KFF, 128)], axis=2).reshape(D, 2 * DFF)
    wf1 = arrange_w(f1i, 16)
    wf2 = arrange_w(w_ffn_out[l], 44)
    gv = np.ascontiguousarray(np.stack([g_mix[l].reshape(16, 128).T, g_ffn[l].reshape(16, 128).T,
                                        g_final.reshape(16, 128).T], axis=1)).astype(np.float32)
    return dict(wg=wg, wbr=wbr, wo=wo, wf1=wf1, wf2=wf2, gv=gv)


D_IN_GATES = 1024 * 2 + 1024 * 3 + 8 + 1024


NPAR = 24
SQRT_DH = 11.313708498984761
GELU_C = 1.5957691216057308


def emit_A(P, ps, pb, S, d):
    nc = P.nc
    TT = 512
    NT = S // TT
    NKB = S // 128
    NTOK = S // 4
    hfull, hfbuf = d["hfull"], d["hfbuf"]
    w1d, w2d, wrgd, wpld, pad, ptabd, yo = d["w1"], d["w2"], d["wrg"], d["wpl"], d["pa"], d["ptab"], d["yo"]
    yob = [Buf("yo%d" % i) for i in range(8)]
    P.push_scope()
    ones_bf = P.sbuf("ones_bfA", [128, 128], BF16)
    onesb = Buf("onesA")
    P.op(P.dve, [], [onesb], lambda: nc.vector.memset(ones_bf[:], 1.0))

    def sb(name, shape, dt=F32):
        return P.sbuf(name, shape, dt), Buf(name)

    pa, pabuf = sb("pa_sb", [128, NPAR])
    ptab, ptabbuf = sb("ptab_sb", [128, 4, 16])
    cst, cstbuf = sb("cst", [128, 8])
    w1, w1buf = sb("w1_sb", [128, 16, 768], BF16)
    w2, w2buf = sb("w2_sb", [128, 16, 512], BF16)
    wrg, wrgbuf = sb("wrg_sb", [128, 4, 128], BF16)
    wpl, wplbuf = sb("wpl_sb", [128, 2, 256], BF16)
    hT = P.sbuf("hT", [128, 2, KD, TT], BF16)
    hbufs = [[Buf("h%d_%d" % (j, k)) for k in range(KD)] for j in range(2)]
    NTMP = 14
    tmp = P.sbuf("tmp", [128, NTMP, TT], F32)
    tb = [Buf("t%d" % i) for i in range(NTMP)]
    tmpb = P.sbuf("tmpb", [128, 6, TT], BF16)
    tbb = [Buf("tb%d" % i) for i in range(6)]
    xb = P.sbuf("xb", [128, 2, TT + 3], F32)
    xbb = [Buf("xb0"), Buf("xb1")]
    hcar = P.sbuf("hcar", [128, 2], F32)
    hcb = [Buf("hc0"), Buf("hc1")]
    ub = P.sbuf("ub", [128, 2, TT + 15], F32)
    ubb = [Buf("ub0"), Buf("ub1")]
    sw = P.sbuf("sw", [128, 4, TT + 15], F32)
    swb = [Buf("sw%d" % i) for i in range(4)]
    t16 = P.sbuf("t16", [128, 2, 16], F32)
    t16b = [Buf("t16a"), Buf("t16b")]
    KT, ktbuf_all = sb("KT", [128, S], BF16)
    ktb = [Buf("kt%d" % i) for i in range(NT)]
    V = P.sbuf("V", [128, NKB, 128], BF16)
    vb = [Buf("v%d" % i) for i in range(NT)]
    negF = P.sbuf("negF", [128, NKB], F32)
    nfb = [Buf("nf%d" % i) for i in range(NT)]
    bm, bmbuf = sb("bm", [128, NKB])
    ct, ctbuf = sb("ct", [128, 1])
    fc = P.sbuf("fc", [128, 2], F32)
    fcb = [Buf("fc0"), Buf("fc1")]
    one32, one32b = sb("one32", [128, 128], F32)
    onesT, onesTb = sb("onesT", [128, TT], F32)
    Et = P.sbuf("Et", [128, 5, TT], BF16)
    etb = [Buf("et%d" % i) for i in range(5)]
    negcol, ncbuf = sb("negcol", [128, 4])
    tri, trib = sb("tri", [128, 128], BF16)
    pT = P.sbuf("pT", [128, 3, TT], BF16)
    ptb = [Buf("pt%d" % i) for i in range(3)]
    state = {"bank": 0, "pt": 0, "yo": 0, "tmp": 0}

    P.dma(P.sp, "c0a", [], [pabuf], lambda: nc.sync.dma_start(out=pa[:], in_=pad))
    P.dma(P.sp, "c0b", [], [ptabbuf], lambda: nc.sync.dma_start(out=ptab[:], in_=ptabd))
    for k in range(KD):
        P.dma(P.pool, "cw1", [], [w1buf],
              lambda k=k: nc.gpsimd.dma_start(out=w1[:, k, :], in_=w1d[:, k, :], max_dma_last_dim=4096))
    P.dma(P.pool, "cwr", [], [wrgbuf], lambda: nc.gpsimd.dma_start(out=wrg[:], in_=wrgd, max_dma_last_dim=4096))
    P.dma(P.pool, "cwp", [], [wplbuf], lambda: nc.gpsimd.dma_start(out=wpl[:], in_=wpld, max_dma_last_dim=4096))
    P.op(P.dve, [], [one32b], lambda: nc.vector.memset(one32[:], 1.0))
    P.op(P.dve, [], [one32b], lambda: nc.vector.memset(onesT[:], 1.0))
    P.op(P.dve, [pabuf], [pabuf],
         lambda: nc.vector.tensor_scalar(out=pa[:, 18:20], in0=pa[:, 18:20], scalar1=-1.0, scalar2=None,
                                         op0=ALU.mult))
    P.op(P.dve, [], [xbb[0], xbb[1]], lambda: nc.vector.memset(xb[:], 0.0))
    P.op(P.dve, [], [ubb[0], ubb[1]], lambda: nc.vector.memset(ub[:], 0.0))
    P.op(P.dve, [], [hcb[0], hcb[1]], lambda: nc.vector.memset(hcar[:], 0.0))
    P.op(P.dve, [], swb, lambda: nc.vector.memset(sw[:], 0.0))
    P.op(P.dve, [], [ncbuf], lambda: nc.vector.memset(negcol[:], 0.0))
    P.op(P.dve, [], [trib], lambda: nc.vector.memset(tri[:], 1.0))
    P.op(P.pool, [trib], [trib],
         lambda: nc.gpsimd.affine_select(out=tri[:], in_=tri[:], pattern=[[1, 128]], compare_op=ALU.is_ge,
                                         fill=0.0, base=0, channel_multiplier=-1))
    P.op(P.act, [pabuf], [cstbuf],
         lambda: nc.scalar.activation(out=cst[:, 0:2], in_=pa[:, 14:16], func=AF.Exp, scale=-1.0))
    P.op(P.act, [cstbuf], [cstbuf],
         lambda: nc.scalar.activation(out=cst[:, 0:2], in_=cst[:, 0:2], func=AF.Ln, bias=one32[:, 0:1]))
    P.op(P.dve, [cstbuf], [cstbuf],
         lambda: nc.vector.tensor_scalar(out=cst[:, 2:4], in0=cst[:, 0:2], scalar1=-16.0, scalar2=None,
                                         op0=ALU.mult))
    P.op(P.dve, [cstbuf], [cstbuf],
         lambda: nc.vector.tensor_scalar(out=cst[:, 0:2], in0=cst[:, 0:2], scalar1=-8.0, scalar2=None,
                                         op0=ALU.mult))

    def load_w2(hd):
        for k in range(KD):
            P.dma(P.pool, "cw2", [], [w2buf],
                  lambda k=k: nc.gpsimd.dma_start(out=w2[:, k, :], in_=w2d[hd * 128:(hd + 1) * 128, k, :],
                                                  max_dma_last_dim=4096))
    load_w2(0)
    d["after_setup"]()

    def bank():
        b = state["bank"]
        state["bank"] = (b + 1) % 8
        return b

    def T():
        i = state["tmp"]
        state["tmp"] = (i + 1) % NTMP
        return i

    def gemm_chunk(wt, wbuf, c0, hj, bk):
        def emit():
            ins = None
            for k in range(KD):
                ins = nc.tensor.matmul(ps[:, bk, :], lhsT=wt[:, k, c0:c0 + 128], rhs=hT[:, hj, k, :],
                                       start=(k == 0), stop=(k == KD - 1))
            return ins
        P.op(P.pe, [wbuf] + hbufs[hj], [pb[bk]], emit)

    def store_y(which, row0, src_ap, src_buf, t0):
        i = state["yo"]
        state["yo"] = (i + 1) % 8
        P.dma(P.sp, "yo%d" % i, [src_buf], [yob[i]],
              lambda: nc.sync.dma_start(
                  out=yo[(t0 // NTOK) * 768 + which * 256 + row0:(t0 // NTOK) * 768 + which * 256 + row0 + 128,
                         (t0 % NTOK):(t0 % NTOK) + TT], in_=src_ap))

    hf3 = hfull.rearrange("(c q) t -> c q t", q=4 * D)

    def load_h(it, hj):
        r, off = divmod(it * TT, NTOK)
        c = off // 256
        for k in range(KD):
            P.dma(P.sp, "hl%d" % hj, [hfbuf[c], hfbuf[c + 1]], [hbufs[hj][k]],
                  lambda k=k: nc.sync.dma_start(
                      out=hT[:, hj, k, :].rearrange("p (c t) -> p c t", c=2),
                      in_=hf3[c:c + 2, r * D + k * 128:r * D + (k + 1) * 128, :].rearrange("c p t -> p c t")))

    for it in range(NT):
        t0 = it * TT
        hj = it % 2
        if it == 0:
            load_h(0, 0)
        if it + 1 < NT:
            load_h(it + 1, (it + 1) % 2)

        def lru_gen(blk):
            bk = 3 * blk
            gemm_chunk(w1, w1buf, blk * 128, hj, bk)
            yield
            P.op(P.act, [pb[bk]], [xbb[blk]],
                 lambda bk=bk: nc.scalar.activation(out=xb[:, blk, 3:TT + 3], in_=ps[:, bk, :], func=AF.Copy))
            yield
            u, r, ig, a, a2, z = [blk * 6 + i_ for i_ in range(6)]
            P.op(P.act, [xbb[blk], pabuf], [tb[u]],
                 lambda: nc.scalar.activation(out=tmp[:, u, :], in_=xb[:, blk, 3:TT + 3], func=AF.Identity,
                                              scale=pa[:, blk * 4 + 3:blk * 4 + 4], bias=pa[:, 8 + blk:9 + blk]))
            yield
            for tap in (2, 1, 0):
                P.op(P.dve, [xbb[blk], pabuf, tb[u]], [tb[u]],
                     lambda tap=tap: nc.vector.scalar_tensor_tensor(
                         out=tmp[:, u, :], in0=xb[:, blk, tap:tap + TT], scalar=pa[:, blk * 4 + tap:blk * 4 + tap + 1],
                         in1=tmp[:, u, :], op0=ALU.mult, op1=ALU.add))
                yield
            P.op(P.dve, [xbb[blk]], [xbb[blk]],
                 lambda: nc.vector.tensor_copy(out=xb[:, blk, 0:3], in_=xb[:, blk, TT:TT + 3]))
            yield
            ubf = blk
            P.op(P.dve, [tb[u]], [tbb[ubf]], lambda: nc.vector.tensor_copy(out=tmpb[:, ubf, :], in_=tmp[:, u, :]))
            yield
            bkr, bki = 3 * blk + 1, 3 * blk + 2
            P.op(P.pe, [tbb[ubf], wrgbuf], [pb[bkr]],
                 lambda: nc.tensor.matmul(ps[:, bkr, :], lhsT=wrg[:, blk * 2, :], rhs=tmpb[:, ubf, :],
                                          start=True, stop=True))
            yield
            P.op(P.pe, [tbb[ubf], wrgbuf], [pb[bki]],
                 lambda: nc.tensor.matmul(ps[:, bki, :], lhsT=wrg[:, blk * 2 + 1, :], rhs=tmpb[:, ubf, :],
                                          start=True, stop=True))
            yield
            P.op(P.act, [pb[bkr], pabuf], [tb[r]],
                 lambda: nc.scalar.activation(out=tmp[:, r, :], in_=ps[:, bkr, :], func=AF.Sigmoid,
                                              bias=pa[:, 10 + blk:11 + blk]))
            yield
            P.op(P.act, [pb[bki], pabuf], [tb[ig]],
                 lambda: nc.scalar.activation(out=tmp[:, ig, :], in_=ps[:, bki, :], func=AF.Sigmoid,
                                              bias=pa[:, 12 + blk:13 + blk]))
            yield
            P.op(P.act, [tb[r], cstbuf], [tb[a]],
                 lambda: nc.scalar.activation(out=tmp[:, a, :], in_=tmp[:, r, :], func=AF.Exp,
                                              scale=cst[:, blk:blk + 1]))
            yield
            P.op(P.act, [tb[r], cstbuf], [tb[a2]],
                 lambda: nc.scalar.activation(out=tmp[:, a2, :], in_=tmp[:, r, :], func=AF.Exp,
                                              scale=cst[:, 2 + blk:3 + blk]))
            yield
            P.op(P.dve, [tb[a2]], [tb[a2]],
                 lambda: nc.vector.tensor_scalar(out=tmp[:, a2, :], in0=tmp[:, a2, :], scalar1=-1.0, scalar2=1.0,
                                                 op0=ALU.mult, op1=ALU.add))
            yield
            P.op(P.act, [tb[a2]], [tb[a2]],
                 lambda: nc.scalar.activation(out=tmp[:, a2, :], in_=tmp[:, a2, :], func=AF.Sqrt))
            yield
            P.op(P.dve, [tb[ig], tb[u]], [tb[ig]],
                 lambda: nc.vector.tensor_tensor(out=tmp[:, ig, :], in0=tmp[:, ig, :], in1=tmp[:, u, :], op=ALU.mult))
            yield
            P.op(P.dve, [tb[ig], tb[a2]], [tb[ig]],
                 lambda: nc.vector.tensor_tensor(out=tmp[:, ig, :], in0=tmp[:, ig, :], in1=tmp[:, a2, :],
                                                 op=ALU.mult))
            yield
            P.op(P.dve, [tb[a], tb[ig], hcb[blk]], [tb[r]],
                 lambda: nc.vector.tensor_tensor_scan(out=tmp[:, r, :], data0=tmp[:, a, :], data1=tmp[:, ig, :],
                                                      initial=hcar[:, blk:blk + 1], op0=ALU.mult, op1=ALU.add))
            yield
            P.op(P.dve, [tb[r]], [hcb[blk]],
                 lambda: nc.vector.tensor_copy(out=hcar[:, blk:blk + 1], in_=tmp[:, r, TT - 1:TT]))
            yield
            bky = 3 * blk
            gemm_chunk(w1, w1buf, 256 + blk * 128, hj, bky)
            yield
            P.op(P.act, [pb[bky]], [tb[z]],
                 lambda: nc.scalar.activation(out=tmp[:, z, :], in_=ps[:, bky, :], func=AF.Square))
            yield
            P.op(P.dve, [tb[z]], [tb[z]],
                 lambda: nc.vector.tensor_scalar(out=tmp[:, z, :], in0=tmp[:, z, :], scalar1=0.044715, scalar2=1.0,
                                                 op0=ALU.mult, op1=ALU.add))
            yield
            P.op(P.dve, [tb[z], pb[bky]], [tb[z]],
                 lambda: nc.vector.tensor_tensor(out=tmp[:, z, :], in0=ps[:, bky, :], in1=tmp[:, z, :], op=ALU.mult))
            yield
            P.op(P.act, [tb[z]], [tb[z]],
                 lambda: nc.scalar.activation(out=tmp[:, z, :], in_=tmp[:, z, :], func=AF.Sigmoid, scale=GELU_C))
            yield
            P.op(P.dve, [tb[z], pb[bky]], [tb[z]],
                 lambda: nc.vector.tensor_tensor(out=tmp[:, z, :], in0=ps[:, bky, :], in1=tmp[:, z, :], op=ALU.mult))
            yield
            yb_ = 2 + blk
            P.op(P.dve, [tb[z], tb[r]], [tbb[yb_]],
                 lambda: nc.vector.tensor_tensor(out=tmpb[:, yb_, :], in0=tmp[:, z, :], in1=tmp[:, r, :], op=ALU.mult))
            yield
            store_y(0, blk * 128, tmpb[:, yb_, :], tbb[yb_], t0)
            yield

        def pool_gen(cc):
            bk = 6
            gemm_chunk(w1, w1buf, 512 + cc * 128, hj, bk)
            yield
            P.op(P.act, [pb[bk]], [ubb[cc]],
                 lambda bk=bk: nc.scalar.activation(out=ub[:, cc, 15:TT + 15], in_=ps[:, bk, :], func=AF.Copy))
            yield
            W_ = TT + 15
            P.op(P.dve, [ubb[cc]], [swb[0]],
                 lambda: nc.vector.tensor_tensor(out=sw[:, 0, 1:W_], in0=ub[:, cc, 1:W_], in1=ub[:, cc, 0:W_ - 1],
                                                 op=ALU.add))
            yield
            P.op(P.dve, [swb[0]], [swb[1]],
                 lambda: nc.vector.tensor_tensor(out=sw[:, 1, 3:W_], in0=sw[:, 0, 3:W_], in1=sw[:, 0, 1:W_ - 2],
                                                 op=ALU.add))
            yield
            P.op(P.dve, [swb[1]], [swb[2]],
                 lambda: nc.vector.tensor_tensor(out=sw[:, 2, 7:W_], in0=sw[:, 1, 7:W_], in1=sw[:, 1, 3:W_ - 4],
                                                 op=ALU.add))
            yield
            P.op(P.dve, [swb[2]], [swb[3]],
                 lambda: nc.vector.tensor_tensor(out=sw[:, 3, 15:W_], in0=sw[:, 2, 15:W_], in1=sw[:, 2, 7:W_ - 8],
                                                 op=ALU.add))
            yield
            rv = 12 + cc
            P.op(P.dve, [swb[0], ubb[cc], pabuf], [tb[rv]],
                 lambda: nc.vector.scalar_tensor_tensor(out=tmp[:, rv, :], in0=sw[:, 0, 15:W_], scalar=pa[:, 20:21],
                                                        in1=ub[:, cc, 15:W_], op0=ALU.mult, op1=ALU.subtract))
            yield
            for wi in (1, 2):
                P.op(P.dve, [swb[wi], tb[rv], pabuf], [tb[rv]],
                     lambda wi=wi: nc.vector.scalar_tensor_tensor(out=tmp[:, rv, :], in0=sw[:, wi, 15:W_],
                                                                  scalar=pa[:, 20 + wi:21 + wi], in1=tmp[:, rv, :],
                                                                  op0=ALU.mult, op1=ALU.add))
                yield
            pl = 4 + cc
            P.op(P.dve, [swb[3], tb[rv], pabuf], [tbb[pl]],
                 lambda: nc.vector.scalar_tensor_tensor(out=tmpb[:, pl, :], in0=sw[:, 3, 15:W_], scalar=pa[:, 23:24],
                                                        in1=tmp[:, rv, :], op0=ALU.mult, op1=ALU.add))
            yield
            if it == 0:
                P.op(P.dve, [swb[0], ptabbuf], [t16b[0]],
                     lambda: nc.vector.tensor_tensor(out=t16[:, 0, :], in0=sw[:, 0, 15:31], in1=ptab[:, 0, :],
                                                     op=ALU.mult))
                yield
                for wi in (1, 2, 3):
                    P.op(P.dve, [swb[wi], ptabbuf], [t16b[1]],
                         lambda wi=wi: nc.vector.tensor_tensor(out=t16[:, 1, :], in0=sw[:, wi, 15:31],
                                                               in1=ptab[:, wi, :], op=ALU.mult))
                    yield
                    P.op(P.dve, [t16b[0], t16b[1]], [t16b[0]],
                         lambda: nc.vector.tensor_tensor(out=t16[:, 0, :], in0=t16[:, 0, :], in1=t16[:, 1, :],
                                                         op=ALU.add))
                    yield
                P.op(P.dve, [t16b[0], ubb[cc]], [tbb[pl]],
                     lambda: nc.vector.tensor_tensor(out=tmpb[:, pl, 0:16], in0=t16[:, 0, :], in1=ub[:, cc, 15:31],
                                                     op=ALU.subtract))
                yield
            P.op(P.dve, [ubb[cc]], [ubb[cc]],
                 lambda: nc.vector.tensor_copy(out=ub[:, cc, 0:15], in_=ub[:, cc, TT:TT + 15]))
            yield
        def pool_both():
            yield from pool_gen(0)
            yield from pool_gen(1)
        gens = [lru_gen(0), lru_gen(1), pool_both()]
        while gens:
            for g_ in list(gens):
                try:
                    next(g_)
                except StopIteration:
                    gens.remove(g_)
        for dc in range(2):
            bk = 6 + dc
            def emit(bk=bk, dc=dc):
                ins = None
                for cc in range(2):
                    ins = nc.tensor.matmul(ps[:, bk, :], lhsT=wpl[:, cc, dc * 128:(dc + 1) * 128],
                                           rhs=tmpb[:, 4 + cc, :], start=(cc == 0), stop=(cc == 1))
                return ins
            P.op(P.pe, [tbb[4], tbb[5], wplbuf], [pb[bk]], emit)
            yi = dc
            pti = state["pt"]
            state["pt"] = (pti + 1) % 3
            P.op(P.act, [pb[bk], pabuf], [ptb[pti]],
                 lambda bk=bk, pti=pti, dc=dc: nc.scalar.activation(out=pT[:, pti, :], in_=ps[:, bk, :],
                                                                    func=AF.Identity,
                                                                    scale=pa[:, 16 + dc:17 + dc]))
            store_y(2, dc * 128, pT[:, pti, :], ptb[pti], t0)

    d["after_part"]((0, 1, 4, 5))
    SB0, SB1, OB, SMB, QKB, VB, FB, SMALL = 0, 1, 2, 3, 4, 5, 6, 7
    scale = 1.0 / SQRT_DH
    for hd in range(2):
        if hd == 1:
            load_w2(1)
            d["after_part"]((2,))
        P.op(P.dve, [], [fcb[0]], lambda: nc.vector.memset(fc[:, 0:1], 0.0))
        for it in range(NT):
            t0 = it * TT
            hj = it % 2
            if it == 0:
                load_h(0, 0)
            if it + 1 < NT:
                load_h(it + 1, (it + 1) % 2)
            def emit_f():
                ins = None
                for k in range(KD):
                    ins = nc.tensor.matmul(ps[:, FB, :], lhsT=w2[:, k, 384:512], rhs=hT[:, hj, k, :],
                                           start=(k == 0), stop=(k == KD - 1))
                return ins
            P.op(P.pe, [w2buf] + hbufs[hj], [pb[FB]], emit_f)
            e, fa, fr = T(), T(), T()
            P.op(P.act, [pb[FB], pabuf], [tb[e]],
                 lambda: nc.scalar.activation(out=tmp[:, e, :], in_=ps[:, FB, :], func=AF.Exp, scale=-1.0,
                                              bias=pa[:, 18 + hd:19 + hd]))
            P.op(P.act, [tb[e], one32b], [tb[e]],
                 lambda: nc.scalar.activation(out=tmp[:, e, :], in_=tmp[:, e, :], func=AF.Ln, bias=one32[:, 0:1]))
            fi, fo = it % 2, (it + 1) % 2
            P.op(P.dve, [tb[e], one32b, fcb[fi]], [tb[fa]],
                 lambda: nc.vector.tensor_tensor_scan(out=tmp[:, fa, :], data0=onesT[:, :], data1=tmp[:, e, :],
                                                      initial=fc[:, fi:fi + 1], op0=ALU.mult, op1=ALU.subtract))
            P.op(P.dve, [tb[fa]], [fcb[fo]],
                 lambda: nc.vector.tensor_copy(out=fc[:, fo:fo + 1], in_=tmp[:, fa, TT - 1:TT]))
            P.op(P.dve, [tb[fa], fcb[fi]], [tb[fr]],
                 lambda: nc.vector.tensor_scalar(out=tmp[:, fr, :], in0=tmp[:, fa, :], scalar1=fc[:, fi:fi + 1],
                                                 scalar2=None, op0=ALU.subtract))
            for j in (1, 2, 3):
                P.op(P.dve, [tb[fr]], [ncbuf],
                     lambda j=j: nc.vector.tensor_scalar(out=negcol[:, j:j + 1], in0=tmp[:, fr, j * 128 - 1:j * 128],
                                                         scalar1=-1.0, scalar2=None, op0=ALU.mult))
            P.op(P.act, [tb[fr]], [etb[0]],
                 lambda: nc.scalar.activation(out=Et[:, 0, :], in_=tmp[:, fr, :], func=AF.Exp))
            for j in range(4):
                P.op(P.act, [tb[fr], ncbuf], [etb[1 + j]],
                     lambda j=j: nc.scalar.activation(out=Et[:, 1 + j, j * 128:TT], in_=tmp[:, fr, j * 128:TT],
                                                      func=AF.Exp, bias=negcol[:, j:j + 1]))
                P.op(P.dve, [etb[1 + j], trib], [etb[1 + j]],
                     lambda j=j: nc.vector.tensor_tensor(out=Et[:, 1 + j, j * 128:(j + 1) * 128],
                                                         in0=Et[:, 1 + j, j * 128:(j + 1) * 128], in1=tri[:, :],
                                                         op=ALU.mult))
            qi = 0
            gemm_chunk(w2, w2buf, 0, hj, QKB)
            P.op(P.act, [pb[QKB]], [tbb[qi]],
                 lambda: nc.scalar.activation(out=tmpb[:, qi, :], in_=ps[:, QKB, :], func=AF.Copy))
            gemm_chunk(w2, w2buf, 128, hj, QKB)
            P.op(P.act, [pb[QKB]], [ktb[it]],
                 lambda: nc.scalar.activation(out=KT[:, t0:t0 + TT], in_=ps[:, QKB, :], func=AF.Copy))
            def emit_v():
                ins = None
                for tbk in range(4):
                    for k in range(KD):
                        ins = nc.tensor.matmul(ps[:, VB, tbk * 128:(tbk + 1) * 128],
                                               lhsT=hT[:, hj, k, tbk * 128:(tbk + 1) * 128], rhs=w2[:, k, 256:384],
                                               start=(k == 0), stop=(k == KD - 1))
                return ins
            P.op(P.pe, [w2buf] + hbufs[hj], [pb[VB]], emit_v)
            P.op(P.act, [pb[VB]], [vb[it]],
                 lambda: nc.scalar.activation(out=V[:, it * 4:(it + 1) * 4, :],
                                              in_=ps[:, VB, :].rearrange("p (a b) -> p a b", a=4), func=AF.Copy))
            def emit_small():
                ins = None
                for tbk in range(4):
                    ins = nc.tensor.matmul(ps[:, SMALL, tbk:tbk + 1], lhsT=tmp[0:1, fa, tbk * 128:(tbk + 1) * 128],
                                           rhs=one32[0:1, 0:1], start=True, stop=True)
                ins = nc.tensor.matmul(ps[:, SMALL, 4:5], lhsT=one32[0:1, :], rhs=fc[0:1, fi:fi + 1],
                                       start=True, stop=True)
                return ins
            P.op(P.pe, [tb[fa], one32b, fcb[fi]], [pb[SMALL]], emit_small)
            P.op(P.dve, [pb[SMALL]], [nfb[it]],
                 lambda: nc.vector.tensor_scalar(out=negF[:, it * 4:(it + 1) * 4], in0=ps[:, SMALL, 0:4],
                                                 scalar1=-1.0, scalar2=None, op0=ALU.mult))
            P.op(P.dve, [pb[SMALL]], [ctbuf], lambda: nc.vector.tensor_copy(out=ct[:, :], in_=ps[:, SMALL, 4:5]))
            nkb = (it + 1) * 4
            P.op(P.dve, nfb[0:it + 1] + [ctbuf], [bmbuf],
                 lambda: nc.vector.tensor_scalar(out=bm[:, 0:nkb], in0=negF[:, 0:nkb], scalar1=ct[:, 0:1],
                                                 scalar2=None, op0=ALU.add))
            P.op(P.dve, [bmbuf, ncbuf], [bmbuf],
                 lambda: nc.vector.tensor_tensor(out=bm[:, it * 4:it * 4 + 4], in0=bm[:, it * 4:it * 4 + 4],
                                                 in1=negcol[:, 0:4], op=ALU.subtract))
            SBANKS = [SB0, SB1, QKB, VB]
            PRE = 3

            def issue_s(kb):
                j = kb - it * 4
                c0 = max(j, 0) * 128
                sbk = SBANKS[kb % 4]

                def emit_s():
                    return nc.tensor.matmul(ps[:, sbk, c0:TT], lhsT=KT[:, kb * 128:(kb + 1) * 128],
                                            rhs=tmpb[:, qi, c0:TT], start=True, stop=True)
                P.op(P.pe, [ktb[kb // 4], tbb[qi]], [pb[sbk]], emit_s)

            for kb in range(min(PRE, nkb)):
                issue_s(kb)
            for kb in range(nkb):
                if kb + PRE < nkb:
                    issue_s(kb + PRE)
                j = kb - it * 4
                c0 = max(j, 0) * 128
                sbk = SBANKS[kb % 4]
                pti = state["pt"]
                state["pt"] = (pti + 1) % 3
                P.op(P.act, [pb[sbk], bmbuf], [ptb[pti]],
                     lambda: nc.scalar.activation(out=pT[:, pti, c0:TT], in_=ps[:, sbk, c0:TT], func=AF.Exp,
                                                  scale=scale, bias=bm[:, kb:kb + 1]))
                ei = 0 if j < 0 else 1 + j
                P.op(P.dve, [ptb[pti], etb[ei]], [ptb[pti]],
                     lambda: nc.vector.tensor_tensor(out=pT[:, pti, c0:TT], in0=pT[:, pti, c0:TT],
                                                     in1=Et[:, ei, c0:TT], op=ALU.mult))

                def emit_o():
                    nc.tensor.matmul(ps[:, OB, c0:TT], lhsT=V[:, kb, :], rhs=pT[:, pti, c0:TT],
                                     start=(kb == 0), stop=(kb == nkb - 1))
                    return nc.tensor.matmul(ps[:, SMB, c0:TT], lhsT=ones_bf[:, :], rhs=pT[:, pti, c0:TT],
                                            start=(kb == 0), stop=(kb == nkb - 1))
                P.op(P.pe, [vb[kb // 4], ptb[pti], onesb], [pb[OB], pb[SMB]], emit_o)
            rc = T()
            P.op(P.dve, [pb[SMB]], [tb[rc]], lambda: nc.vector.reciprocal(out=tmp[:, rc, :], in_=ps[:, SMB, :]))
            yi = 1 + it % 2
            P.op(P.dve, [pb[OB], tb[rc]], [tbb[yi]],
                 lambda: nc.vector.tensor_tensor(out=tmpb[:, yi, :], in0=ps[:, OB, :], in1=tmp[:, rc, :], op=ALU.mult))
            store_y(1, hd * 128, tmpb[:, yi, :], tbb[yi], t0)
    d["after_part"]((3,))
    P.pop_scope()


POOL_WINDOWS = (2, 4, 8, 16)


def prep_A_inputs(l, b, j, xT_b, g_mix, w_in, b_forget, conv_w, conv_b, w_rg, b_rg, w_ig, b_ig, lru_lambda,
                  w_pool, pool_scale):
    W = w_in[l]

    def arr(cols):
        return np.ascontiguousarray(cols.reshape(16, 128, cols.shape[1]).transpose(1, 0, 2))
    w1 = np.concatenate([W[:, 2 * j * 128:(2 * j + 2) * 128], W[:, 1024 + 2 * j * 128:1024 + (2 * j + 2) * 128],
                         W[:, 5128 + j * 256:5128 + (j + 1) * 256]], axis=1)
    w2 = []
    for hd in range(2):
        h = 2 * j + hd
        w2.append(arr(np.concatenate([W[:, 2048 + h * 128:2048 + (h + 1) * 128],
                                      W[:, 3072 + h * 128:3072 + (h + 1) * 128],
                                      W[:, 4096 + h * 128:4096 + (h + 1) * 128],
                                      np.repeat(W[:, 5120 + h:5121 + h], 128, axis=1)], axis=1)))
    wrg = np.stack([w_rg[l][2 * j], w_ig[l][2 * j], w_rg[l][2 * j + 1], w_ig[l][2 * j + 1]], axis=1)
    wpl = np.ascontiguousarray(w_pool[l][j].reshape(2, 128, 256).transpose(1, 0, 2))
    pa = np.zeros((128, NPAR), np.float32)
    for blk in range(2):
        ch = slice((2 * j + blk) * 128, (2 * j + blk + 1) * 128)
        for tap in range(4):
            pa[:, blk * 4 + tap] = conv_w[l][tap, ch]
        pa[:, 8 + blk] = conv_b[l][ch]
        pa[:, 10 + blk] = b_rg[l][ch]
        pa[:, 12 + blk] = b_ig[l][ch]
        pa[:, 14 + blk] = lru_lambda[l][ch]
        pa[:, 16 + blk] = pool_scale[l][j * 256 + blk * 128:j * 256 + (blk + 1) * 128]
        pa[:, 18 + blk] = b_forget[l][2 * j + blk]
    ptab = np.zeros((128, 4, 16), np.float32)
    for wi, w in enumerate(POOL_WINDOWS):
        if wi == j:
            pa[:, 20 + wi] = 1.0 / w
            ptab[:, wi, :] = 1.0 / np.minimum(np.arange(1, 17), w)
    return dict(w1=arr(w1), w2=np.stack(w2),
                wrg=np.ascontiguousarray(wrg), wpl=wpl, pa=pa, ptab=ptab)


GROUPS = [[0, 1, 2, 3], [4, 5, 6, 7]]


def build_fused(S, B_TT):
    P = Prog()
    nc = P.nc
    NTOK = S // 4
    ntt = NTOK // B_TT
    ps = P.psum("ps", [128, 8, 512], F32)
    pb = [Buf("p%d" % i) for i in range(8)]
    xT = P.dram("xT", [D, NTOK], F32, "ExternalInput")
    gm0 = P.dram("gm0", [128, 16], F32, "ExternalInput")
    ptab = P.dram("ptab", [128, 4, 16], F32, "ExternalInput")
    oT = P.dram("oT", [D, NTOK], F32, "ExternalOutput")
    A_in, B_in = [], []
    for l in range(2):
        A_in.append(dict(
            w1=P.dram("w1_%d" % l, [128, 16, 768], F32, "ExternalInput"),
            w2=P.dram("w2_%d" % l, [256, 16, 512], F32, "ExternalInput"),
            wrg=P.dram("wrg_%d" % l, [128, 4, 128], F32, "ExternalInput"),
            wpl=P.dram("wpl_%d" % l, [128, 2, 256], F32, "ExternalInput"),
            pa=P.dram("pa_%d" % l, [128, NPAR], F32, "ExternalInput"),
            ptab=ptab))
        B_in.append(dict(
            wg=P.dram("wg_%d" % l, [48, 128, 16, 128], F32, "ExternalInput"),
            wbr=P.dram("wbr_%d" % l, [48, 128, 8, 128], F32, "ExternalInput"),
            wo=P.dram("wo_%d" % l, [16, 128, 16, 128], F32, "ExternalInput"),
            wf1=P.dram("wf1_%d" % l, [88, 128, 16, 128], F32, "ExternalInput"),
            wf2=P.dram("wf2_%d" % l, [16, 128, 44, 128], F32, "ExternalInput"),
            gv=P.dram("gv_%d" % l, [128, 3, 16], F32, "ExternalInput")))
    NSR = NTOK // 256
    hloc = [P.dram("hloc%d" % l, [NSR * D, 256], BF16) for l in range(2)]
    hfull = [P.dram("hfull%d" % l, [NSR * 4 * D, 256], BF16) for l in range(2)]
    yo = [P.dram("yo%d" % l, [4 * 768, NTOK], BF16) for l in range(2)]
    yfull = [P.dram("yfull%d" % l, [4 * 4 * 768, NTOK], BF16) for l in range(2)]
    x1T = [P.dram("x1T%d" % l, [D, NTOK], F32) for l in range(2)]
    x2T = [P.dram("x2T%d" % l, [D, NTOK], F32) for l in range(2)]
    ysl = [P.dram("ysl%d" % l, [4 * 768, NTOK], BF16) for l in range(2)]
    qreg = nc.sync.partition_id() % 4

    def mk():
        return [[Buf("b") for _ in range(16)] for _ in range(ntt)]

    emit_N(P, ps, pb, NTOK, B_TT, xT, gm0, hloc[0], mk())
    x2prev = None
    outb = None
    for l in range(2):
        hfbuf = [Buf("hf%d_%d" % (l, c)) for c in range(NSR)]

        def after_setup(l=l, hfbuf=hfbuf):
            for c in range(NSR):
                P.cc([], [hfbuf[c]], lambda c=c: nc.gpsimd.collective_compute(
                    "AllGather", ALU.bypass, replica_groups=GROUPS, ins=[hloc[l][c * D:(c + 1) * D, :]],
                    outs=[hfull[l][c * 4 * D:(c + 1) * 4 * D, :]]))
        ybuf = Buf("yf%d" % l)

        def after_part(rbs, l=l, ybuf=ybuf):
            P._wait(P.pool, {S_.sem: S_.n for nm, S_ in P.dmaq.items() if nm.startswith("yo") and S_.n > 0})
            for q in range(4):
                for rb in rbs:
                    qb = q * 6 + rb
                    P.cc([], [ybuf], lambda qb=qb: nc.gpsimd.collective_compute(
                        "AllGather", ALU.bypass, replica_groups=GROUPS, ins=[yo[l][qb * 128:(qb + 1) * 128, :]],
                        outs=[yfull[l][qb * 512:(qb + 1) * 512, :]]))
        emit_A(P, ps, pb, S, dict(A_in[l], hfull=hfull[l], hfbuf=hfbuf, yo=yo[l], after_part=after_part,
                                 after_setup=after_setup))
        yslb = Buf("ysl%d" % l)
        for r in range(4):
            P.dma(P.sp, "ysl", [ybuf], [yslb],
                  lambda r=r: nc.sync.dma_start(out=ysl[l][r * 768:(r + 1) * 768, :],
                                                in_=yfull[l][ds(qreg * 3072 + r * 768, 768), :]))
        x2b = mk()
        outb = mk()
        dd = dict(B_in[l], xsrc=(xT if l == 0 else x2T[0]),
                  xsrc_buf=((lambda tt, k: None) if l == 0 else (lambda tt, k, xp=x2prev: xp[tt][k])),
                  yfull=ysl[l], ybuf=yslb, x1T=x1T[l], x2T=x2T[l], dst=(hloc[1] if l == 0 else oT),
                  x2bufs=x2b, outbufs=outb)
        emit_B(P, ps, pb, NTOK, B_TT, dd, last=(l == 1))
        x2prev = x2b
    P.finish([b for row in outb for b in row])
    return P


_CACHE = {}
B_TT_FULL = 1024


def prep_inputs(x, g_mix, w_in, b_forget, conv_w, conv_b, w_rg, b_rg, w_ig, b_ig, lru_lambda, w_pool, pool_scale,
                w_branch_rnn, w_branch_attn, w_branch_pool, w_out, g_ffn, w_ffn_in, w_ffn_out, g_final):
    nb, S, _ = x.shape
    NTOK = S // 4
    BW = []
    for l in range(2):
        g_next = g_mix[1] if l == 0 else g_final
        W = prep_B_weights(l, w_in, w_branch_rnn, w_branch_attn, w_branch_pool, w_out, w_ffn_in, w_ffn_out,
                           g_mix, g_ffn, g_next)
        BW.append({"%s_%d" % (k, l): v for k, v in W.items()})
    in_maps = []
    for c in range(NCORES):
        b, j = c // 4, c % 4
        m = {"xT": np.ascontiguousarray(x[b, j * NTOK:(j + 1) * NTOK, :].T),
             "gm0": np.ascontiguousarray(g_mix[0].reshape(16, 128).T)}
        for l in range(2):
            a = prep_A_inputs(l, b, j, None, g_mix, w_in, b_forget, conv_w, conv_b, w_rg, b_rg, w_ig, b_ig,
                              lru_lambda, w_pool, pool_scale)
            m["ptab"] = a["ptab"]
            m["w1_%d" % l] = a["w1"]
            m["w2_%d" % l] = np.ascontiguousarray(a["w2"].reshape(256, 16, 512))
            m["wrg_%d" % l] = a["wrg"]
            m["wpl_%d" % l] = a["wpl"]
            m["pa_%d" % l] = a["pa"]
            m.update(BW[l])
        in_maps.append(m)
    return in_maps


def kernel(x, g_mix, w_in, b_forget, conv_w, conv_b, w_rg, b_rg, w_ig, b_ig, lru_lambda, w_pool, pool_scale,
           w_branch_rnn, w_branch_attn, w_branch_pool, w_out, g_ffn, w_ffn_in, w_ffn_out, g_final):
    args = [np.asarray(a, dtype=np.float32) for a in (
        x, g_mix, w_in, b_forget, conv_w, conv_b, w_rg, b_rg, w_ig, b_ig, lru_lambda, w_pool, pool_scale,
        w_branch_rnn, w_branch_attn, w_branch_pool, w_out, g_ffn, w_ffn_in, w_ffn_out, g_final)]
    x = args[0]
    nb, S, _ = x.shape
    NTOK = S // 4
    tt = B_TT_FULL if NTOK % B_TT_FULL == 0 else 512
    key = (S, tt)
    if key not in _CACHE:
        _CACHE[key] = build_fused(S, tt)
    P = _CACHE[key]
    in_maps = prep_inputs(*args)
    res = run_bass_kernel_spmd(P.nc, in_maps, core_ids=list(range(NCORES)))
    out = np.empty((nb, S, D), np.float32)
    for c in range(NCORES):
        b, j = c // 4, c % 4
        out[b, j * NTOK:(j + 1) * NTOK, :] = res.results[c]["oT"].T
    return out
```

```python
import contextlib
import numpy as np
import ml_dtypes
import concourse.bass as bass
import concourse.mybir as mybir
from concourse.bass import ds
from concourse.bass_utils import run_bass_kernel_spmd

F32 = mybir.dt.float32
BF16 = mybir.dt.bfloat16
AF = mybir.ActivationFunctionType
ALU = mybir.AluOpType

D = 2048
KD = D // 128
DFF = 5632
KFF = DFF // 128
EPS = 1e-6
NCORES = 8
SEQ = 16384
SAME_ENGINE_SYNC = True
SEM_LIMIT = 30000


class Eng:
    def __init__(self, P, name, h, step):
        self.P, self.name, self.h, self.step = P, name, h, step
        self.n = 0
        self.sem = P.new_sem(name)
        self.nsem = 0
        self.waited = {}

    def tick(self):
        if self.n >= SEM_LIMIT * self.step:
            self.nsem += 1
            self.sem = self.P.new_sem("%s_%d" % (self.name, self.nsem))
            self.n = 0
        self.n += self.step
        return (self.sem, self.n)


class Buf:
    __slots__ = ("name", "writer", "readers")

    def __init__(self, name):
        self.name = name
        self.writer = None
        self.readers = {}


class Prog:
    def __init__(self):
        self.nc = bass.Bass("TRN2", target_bir_lowering=False)
        self.stack = contextlib.ExitStack()
        self.nsem_total = 0
        nc = self.nc
        self.pe = Eng(self, "pe", nc.tensor, 1)
        self.act = Eng(self, "act", nc.scalar, 1)
        self.dve = Eng(self, "dve", nc.vector, 1)
        self.pool = Eng(self, "pool", nc.gpsimd, 1)
        self.sp = Eng(self, "sp", nc.sync, 1)
        self.dmaq = {}
        self.scopes = []
        self.nname = 0
        self.xr = 0

    def new_sem(self, name):
        self.nsem_total += 1
        return self.stack.enter_context(self.nc.semaphore("s_%s_%d" % (name, self.nsem_total)))

    def sbuf(self, name, shape, dt):
        st = self.scopes[-1] if self.scopes else self.stack
        self.nname += 1
        return st.enter_context(self.nc.sbuf_tensor("%s_%d" % (name, self.nname), shape, dt))

    def push_scope(self):
        self.scopes.append(contextlib.ExitStack())

    def pop_scope(self):
        self.barrier()
        self.scopes.pop().close()

    def barrier(self):
        engs = [self.pe, self.act, self.dve, self.pool, self.sp]
        targets = {}
        for E in engs + list(self.dmaq.values()):
            if E.n > 0:
                targets[E.sem] = E.n
        for E in engs:
            self._wait(E, {s: n for s, n in targets.items() if s is not E.sem})

    def cc(self, reads, writes, emit):
        if "cc" not in self.dmaq:
            self.dmaq["cc"] = Eng(self, "cc", None, 1)
        S = self.dmaq["cc"]
        self._wait(self.pool, self._deps(reads, writes))
        ins = emit()
        tok = S.tick()
        ins.then_inc(tok[0])
        for b in reads:
            b.readers[id(S)] = tok
        for b in writes:
            b.writer = tok
            b.readers = {}
        return tok

    def psum(self, name, shape, dt=F32):
        return self.stack.enter_context(self.nc.psum_tensor(name, shape, dt))

    def dram(self, name, shape, dt, kind="Internal"):
        return self.nc.dram_tensor(name, shape, dt, kind=kind).ap()

    def _deps(self, reads, writes):
        deps = {}
        for b in reads:
            if b.writer is not None:
                s, n = b.writer
                deps[s] = max(deps.get(s, 0), n)
        for b in writes:
            if b.writer is not None:
                s, n = b.writer
                deps[s] = max(deps.get(s, 0), n)
            for (s, n) in b.readers.values():
                deps[s] = max(deps.get(s, 0), n)
        return deps

    def _wait(self, E, deps, skip_sem=None):
        for s, n in deps.items():
            if s is skip_sem:
                continue
            if E.waited.get(s, 0) < n:
                E.h.wait_ge(s, n)
                E.waited[s] = n

    def op(self, E, reads, writes, emit):
        deps = self._deps(reads, writes)
        skip = None
        if E is self.pe or not SAME_ENGINE_SYNC:
            skip = E.sem
        self._wait(E, deps, skip)
        ins = emit()
        tok = E.tick()
        ins.then_inc(tok[0], 1)
        for b in reads:
            b.readers[id(E)] = tok
        for b in writes:
            b.writer = tok
            b.readers = {}
        return tok

    def dma(self, Q, slot, reads, writes, emit):
        if slot not in self.dmaq:
            self.dmaq[slot] = Eng(self, "d" + slot, None, 16)
        S = self.dmaq[slot]
        deps = self._deps(reads, writes)
        self._wait(Q, deps)
        ins = emit()
        tok = S.tick()
        ins.then_inc(tok[0], 16)
        for b in reads:
            b.readers[id(S) + len(b.readers) * 0] = tok
        for b in writes:
            b.writer = tok
            b.readers = {}
        return tok

    def finish(self, bufs):
        deps = {}
        for b in bufs:
            if b.writer is not None:
                s, n = b.writer
                deps[s] = max(deps.get(s, 0), n)
        self._wait(self.sp, deps)
        self.stack.close()


class WRing:
    def __init__(self, P, nslots):
        self.P = P
        self.n = nslots
        self.t = P.sbuf("wring", [128, nslots, 16, 128], BF16)
        self.bufs = [Buf("w%d" % i) for i in range(nslots)]
        P.wring_gen = getattr(P, "wring_gen", 0) + 1
        self.gen = P.wring_gen
        self.i = 0

    def load(self, src, kc):
        P = self.P
        i = self.i
        self.i = (self.i + 1) % self.n
        dst = self.t[:, i, 0:kc, :]
        b = self.bufs[i]
        P.dma(P.pool, "w%d_%d" % (self.gen, i), [], [b],
              lambda: P.nc.gpsimd.dma_start(out=dst, in_=src, max_dma_last_dim=4096))
        return self.t[:, i], b


def emit_norm(P, C, src_chunk, src_buf, gcol, out_fn, TT):
    nc = P.nc
    nsub = TT // 512
    xring, xbufs, sq, sqbufs, ps, pb, rstd, rstdbuf, ones_bf, onesb = (
        C["xring"], C["xbufs"], C["sq"], C["sqbufs"], C["ps"], C["pb"], C["rstd"], C["rstdbuf"], C["ones_bf"],
        C["onesb"])
    nx = len(xbufs)
    for k in range(KD):
        xi = P.xr
        P.xr = (P.xr + 1) % nx
        sb = src_buf(k)
        P.dma(P.sp, "x%d" % xi, [sb] if sb else [], [xbufs[xi]],
              lambda xi=xi, k=k: nc.sync.dma_start(out=xring[:, xi, :], in_=src_chunk(k)))
        si = k % 2
        P.op(P.act, [xbufs[xi]], [sqbufs[si]],
             lambda xi=xi, si=si: nc.scalar.activation(out=sq[:, si, :], in_=xring[:, xi, :], func=AF.Square))
        for s in range(nsub):
            P.op(P.pe, [sqbufs[si], onesb], [pb[6 + s]],
                 lambda s=s, si=si, k=k: nc.tensor.matmul(ps[:, 6 + s, :], lhsT=ones_bf[:, :],
                                                          rhs=sq[:, si, s * 512:(s + 1) * 512],
                                                          start=(k == 0), stop=(k == KD - 1)))
    for s in range(nsub):
        P.op(P.act, [pb[6 + s]], [rstdbuf],
             lambda s=s: nc.scalar.activation(out=rstd[:, s * 512:(s + 1) * 512], in_=ps[:, 6 + s, :],
                                              func=AF.Sqrt, scale=1.0 / D, bias=C["eps"][:, 0:1]))
    P.op(P.dve, [rstdbuf], [rstdbuf], lambda: nc.vector.reciprocal(out=rstd[:, :], in_=rstd[:, :]))
    for k in range(KD):
        xi = P.xr
        P.xr = (P.xr + 1) % nx
        sb = src_buf(k)
        P.dma(P.sp, "x%d" % xi, [sb] if sb else [], [xbufs[xi]],
              lambda xi=xi, k=k: nc.sync.dma_start(out=xring[:, xi, :], in_=src_chunk(k)))
        out_fn(k, xring[:, xi, :], gcol[:, k:k + 1], rstd[:, :], [xbufs[xi], rstdbuf])


def common_setup(P, TT, ps, pb, NX=4):
    nc = P.nc
    C = {}
    C["ones_bf"] = P.sbuf("ones_bf", [128, 128], BF16)
    C["onesb"] = Buf("ones")
    C["eps"] = P.sbuf("eps", [128, 1], F32)
    C["xring"] = P.sbuf("xring", [128, NX, TT], F32)
    C["xbufs"] = [Buf("x%d" % i) for i in range(NX)]
    C["sq"] = P.sbuf("sq", [128, 2, TT], BF16)
    C["sqbufs"] = [Buf("sq0"), Buf("sq1")]
    C["rstd"] = P.sbuf("rstd", [128, TT], F32)
    C["rstdbuf"] = Buf("rstd")
    C["ps"] = ps
    C["pb"] = pb
    P.xr = 0
    P.op(P.dve, [], [C["onesb"]], lambda: nc.vector.memset(C["ones_bf"][:], 1.0))
    P.op(P.dve, [], [C["rstdbuf"]], lambda: nc.vector.memset(C["eps"][:], EPS))
    return C


def emit_B(P, ps_, pb_, NTOK, TT, d, last):
    nc = P.nc
    nsub = TT // 512
    xT, xsrc_buf = d["xsrc"], d["xsrc_buf"]
    yfull, ybuf = d["yfull"], d["ybuf"]
    wg, wbr, wo, wf1, wf2, gv = d["wg"], d["wbr"], d["wo"], d["wf1"], d["wf2"], d["gv"]
    x1T, x2T, dst = d["x1T"], d["x2T"], d["dst"]
    ntt = NTOK // TT
    x1bufs = [[Buf("x1_%d_%d" % (t, n)) for n in range(16)] for t in range(ntt)]
    x2bufs = d["x2bufs"]
    outbufs = d["outbufs"]
    final_norm = True

    P.push_scope()
    C = common_setup(P, TT, ps_, pb_)
    ps, pb, xring, xbufs = C["ps"], C["pb"], C["xring"], C["xbufs"]
    NX = len(xbufs)
    g_sb = P.sbuf("g_sb", [128, 3, 16], F32)
    gbuf = Buf("g")
    hT = P.sbuf("hT", [128, KD, TT], BF16)
    hbufs = [Buf("h%d" % k) for k in range(KD)]
    R = P.sbuf("R", [128, KFF, TT], BF16)
    rb = [Buf("r%d" % k) for k in range(KFF)]
    tmp = P.sbuf("tmp", [128, 3, TT], F32)
    tb = [Buf("t%d" % i) for i in range(3)]
    ost = P.sbuf("ost", [128, 2, TT], F32)
    ob = [Buf("o%d" % i) for i in range(2)]
    hst = P.sbuf("hst", [128, 2, TT], BF16)
    hsb_ = [Buf("hst%d" % i) for i in range(2)]
    wr = WRing(P, 8)

    P.dma(P.sp, "g", [], [gbuf], lambda: nc.sync.dma_start(out=g_sb[:], in_=gv))

    state = {"pset": 0, "oi": 0}

    def v3(ap2d):
        return ap2d.rearrange("p (s t) -> p s t", s=nsub)

    def pview(pset):
        return ps[:, pset * 2:pset * 2 + nsub, :]

    def pbufs(pset):
        return pb[pset * 2:pset * 2 + nsub]

    def gemm(wsrcs, kc_list, rhs_fn, rhs_bufs):
        pset = state["pset"]
        state["pset"] ^= 1
        slots = [wr.load(src, kc) for src, kc in zip(wsrcs, kc_list)]
        ktot = sum(kc_list)
        for s in range(nsub):
            def emit(s=s):
                kk = 0
                ins = None
                for (wt, wbuf), kc in zip(slots, kc_list):
                    for k in range(kc):
                        ins = nc.tensor.matmul(ps[:, pset * 2 + s, :], lhsT=wt[:, k, :], rhs=rhs_fn(kk, s),
                                               start=(kk == 0), stop=(kk == ktot - 1))
                        kk += 1
                return ins
            P.op(P.pe, [wb_ for (_, wb_) in slots] + rhs_bufs, [pb[pset * 2 + s]], emit)
        return pset

    def h_rhs(kk, s):
        return hT[:, kk, s * 512:(s + 1) * 512]

    def R_rhs(kk, s):
        return R[:, kk, s * 512:(s + 1) * 512]

    def norm_to_h(k, x_ap, g_ap, rstd_ap, reads):
        P.op(P.dve, reads + [gbuf], [hbufs[k]],
             lambda: nc.vector.scalar_tensor_tensor(out=hT[:, k, :], in0=x_ap, scalar=g_ap, in1=rstd_ap,
                                                    op0=ALU.mult, op1=ALU.mult))

    for tt in range(ntt):
        t0 = tt * TT
        emit_norm(P, C, lambda k: xT[k * 128:(k + 1) * 128, t0:t0 + TT], lambda k: xsrc_buf(tt, k), g_sb[:, 0, :],
                  norm_to_h, TT)
        for i in range(3):
            for k in range(8):
                r = 16 + i * 8 + k
                row0 = ((i * 2 + k % 2) * 4 + k // 2) * 128
                P.dma(P.sp, "y%d" % (i * 8 + k), [ybuf], [rb[r]],
                      lambda r=r, row0=row0: nc.sync.dma_start(
                          out=R[:, r, :], in_=yfull[row0:row0 + 128, t0:t0 + TT]))
        for m in range(16):
            for i in range(3):
                pset = gemm([wg[i * 16 + m]], [16], h_rhs, hbufs)
                P.op(P.act, pbufs(pset), [tb[0]],
                     lambda pset=pset: nc.scalar.activation(out=v3(tmp[:, 0, :]), in_=pview(pset), func=AF.Sigmoid))
                pset = gemm([wbr[i * 16 + m]], [8],
                            lambda kk, s, i=i: R[:, 16 + i * 8 + kk, s * 512:(s + 1) * 512],
                            rb[16 + i * 8:16 + i * 8 + 8])
                dsti = 1 if i == 0 else 2
                P.op(P.dve, pbufs(pset) + [tb[0]], [tb[dsti]],
                     lambda pset=pset, dsti=dsti: nc.vector.tensor_tensor(
                         out=v3(tmp[:, dsti, :]), in0=pview(pset), in1=v3(tmp[:, 0, :]), op=ALU.mult))
                if i == 1:
                    P.op(P.dve, [tb[1], tb[2]], [tb[1]],
                         lambda: nc.vector.tensor_tensor(out=tmp[:, 1, :], in0=tmp[:, 1, :], in1=tmp[:, 2, :],
                                                         op=ALU.add))
                elif i == 2:
                    P.op(P.dve, [tb[1], tb[2]], [rb[m]],
                         lambda m=m: nc.vector.tensor_tensor(out=R[:, m, :], in0=tmp[:, 1, :], in1=tmp[:, 2, :],
                                                             op=ALU.add))
        for n in range(16):
            pset = gemm([wo[n]], [16], R_rhs, rb[0:16])
            xi = P.xr
            P.xr = (P.xr + 1) % NX
            xsb = xsrc_buf(tt, n)
            P.dma(P.sp, "x%d" % xi, [xsb] if xsb else [], [xbufs[xi]],
                  lambda xi=xi, n=n: nc.sync.dma_start(out=xring[:, xi, :],
                                                       in_=xT[n * 128:(n + 1) * 128, t0:t0 + TT]))
            oi = state["oi"]
            state["oi"] ^= 1
            P.op(P.dve, pbufs(pset) + [xbufs[xi]], [ob[oi]],
                 lambda pset=pset, xi=xi, oi=oi: nc.vector.tensor_tensor(
                     out=v3(ost[:, oi, :]), in0=pview(pset), in1=v3(xring[:, xi, :]), op=ALU.add))
            P.dma(P.sp, "x1w%d" % oi, [ob[oi]], [x1bufs[tt][n]],
                  lambda oi=oi, n=n: nc.sync.dma_start(out=x1T[n * 128:(n + 1) * 128, t0:t0 + TT],
                                                       in_=ost[:, oi, :]))
        emit_norm(P, C, lambda k: x1T[k * 128:(k + 1) * 128, t0:t0 + TT], lambda k: x1bufs[tt][k],
                  g_sb[:, 1, :], norm_to_h, TT)
        for j in range(KFF):
            pg = gemm([wf1[2 * j]], [16], h_rhs, hbufs)
            ti = j % 2
            P.op(P.act, pbufs(pg), [tb[ti]],
                 lambda pg=pg, ti=ti: nc.scalar.activation(out=v3(tmp[:, ti, :]), in_=pview(pg), func=AF.Silu))
            pu = gemm([wf1[2 * j + 1]], [16], h_rhs, hbufs)
            P.op(P.dve, pbufs(pu) + [tb[ti]], [rb[j]],
                 lambda pu=pu, ti=ti, j=j: nc.vector.tensor_tensor(
                     out=v3(R[:, j, :]), in0=pview(pu), in1=v3(tmp[:, ti, :]), op=ALU.mult))
        for n in range(16):
            pset = gemm([wf2[n, :, 0:16, :], wf2[n, :, 16:32, :], wf2[n, :, 32:44, :]], [16, 16, 12], R_rhs, rb)
            xi = P.xr
            P.xr = (P.xr + 1) % NX
            P.dma(P.sp, "x%d" % xi, [x1bufs[tt][n]], [xbufs[xi]],
                  lambda xi=xi, n=n: nc.sync.dma_start(out=xring[:, xi, :],
                                                       in_=x1T[n * 128:(n + 1) * 128, t0:t0 + TT]))
            oi = state["oi"]
            state["oi"] ^= 1
            P.op(P.dve, pbufs(pset) + [xbufs[xi]], [ob[oi]],
                 lambda pset=pset, xi=xi, oi=oi: nc.vector.tensor_tensor(
                     out=v3(ost[:, oi, :]), in0=pview(pset), in1=v3(xring[:, xi, :]), op=ALU.add))
            if final_norm:
                P.dma(P.sp, "x2w%d" % oi, [ob[oi]], [x2bufs[tt][n]],
                      lambda oi=oi, n=n: nc.sync.dma_start(out=x2T[n * 128:(n + 1) * 128, t0:t0 + TT],
                                                           in_=ost[:, oi, :]))
            else:
                P.dma(P.sp, "ow", [ob[oi]], [outbufs[tt][n]],
                      lambda oi=oi, n=n: nc.sync.dma_start(out=oT[n * 128:(n + 1) * 128, t0:t0 + TT],
                                                           in_=ost[:, oi, :]))
        if final_norm:
            def norm_to_out(k, x_ap, g_ap, rstd_ap, reads):
                oi = state["oi"]
                state["oi"] ^= 1
                P.op(P.dve, reads + [gbuf], [ob[oi]],
                     lambda: nc.vector.scalar_tensor_tensor(out=ost[:, oi, :], in0=x_ap, scalar=g_ap, in1=rstd_ap,
                                                            op0=ALU.mult, op1=ALU.mult))
                P.dma(P.sp, "ow", [ob[oi]], [outbufs[tt][k]],
                      lambda: nc.sync.dma_start(out=dst[k * 128:(k + 1) * 128, t0:t0 + TT], in_=ost[:, oi, :]))

            def norm_to_hloc(k, x_ap, g_ap, rstd_ap, reads):
                oi = state["oi"]
                state["oi"] ^= 1
                P.op(P.dve, reads + [gbuf], [hsb_[oi]],
                     lambda: nc.vector.scalar_tensor_tensor(out=hst[:, oi, :], in0=x_ap, scalar=g_ap, in1=rstd_ap,
                                                            op0=ALU.mult, op1=ALU.mult))
                P.dma(P.sp, "ow", [hsb_[oi]], [outbufs[tt][k]],
                      lambda: nc.sync.dma_start(out=hloc_view(dst, t0, TT, k), in_=hst[:, oi, :].rearrange(
                          "p (c t) -> p c t", c=TT // 256)))
            emit_norm(P, C, lambda k: x2T[k * 128:(k + 1) * 128, t0:t0 + TT], lambda k: x2bufs[tt][k],
                      g_sb[:, 2, :], norm_to_out if last else norm_to_hloc, TT)
    P.pop_scope()


def hloc_view(hloc, t0, TT, k):
    h3 = hloc.rearrange("(c f) t -> c f t", f=D)
    return h3[t0 // 256:(t0 + TT) // 256, k * 128:(k + 1) * 128, :].rearrange("c p t -> p c t")


def emit_N(P, ps_, pb_, NTOK, TT, xT, gm, hloc, outbufs):
    nc = P.nc
    P.push_scope()
    C = common_setup(P, TT, ps_, pb_)
    g_sb = P.sbuf("gN", [128, 16], F32)
    gbuf = Buf("gN")
    hst = P.sbuf("hstN", [128, 2, TT], BF16)
    hsb_ = [Buf("hstN%d" % i) for i in range(2)]
    P.dma(P.sp, "g", [], [gbuf], lambda: nc.sync.dma_start(out=g_sb[:], in_=gm))
    st = {"oi": 0}
    for tt in range(NTOK // TT):
        t0 = tt * TT

        def out_fn(k, x_ap, g_ap, rstd_ap, reads):
            oi = st["oi"]
            st["oi"] ^= 1
            P.op(P.dve, reads + [gbuf], [hsb_[oi]],
                 lambda: nc.vector.scalar_tensor_tensor(out=hst[:, oi, :], in0=x_ap, scalar=g_ap, in1=rstd_ap,
                                                        op0=ALU.mult, op1=ALU.mult))
            P.dma(P.sp, "ow", [hsb_[oi]], [outbufs[tt][k]],
                  lambda: nc.sync.dma_start(out=hloc_view(hloc, t0, TT, k), in_=hst[:, oi, :].rearrange(
                      "p (c t) -> p c t", c=TT // 256)))
        emit_norm(P, C, lambda k: xT[k * 128:(k + 1) * 128, t0:t0 + TT], lambda k: None, g_sb, out_fn, TT)
    P.pop_scope()


def arrange_w(w, kc):
    K, N = w.shape
    return np.ascontiguousarray(w.reshape(K // 128, 128, N // 128, 128).transpose(2, 1, 0, 3))


def prep_B_weights(l, w_in, w_branch_rnn, w_branch_attn, w_branch_pool, w_out, w_ffn_in, w_ffn_out,
                   g_mix, g_ffn, g_final):
    off = D_IN_GATES
    wg = arrange_w(w_in[l][:, off:off + 3 * D], 16)
    wbr = np.concatenate([arrange_w(w_branch_rnn[l], 8), arrange_w(w_branch_attn[l], 8),
                          arrange_w(w_branch_pool[l], 8)], axis=0)
    wo = arrange_w(w_out[l], 16)
    f1 = w_ffn_in[l]
    f1i = np.stack([f1[:, :DFF].reshape(D, KFF, 128), f1[:, DFF:].reshape(D, KFF, 128)], axis=2).reshape(D, 2 * DFF)
    wf1 = arrange_w(f1i, 16)
    wf2 = arrange_w(w_ffn_out[l], 44)
    gv = np.ascontiguousarray(np.stack([g_mix[l].reshape(16, 128).T, g_ffn[l].reshape(16, 128).T,
                                        g_final.reshape(16, 128).T], axis=1)).astype(np.float32)
    return dict(wg=wg, wbr=wbr, wo=wo, wf1=wf1, wf2=wf2, gv=gv)


D_IN_GATES = 1024 * 2 + 1024 * 3 + 8 + 1024


NPAR = 24
SQRT_DH = 11.313708498984761
GELU_C = 1.5957691216057308


def emit_A(P, ps, pb, S, d):
    nc = P.nc
    TT = 512
    NT = S // TT
    NKB = S // 128
    NTOK = S // 4
    hfull, hfbuf = d["hfull"], d["hfbuf"]
    w1d, w2d, wrgd, wpld, pad, ptabd, yo = d["w1"], d["w2"], d["wrg"], d["wpl"], d["pa"], d["ptab"], d["yo"]
    yob = [Buf("yo%d" % i) for i in range(8)]
    P.push_scope()
    ones_bf = P.sbuf("ones_bfA", [128, 128], BF16)
    onesb = Buf("onesA")
    P.op(P.dve, [], [onesb], lambda: nc.vector.memset(ones_bf[:], 1.0))

    def sb(name, shape, dt=F32):
        return P.sbuf(name, shape, dt), Buf(name)

    pa, pabuf = sb("pa_sb", [128, NPAR])
    ptab, ptabbuf = sb("ptab_sb", [128, 4, 16])
    cst, cstbuf = sb("cst", [128, 8])
    w1, w1buf = sb("w1_sb", [128, 16, 768], BF16)
    w2, w2buf = sb("w2_sb", [128, 16, 512], BF16)
    wrg, wrgbuf = sb("wrg_sb", [128, 4, 128], BF16)
    wpl, wplbuf = sb("wpl_sb", [128, 2, 256], BF16)
    hT = P.sbuf("hT", [128, 2, KD, TT], BF16)
    hbufs = [[Buf("h%d_%d" % (j, k)) for k in range(KD)] for j in range(2)]
    NTMP = 14
    tmp = P.sbuf("tmp", [128, NTMP, TT], F32)
    tb = [Buf("t%d" % i) for i in range(NTMP)]
    tmpb = P.sbuf("tmpb", [128, 6, TT], BF16)
    tbb = [Buf("tb%d" % i) for i in range(6)]
    xb = P.sbuf("xb", [128, 2, TT + 3], F32)
    xbb = [Buf("xb0"), Buf("xb1")]
    hcar = P.sbuf("hcar", [128, 2], F32)
    hcb = [Buf("hc0"), Buf("hc1")]
    ub = P.sbuf("ub", [128, 2, TT + 15], F32)
    ubb = [Buf("ub0"), Buf("ub1")]
    sw = P.sbuf("sw", [128, 4, TT + 15], F32)
    swb = [Buf("sw%d" % i) for i in range(4)]
    t16 = P.sbuf("t16", [128, 2, 16], F32)
    t16b = [Buf("t16a"), Buf("t16b")]
    KT, ktbuf_all = sb("KT", [128, S], BF16)
    ktb = [Buf("kt%d" % i) for i in range(NT)]
    V = P.sbuf("V", [128, NKB, 128], BF16)
    vb = [Buf("v%d" % i) for i in range(NT)]
    negF = P.sbuf("negF", [128, NKB], F32)
    nfb = [Buf("nf%d" % i) for i in range(NT)]
    bm, bmbuf = sb("bm", [128, NKB])
    ct, ctbuf = sb("ct", [128, 1])
    fc = P.sbuf("fc", [128, 2], F32)
    fcb = [Buf("fc0"), Buf("fc1")]
    one32, one32b = sb("one32", [128, 128], F32)
    onesT, onesTb = sb("onesT", [128, TT], F32)
    Et = P.sbuf("Et", [128, 5, TT], BF16)
    etb = [Buf("et%d" % i) for i in range(5)]
    negcol, ncbuf = sb("negcol", [128, 4])
    tri, trib = sb("tri", [128, 128], BF16)
    pT = P.sbuf("pT", [128, 3, TT], BF16)
    ptb = [Buf("pt%d" % i) for i in range(3)]
    state = {"bank": 0, "pt": 0, "yo": 0, "tmp": 0}

    P.dma(P.sp, "c0a", [], [pabuf], lambda: nc.sync.dma_start(out=pa[:], in_=pad))
    P.dma(P.sp, "c0b", [], [ptabbuf], lambda: nc.sync.dma_start(out=ptab[:], in_=ptabd))
    for k in range(KD):
        P.dma(P.pool, "cw1", [], [w1buf],
              lambda k=k: nc.gpsimd.dma_start(out=w1[:, k, :], in_=w1d[:, k, :], max_dma_last_dim=4096))
    P.dma(P.pool, "cwr", [], [wrgbuf], lambda: nc.gpsimd.dma_start(out=wrg[:], in_=wrgd, max_dma_last_dim=4096))
    P.dma(P.pool, "cwp", [], [wplbuf], lambda: nc.gpsimd.dma_start(out=wpl[:], in_=wpld, max_dma_last_dim=4096))
    P.op(P.dve, [], [one32b], lambda: nc.vector.memset(one32[:], 1.0))
    P.op(P.dve, [], [one32b], lambda: nc.vector.memset(onesT[:], 1.0))
    P.op(P.dve, [pabuf], [pabuf],
         lambda: nc.vector.tensor_scalar(out=pa[:, 18:20], in0=pa[:, 18:20], scalar1=-1.0, scalar2=None,
                                         op0=ALU.mult))
    P.op(P.dve, [], [xbb[0], xbb[1]], lambda: nc.vector.memset(xb[:], 0.0))
    P.op(P.dve, [], [ubb[0], ubb[1]], lambda: nc.vector.memset(ub[:], 0.0))
    P.op(P.dve, [], [hcb[0], hcb[1]], lambda: nc.vector.memset(hcar[:], 0.0))
    P.op(P.dve, [], swb, lambda: nc.vector.memset(sw[:], 0.0))
    P.op(P.dve, [], [ncbuf], lambda: nc.vector.memset(negcol[:], 0.0))
    P.op(P.dve, [], [trib], lambda: nc.vector.memset(tri[:], 1.0))
    P.op(P.pool, [trib], [trib],
         lambda: nc.gpsimd.affine_select(out=tri[:], in_=tri[:], pattern=[[1, 128]], compare_op=ALU.is_ge,
                                         fill=0.0, base=0, channel_multiplier=-1))
    P.op(P.act, [pabuf], [cstbuf],
         lambda: nc.scalar.activation(out=cst[:, 0:2], in_=pa[:, 14:16], func=AF.Exp, scale=-1.0))
    P.op(P.act, [cstbuf], [cstbuf],
         lambda: nc.scalar.activation(out=cst[:, 0:2], in_=cst[:, 0:2], func=AF.Ln, bias=one32[:, 0:1]))
    P.op(P.dve, [cstbuf], [cstbuf],
         lambda: nc.vector.tensor_scalar(out=cst[:, 2:4], in0=cst[:, 0:2], scalar1=-16.0, scalar2=None,
                                         op0=ALU.mult))
    P.op(P.dve, [cstbuf], [cstbuf],
         lambda: nc.vector.tensor_scalar(out=cst[:, 0:2], in0=cst[:, 0:2], scalar1=-8.0, scalar2=None,
                                         op0=ALU.mult))

    def load_w2(hd):
        for k in range(KD):
            P.dma(P.pool, "cw2", [], [w2buf],
                  lambda k=k: nc.gpsimd.dma_start(out=w2[:, k, :], in_=w2d[hd * 128:(hd + 1) * 128, k, :],
                                                  max_dma_last_dim=4096))
    load_w2(0)
    d["after_setup"]()

    def bank():
        b = state["bank"]
        state["bank"] = (b + 1) % 8
        return b

    def T():
        i = state["tmp"]
        state["tmp"] = (i + 1) % NTMP
        return i

    def gemm_chunk(wt, wbuf, c0, hj, bk):
        def emit():
            ins = None
            for k in range(KD):
                ins = nc.tensor.matmul(ps[:, bk, :], lhsT=wt[:, k, c0:c0 + 128], rhs=hT[:, hj, k, :],
                                       start=(k == 0), stop=(k == KD - 1))
            return ins
        P.op(P.pe, [wbuf] + hbufs[hj], [pb[bk]], emit)

    def store_y(which, row0, src_ap, src_buf, t0):
        i = state["yo"]
        state["yo"] = (i + 1) % 8
        P.dma(P.sp, "yo%d" % i, [src_buf], [yob[i]],
              lambda: nc.sync.dma_start(
                  out=yo[(t0 // NTOK) * 768 + which * 256 + row0:(t0 // NTOK) * 768 + which * 256 + row0 + 128,
                         (t0 % NTOK):(t0 % NTOK) + TT], in_=src_ap))

    hf3 = hfull.rearrange("(c q) t -> c q t", q=4 * D)

    def load_h(it, hj):
        r, off = divmod(it * TT, NTOK)
        c = off // 256
        for k in range(KD):
            P.dma(P.sp, "hl%d" % hj, [hfbuf[c], hfbuf[c + 1]], [hbufs[hj][k]],
                  lambda k=k: nc.sync.dma_start(
                      out=hT[:, hj, k, :].rearrange("p (c t) -> p c t", c=2),
                      in_=hf3[c:c + 2, r * D + k * 128:r * D + (k + 1) * 128, :].rearrange("c p t -> p c t")))

    for it in range(NT):
        t0 = it * TT
        hj = it % 2
        if it == 0:
            load_h(0, 0)
        if it + 1 < NT:
            load_h(it + 1, (it + 1) % 2)

        def lru_gen(blk):
            bk = 3 * blk
            gemm_chunk(w1, w1buf, blk * 128, hj, bk)
            yield
            P.op(P.act, [pb[bk]], [xbb[blk]],
                 lambda bk=bk: nc.scalar.activation(out=xb[:, blk, 3:TT + 3], in_=ps[:, bk, :], func=AF.Copy))
            yield
            u, r, ig, a, a2, z = [blk * 6 + i_ for i_ in range(6)]
            P.op(P.act, [xbb[blk], pabuf], [tb[u]],
                 lambda: nc.scalar.activation(out=tmp[:, u, :], in_=xb[:, blk, 3:TT + 3], func=AF.Identity,
                                              scale=pa[:, blk * 4 + 3:blk * 4 + 4], bias=pa[:, 8 + blk:9 + blk]))
            yield
            for tap in (2, 1, 0):
                P.op(P.dve, [xbb[blk], pabuf, tb[u]], [tb[u]],
                     lambda tap=tap: nc.vector.scalar_tensor_tensor(
                         out=tmp[:, u, :], in0=xb[:, blk, tap:tap + TT], scalar=pa[:, blk * 4 + tap:blk * 4 + tap + 1],
                         in1=tmp[:, u, :], op0=ALU.mult, op1=ALU.add))
                yield
            P.op(P.dve, [xbb[blk]], [xbb[blk]],
                 lambda: nc.vector.tensor_copy(out=xb[:, blk, 0:3], in_=xb[:, blk, TT:TT + 3]))
            yield
            ubf = blk
            P.op(P.dve, [tb[u]], [tbb[ubf]], lambda: nc.vector.tensor_copy(out=tmpb[:, ubf, :], in_=tmp[:, u, :]))
            yield
            bkr, bki = 3 * blk + 1, 3 * blk + 2
            P.op(P.pe, [tbb[ubf], wrgbuf], [pb[bkr]],
                 lambda: nc.tensor.matmul(ps[:, bkr, :], lhsT=wrg[:, blk * 2, :], rhs=tmpb[:, ubf, :],
                                          start=True, stop=True))
            yield
            P.op(P.pe, [tbb[ubf], wrgbuf], [pb[bki]],
                 lambda: nc.tensor.matmul(ps[:, bki, :], lhsT=wrg[:, blk * 2 + 1, :], rhs=tmpb[:, ubf, :],
                                          start=True, stop=True))
            yield
            P.op(P.act, [pb[bkr], pabuf], [tb[r]],
                 lambda: nc.scalar.activation(out=tmp[:, r, :], in_=ps[:, bkr, :], func=AF.Sigmoid,
                                              bias=pa[:, 10 + blk:11 + blk]))
            yield
            P.op(P.act, [pb[bki], pabuf], [tb[ig]],
                 lambda: nc.scalar.activation(out=tmp[:, ig, :], in_=ps[:, bki, :], func=AF.Sigmoid,
                                              bias=pa[:, 12 + blk:13 + blk]))
            yield
            P.op(P.act, [tb[r], cstbuf], [tb[a]],
                 lambda: nc.scalar.activation(out=tmp[:, a, :], in_=tmp[:, r, :], func=AF.Exp,
                                              scale=cst[:, blk:blk + 1]))
            yield
            P.op(P.act, [tb[r], cstbuf], [tb[a2]],
                 lambda: nc.scalar.activation(out=tmp[:, a2, :], in_=tmp[:, r, :], func=AF.Exp,
                                              scale=cst[:, 2 + blk:3 + blk]))
            yield
            P.op(P.dve, [tb[a2]], [tb[a2]],
                 lambda: nc.vector.tensor_scalar(out=tmp[:, a2, :], in0=tmp[:, a2, :], scalar1=-1.0, scalar2=1.0,
                                                 op0=ALU.mult, op1=ALU.add))
            yield
            P.op(P.act, [tb[a2]], [tb[a2]],
                 lambda: nc.scalar.activation(out=tmp[:, a2, :], in_=tmp[:, a2, :], func=AF.Sqrt))
            yield
            P.op(P.dve, [tb[ig], tb[u]], [tb[ig]],
                 lambda: nc.vector.tensor_tensor(out=tmp[:, ig, :], in0=tmp[:, ig, :], in1=tmp[:, u, :], op=ALU.mult))
            yield
            P.op(P.dve, [tb[ig], tb[a2]], [tb[ig]],
                 lambda: nc.vector.tensor_tensor(out=tmp[:, ig, :], in0=tmp[:, ig, :], in1=tmp[:, a2, :],
                                                 op=ALU.mult))
            yield
            P.op(P.dve, [tb[a], tb[ig], hcb[blk]], [tb[r]],
                 lambda: nc.vector.tensor_tensor_scan(out=tmp[:, r, :], data0=tmp[:, a, :], data1=tmp[:, ig, :],
                                                      initial=hcar[:, blk:blk + 1], op0=ALU.mult, op1=ALU.add))
            yield
            P.op(P.dve, [tb[r]], [hcb[blk]],
                 lambda: nc.vector.tensor_copy(out=hcar[:, blk:blk + 1], in_=tmp[:, r, TT - 1:TT]))
            yield
            bky = 3 * blk
            gemm_chunk(w1, w1buf, 256 + blk * 128, hj, bky)
            yield
            P.op(P.act, [pb[bky]], [tb[z]],
                 lambda: nc.scalar.activation(out=tmp[:, z, :], in_=ps[:, bky, :], func=AF.Square))
            yield
            P.op(P.dve, [tb[z]], [tb[z]],
                 lambda: nc.vector.tensor_scalar(out=tmp[:, z, :], in0=tmp[:, z, :], scalar1=0.044715, scalar2=1.0,
                                                 op0=ALU.mult, op1=ALU.add))
            yield
            P.op(P.dve, [tb[z], pb[bky]], [tb[z]],
                 lambda: nc.vector.tensor_tensor(out=tmp[:, z, :], in0=ps[:, bky, :], in1=tmp[:, z, :], op=ALU.mult))
            yield
            P.op(P.act, [tb[z]], [tb[z]],
                 lambda: nc.scalar.activation(out=tmp[:, z, :], in_=tmp[:, z, :], func=AF.Sigmoid, scale=GELU_C))
            yield
            P.op(P.dve, [tb[z], pb[bky]], [tb[z]],
                 lambda: nc.vector.tensor_tensor(out=tmp[:, z, :], in0=ps[:, bky, :], in1=tmp[:, z, :], op=ALU.mult))
            yield
            yb_ = 2 + blk
            P.op(P.dve, [tb[z], tb[r]], [tbb[yb_]],
                 lambda: nc.vector.tensor_tensor(out=tmpb[:, yb_, :], in0=tmp[:, z, :], in1=tmp[:, r, :], op=ALU.mult))
            yield
            store_y(0, blk * 128, tmpb[:, yb_, :], tbb[yb_], t0)
            yield

        def pool_gen(cc):
            bk = 6
            gemm_chunk(w1, w1buf, 512 + cc * 128, hj, bk)
            yield
            P.op(P.act, [pb[bk]], [ubb[cc]],
                 lambda bk=bk: nc.scalar.activation(out=ub[:, cc, 15:TT + 15], in_=ps[:, bk, :], func=AF.Copy))
            yield
            W_ = TT + 15
            P.op(P.dve, [ubb[cc]], [swb[0]],
                 lambda: nc.vector.tensor_tensor(out=sw[:, 0, 1:W_], in0=ub[:, cc, 1:W_], in1=ub[:, cc, 0:W_ - 1],
                                                 op=ALU.add))
            yield
            P.op(P.dve, [swb[0]], [swb[1]],
                 lambda: nc.vector.tensor_tensor(out=sw[:, 1, 3:W_], in0=sw[:, 0, 3:W_], in1=sw[:, 0, 1:W_ - 2],
                                                 op=ALU.add))
            yield
            P.op(P.dve, [swb[1]], [swb[2]],
                 lambda: nc.vector.tensor_tensor(out=sw[:, 2, 7:W_], in0=sw[:, 1, 7:W_], in1=sw[:, 1, 3:W_ - 4],
                                                 op=ALU.add))
            yield
            P.op(P.dve, [swb[2]], [swb[3]],
                 lambda: nc.vector.tensor_tensor(out=sw[:, 3, 15:W_], in0=sw[:, 2, 15:W_], in1=sw[:, 2, 7:W_ - 8],
                                                 op=ALU.add))
            yield
            rv = 12 + cc
            P.op(P.dve, [swb[0], ubb[cc], pabuf], [tb[rv]],
                 lambda: nc.vector.scalar_tensor_tensor(out=tmp[:, rv, :], in0=sw[:, 0, 15:W_], scalar=pa[:, 20:21],
                                                        in1=ub[:, cc, 15:W_], op0=ALU.mult, op1=ALU.subtract))
            yield
            for wi in (1, 2):
                P.op(P.dve, [swb[wi], tb[rv], pabuf], [tb[rv]],
                     lambda wi=wi: nc.vector.scalar_tensor_tensor(out=tmp[:, rv, :], in0=sw[:, wi, 15:W_],
                                                                  scalar=pa[:, 20 + wi:21 + wi], in1=tmp[:, rv, :],
                                                                  op0=ALU.mult, op1=ALU.add))
                yield
            pl = 4 + cc
            P.op(P.dve, [swb[3], tb[rv], pabuf], [tbb[pl]],
                 lambda: nc.vector.scalar_tensor_tensor(out=tmpb[:, pl, :], in0=sw[:, 3, 15:W_], scalar=pa[:, 23:24],
                                                        in1=tmp[:, rv, :], op0=ALU.mult, op1=ALU.add))
            yield
            if it == 0:
                P.op(P.dve, [swb[0], ptabbuf], [t16b[0]],
                     lambda: nc.vector.tensor_tensor(out=t16[:, 0, :], in0=sw[:, 0, 15:31], in1=ptab[:, 0, :],
                                                     op=ALU.mult))
                yield
                for wi in (1, 2, 3):
                    P.op(P.dve, [swb[wi], ptabbuf], [t16b[1]],
                         lambda wi=wi: nc.vector.tensor_tensor(out=t16[:, 1, :], in0=sw[:, wi, 15:31],
                                                               in1=ptab[:, wi, :], op=ALU.mult))
                    yield
                    P.op(P.dve, [t16b[0], t16b[1]], [t16b[0]],
                         lambda: nc.vector.tensor_tensor(out=t16[:, 0, :], in0=t16[:, 0, :], in1=t16[:, 1, :],
                                                         op=ALU.add))
                    yield
                P.op(P.dve, [t16b[0], ubb[cc]], [tbb[pl]],
                     lambda: nc.vector.tensor_tensor(out=tmpb[:, pl, 0:16], in0=t16[:, 0, :], in1=ub[:, cc, 15:31],
                                                     op=ALU.subtract))
                yield
            P.op(P.dve, [ubb[cc]], [ubb[cc]],
                 lambda: nc.vector.tensor_copy(out=ub[:, cc, 0:15], in_=ub[:, cc, TT:TT + 15]))
            yield
        def pool_both():
            yield from pool_gen(0)
            yield from pool_gen(1)
        gens = [lru_gen(0), lru_gen(1), pool_both()]
        while gens:
            for g_ in list(gens):
                try:
                    next(g_)
                except StopIteration:
                    gens.remove(g_)
        for dc in range(2):
            bk = 6 + dc
            def emit(bk=bk, dc=dc):
                ins = None
                for cc in range(2):
                    ins = nc.tensor.matmul(ps[:, bk, :], lhsT=wpl[:, cc, dc * 128:(dc + 1) * 128],
                                           rhs=tmpb[:, 4 + cc, :], start=(cc == 0), stop=(cc == 1))
                return ins
            P.op(P.pe, [tbb[4], tbb[5], wplbuf], [pb[bk]], emit)
            yi = dc
            pti = state["pt"]
            state["pt"] = (pti + 1) % 3
            P.op(P.act, [pb[bk], pabuf], [ptb[pti]],
                 lambda bk=bk, pti=pti, dc=dc: nc.scalar.activation(out=pT[:, pti, :], in_=ps[:, bk, :],
                                                                    func=AF.Identity,
                                                                    scale=pa[:, 16 + dc:17 + dc]))
            store_y(2, dc * 128, pT[:, pti, :], ptb[pti], t0)

    d["after_part"]((0, 1, 4, 5))
    SB0, SB1, OB, SMB, QKB, VB, FB, SMALL = 0, 1, 2, 3, 4, 5, 6, 7
    scale = 1.0 / SQRT_DH
    for hd in range(2):
        if hd == 1:
            load_w2(1)
            d["after_part"]((2,))
        P.op(P.dve, [], [fcb[0]], lambda: nc.vector.memset(fc[:, 0:1], 0.0))
        for it in range(NT):
            t0 = it * TT
            hj = it % 2
            if it == 0:
                load_h(0, 0)
            if it + 1 < NT:
                load_h(it + 1, (it + 1) % 2)
            def emit_f():
                ins = None
                for k in range(KD):
                    ins = nc.tensor.matmul(ps[:, FB, :], lhsT=w2[:, k, 384:512], rhs=hT[:, hj, k, :],
                                           start=(k == 0), stop=(k == KD - 1))
                return ins
            P.op(P.pe, [w2buf] + hbufs[hj], [pb[FB]], emit_f)
            e, fa, fr = T(), T(), T()
            P.op(P.act, [pb[FB], pabuf], [tb[e]],
                 lambda: nc.scalar.activation(out=tmp[:, e, :], in_=ps[:, FB, :], func=AF.Exp, scale=-1.0,
                                              bias=pa[:, 18 + hd:19 + hd]))
            P.op(P.act, [tb[e], one32b], [tb[e]],
                 lambda: nc.scalar.activation(out=tmp[:, e, :], in_=tmp[:, e, :], func=AF.Ln, bias=one32[:, 0:1]))
            fi, fo = it % 2, (it + 1) % 2
            P.op(P.dve, [tb[e], one32b, fcb[fi]], [tb[fa]],
                 lambda: nc.vector.tensor_tensor_scan(out=tmp[:, fa, :], data0=onesT[:, :], data1=tmp[:, e, :],
                                                      initial=fc[:, fi:fi + 1], op0=ALU.mult, op1=ALU.subtract))
            P.op(P.dve, [tb[fa]], [fcb[fo]],
                 lambda: nc.vector.tensor_copy(out=fc[:, fo:fo + 1], in_=tmp[:, fa, TT - 1:TT]))
            P.op(P.dve, [tb[fa], fcb[fi]], [tb[fr]],
                 lambda: nc.vector.tensor_scalar(out=tmp[:, fr, :], in0=tmp[:, fa, :], scalar1=fc[:, fi:fi + 1],
                                                 scalar2=None, op0=ALU.subtract))
            for j in range(4):
                P.op(P.dve, [tb[fr]], [ncbuf],
                     lambda j=j: nc.vector.tensor_scalar(out=negcol[:, j:j + 1],
                                                         in0=tmp[:, fr, j * 128 + 63:j * 128 + 64],
                                                         scalar1=-1.0, scalar2=None, op0=ALU.mult))
            P.op(P.act, [tb[fr]], [etb[0]],
                 lambda: nc.scalar.activation(out=Et[:, 0, :], in_=tmp[:, fr, :], func=AF.Exp))
            for j in range(4):
                P.op(P.act, [tb[fr], ncbuf], [etb[1 + j]],
                     lambda j=j: nc.scalar.activation(out=Et[:, 1 + j, j * 128:TT], in_=tmp[:, fr, j * 128:TT],
                                                      func=AF.Exp, bias=negcol[:, j:j + 1]))
                P.op(P.dve, [etb[1 + j], trib], [etb[1 + j]],
                     lambda j=j: nc.vector.tensor_tensor(out=Et[:, 1 + j, j * 128:(j + 1) * 128],
                                                         in0=Et[:, 1 + j, j * 128:(j + 1) * 128], in1=tri[:, :],
                                                         op=ALU.mult))
            qi = 0
            gemm_chunk(w2, w2buf, 0, hj, QKB)
            P.op(P.act, [pb[QKB]], [tbb[qi]],
                 lambda: nc.scalar.activation(out=tmpb[:, qi, :], in_=ps[:, QKB, :], func=AF.Copy))
            gemm_chunk(w2, w2buf, 128, hj, QKB)
            P.op(P.act, [pb[QKB]], [ktb[it]],
                 lambda: nc.scalar.activation(out=KT[:, t0:t0 + TT], in_=ps[:, QKB, :], func=AF.Copy))
            def emit_v():
                ins = None
                for tbk in range(4):
                    for k in range(KD):
                        ins = nc.tensor.matmul(ps[:, VB, tbk * 128:(tbk + 1) * 128],
                                               lhsT=hT[:, hj, k, tbk * 128:(tbk + 1) * 128], rhs=w2[:, k, 256:384],
                                               start=(k == 0), stop=(k == KD - 1))
                return ins
            P.op(P.pe, [w2buf] + hbufs[hj], [pb[VB]], emit_v)
            P.op(P.act, [pb[VB]], [vb[it]],
                 lambda: nc.scalar.activation(out=V[:, it * 4:(it + 1) * 4, :],
                                              in_=ps[:, VB, :].rearrange("p (a b) -> p a b", a=4), func=AF.Copy))
            def emit_small():
                ins = None
                for tbk in range(4):
                    ins = nc.tensor.matmul(ps[:, SMALL, tbk:tbk + 1], lhsT=tmp[0:1, fa, tbk * 128:(tbk + 1) * 128],
                                           rhs=one32[0:1, 0:1], start=True, stop=True)
                ins = nc.tensor.matmul(ps[:, SMALL, 4:5], lhsT=one32[0:1, :], rhs=fc[0:1, fi:fi + 1],
                                       start=True, stop=True)
                return ins
            P.op(P.pe, [tb[fa], one32b, fcb[fi]], [pb[SMALL]], emit_small)
            P.op(P.dve, [pb[SMALL]], [nfb[it]],
                 lambda: nc.vector.tensor_scalar(out=negF[:, it * 4:(it + 1) * 4], in0=ps[:, SMALL, 0:4],
                                                 scalar1=-1.0, scalar2=None, op0=ALU.mult))
            P.op(P.dve, [pb[SMALL]], [ctbuf], lambda: nc.vector.tensor_copy(out=ct[:, :], in_=ps[:, SMALL, 4:5]))
            nkb = (it + 1) * 4
            P.op(P.dve, nfb[0:it + 1] + [ctbuf], [bmbuf],
                 lambda: nc.vector.tensor_scalar(out=bm[:, 0:nkb], in0=negF[:, 0:nkb], scalar1=ct[:, 0:1],
                                                 scalar2=None, op0=ALU.add))
            P.op(P.dve, [bmbuf, ncbuf], [bmbuf],
                 lambda: nc.vector.tensor_tensor(out=bm[:, it * 4:it * 4 + 4], in0=bm[:, it * 4:it * 4 + 4],
                                                 in1=negcol[:, 0:4], op=ALU.subtract))
            SBANKS = [SB0, SB1, QKB, VB]
            PRE = 3

            def issue_s(kb):
                j = kb - it * 4
                c0 = max(j, 0) * 128
                sbk = SBANKS[kb % 4]

                def emit_s():
                    return nc.tensor.matmul(ps[:, sbk, c0:TT], lhsT=KT[:, kb * 128:(kb + 1) * 128],
                                            rhs=tmpb[:, qi, c0:TT], start=True, stop=True)
                P.op(P.pe, [ktb[kb // 4], tbb[qi]], [pb[sbk]], emit_s)

            for kb in range(min(PRE, nkb)):
                issue_s(kb)
            for kb in range(nkb):
                if kb + PRE < nkb:
                    issue_s(kb + PRE)
                j = kb - it * 4
                c0 = max(j, 0) * 128
                sbk = SBANKS[kb % 4]
                pti = state["pt"]
                state["pt"] = (pti + 1) % 3
                P.op(P.act, [pb[sbk], bmbuf], [ptb[pti]],
                     lambda: nc.scalar.activation(out=pT[:, pti, c0:TT], in_=ps[:, sbk, c0:TT], func=AF.Exp,
                                                  scale=scale, bias=bm[:, kb:kb + 1]))
                ei = 0 if j < 0 else 1 + j
                P.op(P.dve, [ptb[pti], etb[ei]], [ptb[pti]],
                     lambda: nc.vector.tensor_tensor(out=pT[:, pti, c0:TT], in0=pT[:, pti, c0:TT],
                                                     in1=Et[:, ei, c0:TT], op=ALU.mult))

                def emit_o():
                    nc.tensor.matmul(ps[:, OB, c0:TT], lhsT=V[:, kb, :], rhs=pT[:, pti, c0:TT],
                                     start=(kb == 0), stop=(kb == nkb - 1))
                    return nc.tensor.matmul(ps[:, SMB, c0:TT], lhsT=ones_bf[:, :], rhs=pT[:, pti, c0:TT],
                                            start=(kb == 0), stop=(kb == nkb - 1))
                P.op(P.pe, [vb[kb // 4], ptb[pti], onesb], [pb[OB], pb[SMB]], emit_o)
            rc = T()
            P.op(P.dve, [pb[SMB]], [tb[rc]], lambda: nc.vector.reciprocal(out=tmp[:, rc, :], in_=ps[:, SMB, :]))
            yi = 1 + it % 2
            P.op(P.dve, [pb[OB], tb[rc]], [tbb[yi]],
                 lambda: nc.vector.tensor_tensor(out=tmpb[:, yi, :], in0=ps[:, OB, :], in1=tmp[:, rc, :], op=ALU.mult))
            store_y(1, hd * 128, tmpb[:, yi, :], tbb[yi], t0)
    d["after_part"]((3,))
    P.pop_scope()


POOL_WINDOWS = (2, 4, 8, 16)


def prep_A_inputs(l, b, j, xT_b, g_mix, w_in, b_forget, conv_w, conv_b, w_rg, b_rg, w_ig, b_ig, lru_lambda,
                  w_pool, pool_scale):
    W = w_in[l]

    def arr(cols):
        return np.ascontiguousarray(cols.reshape(16, 128, cols.shape[1]).transpose(1, 0, 2))
    w1 = np.concatenate([W[:, 2 * j * 128:(2 * j + 2) * 128], W[:, 1024 + 2 * j * 128:1024 + (2 * j + 2) * 128],
                         W[:, 5128 + j * 256:5128 + (j + 1) * 256]], axis=1)
    w2 = []
    for hd in range(2):
        h = 2 * j + hd
        w2.append(arr(np.concatenate([W[:, 2048 + h * 128:2048 + (h + 1) * 128],
                                      W[:, 3072 + h * 128:3072 + (h + 1) * 128],
                                      W[:, 4096 + h * 128:4096 + (h + 1) * 128],
                                      np.repeat(W[:, 5120 + h:5121 + h], 128, axis=1)], axis=1)))
    wrg = np.stack([w_rg[l][2 * j], w_ig[l][2 * j], w_rg[l][2 * j + 1], w_ig[l][2 * j + 1]], axis=1)
    wpl = np.ascontiguousarray(w_pool[l][j].reshape(2, 128, 256).transpose(1, 0, 2))
    pa = np.zeros((128, NPAR), np.float32)
    for blk in range(2):
        ch = slice((2 * j + blk) * 128, (2 * j + blk + 1) * 128)
        for tap in range(4):
            pa[:, blk * 4 + tap] = conv_w[l][tap, ch]
        pa[:, 8 + blk] = conv_b[l][ch]
        pa[:, 10 + blk] = b_rg[l][ch]
        pa[:, 12 + blk] = b_ig[l][ch]
        pa[:, 14 + blk] = lru_lambda[l][ch]
        pa[:, 16 + blk] = pool_scale[l][j * 256 + blk * 128:j * 256 + (blk + 1) * 128]
        pa[:, 18 + blk] = b_forget[l][2 * j + blk]
    ptab = np.zeros((128, 4, 16), np.float32)
    for wi, w in enumerate(POOL_WINDOWS):
        if wi == j:
            pa[:, 20 + wi] = 1.0 / w
            ptab[:, wi, :] = 1.0 / np.minimum(np.arange(1, 17), w)
    return dict(w1=arr(w1), w2=np.stack(w2),
                wrg=np.ascontiguousarray(wrg), wpl=wpl, pa=pa, ptab=ptab)


GROUPS = [[0, 1, 2, 3], [4, 5, 6, 7]]


def build_fused(S, B_TT):
    P = Prog()
    nc = P.nc
    NTOK = S // 4
    ntt = NTOK // B_TT
    ps = P.psum("ps", [128, 8, 512], F32)
    pb = [Buf("p%d" % i) for i in range(8)]
    xT = P.dram("xT", [D, NTOK], F32, "ExternalInput")
    gm0 = P.dram("gm0", [128, 16], F32, "ExternalInput")
    ptab = P.dram("ptab", [128, 4, 16], F32, "ExternalInput")
    oT = P.dram("oT", [D, NTOK], F32, "ExternalOutput")
    A_in, B_in = [], []
    for l in range(2):
        A_in.append(dict(
            w1=P.dram("w1_%d" % l, [128, 16, 768], F32, "ExternalInput"),
            w2=P.dram("w2_%d" % l, [256, 16, 512], F32, "ExternalInput"),
            wrg=P.dram("wrg_%d" % l, [128, 4, 128], F32, "ExternalInput"),
            wpl=P.dram("wpl_%d" % l, [128, 2, 256], F32, "ExternalInput"),
            pa=P.dram("pa_%d" % l, [128, NPAR], F32, "ExternalInput"),
            ptab=ptab))
        B_in.append(dict(
            wg=P.dram("wg_%d" % l, [48, 128, 16, 128], F32, "ExternalInput"),
            wbr=P.dram("wbr_%d" % l, [48, 128, 8, 128], F32, "ExternalInput"),
            wo=P.dram("wo_%d" % l, [16, 128, 16, 128], F32, "ExternalInput"),
            wf1=P.dram("wf1_%d" % l, [88, 128, 16, 128], F32, "ExternalInput"),
            wf2=P.dram("wf2_%d" % l, [16, 128, 44, 128], F32, "ExternalInput"),
            gv=P.dram("gv_%d" % l, [128, 3, 16], F32, "ExternalInput")))
    NSR = NTOK // 256
    hloc = [P.dram("hloc%d" % l, [NSR * D, 256], BF16) for l in range(2)]
    hfull = [P.dram("hfull%d" % l, [NSR * 4 * D, 256], BF16) for l in range(2)]
    yo = [P.dram("yo%d" % l, [4 * 768, NTOK], BF16) for l in range(2)]
    yfull = [P.dram("yfull%d" % l, [4 * 4 * 768, NTOK], BF16) for l in range(2)]
    x1T = [P.dram("x1T%d" % l, [D, NTOK], F32) for l in range(2)]
    x2T = [P.dram("x2T%d" % l, [D, NTOK], F32) for l in range(2)]
    ysl = [P.dram("ysl%d" % l, [4 * 768, NTOK], BF16) for l in range(2)]
    qreg = nc.sync.partition_id() % 4

    def mk():
        return [[Buf("b") for _ in range(16)] for _ in range(ntt)]

    emit_N(P, ps, pb, NTOK, B_TT, xT, gm0, hloc[0], mk())
    x2prev = None
    outb = None
    for l in range(2):
        hfbuf = [Buf("hf%d_%d" % (l, c)) for c in range(NSR)]

        def after_setup(l=l, hfbuf=hfbuf):
            for c in range(NSR):
                P.cc([], [hfbuf[c]], lambda c=c: nc.gpsimd.collective_compute(
                    "AllGather", ALU.bypass, replica_groups=GROUPS, ins=[hloc[l][c * D:(c + 1) * D, :]],
                    outs=[hfull[l][c * 4 * D:(c + 1) * 4 * D, :]]))
        ybuf = Buf("yf%d" % l)

        def after_part(rbs, l=l, ybuf=ybuf):
            P._wait(P.pool, {S_.sem: S_.n for nm, S_ in P.dmaq.items() if nm.startswith("yo") and S_.n > 0})
            for q in range(4):
                for rb in rbs:
                    qb = q * 6 + rb
                    P.cc([], [ybuf], lambda qb=qb: nc.gpsimd.collective_compute(
                        "AllGather", ALU.bypass, replica_groups=GROUPS, ins=[yo[l][qb * 128:(qb + 1) * 128, :]],
                        outs=[yfull[l][qb * 512:(qb + 1) * 512, :]]))
        emit_A(P, ps, pb, S, dict(A_in[l], hfull=hfull[l], hfbuf=hfbuf, yo=yo[l], after_part=after_part,
                                 after_setup=after_setup))
        yslb = Buf("ysl%d" % l)
        for r in range(4):
            P.dma(P.sp, "ysl", [ybuf], [yslb],
                  lambda r=r: nc.sync.dma_start(out=ysl[l][r * 768:(r + 1) * 768, :],
                                                in_=yfull[l][ds(qreg * 3072 + r * 768, 768), :]))
        x2b = mk()
        outb = mk()
        dd = dict(B_in[l], xsrc=(xT if l == 0 else x2T[0]),
                  xsrc_buf=((lambda tt, k: None) if l == 0 else (lambda tt, k, xp=x2prev: xp[tt][k])),
                  yfull=ysl[l], ybuf=yslb, x1T=x1T[l], x2T=x2T[l], dst=(hloc[1] if l == 0 else oT),
                  x2bufs=x2b, outbufs=outb)
        emit_B(P, ps, pb, NTOK, B_TT, dd, last=(l == 1))
        x2prev = x2b
    P.finish([b for row in outb for b in row])
    return P


_CACHE = {}
B_TT_FULL = 1024


def prep_inputs(x, g_mix, w_in, b_forget, conv_w, conv_b, w_rg, b_rg, w_ig, b_ig, lru_lambda, w_pool, pool_scale,
                w_branch_rnn, w_branch_attn, w_branch_pool, w_out, g_ffn, w_ffn_in, w_ffn_out, g_final):
    nb, S, _ = x.shape
    NTOK = S // 4
    BW = []
    for l in range(2):
        g_next = g_mix[1] if l == 0 else g_final
        W = prep_B_weights(l, w_in, w_branch_rnn, w_branch_attn, w_branch_pool, w_out, w_ffn_in, w_ffn_out,
                           g_mix, g_ffn, g_next)
        BW.append({"%s_%d" % (k, l): v for k, v in W.items()})
    in_maps = []
    for c in range(NCORES):
        b, j = c // 4, c % 4
        m = {"xT": np.ascontiguousarray(x[b, j * NTOK:(j + 1) * NTOK, :].T),
             "gm0": np.ascontiguousarray(g_mix[0].reshape(16, 128).T)}
        for l in range(2):
            a = prep_A_inputs(l, b, j, None, g_mix, w_in, b_forget, conv_w, conv_b, w_rg, b_rg, w_ig, b_ig,
                              lru_lambda, w_pool, pool_scale)
            m["ptab"] = a["ptab"]
            m["w1_%d" % l] = a["w1"]
            m["w2_%d" % l] = np.ascontiguousarray(a["w2"].reshape(256, 16, 512))
            m["wrg_%d" % l] = a["wrg"]
            m["wpl_%d" % l] = a["wpl"]
            m["pa_%d" % l] = a["pa"]
            m.update(BW[l])
        in_maps.append(m)
    return in_maps


def kernel(x, g_mix, w_in, b_forget, conv_w, conv_b, w_rg, b_rg, w_ig, b_ig, lru_lambda, w_pool, pool_scale,
           w_branch_rnn, w_branch_attn, w_branch_pool, w_out, g_ffn, w_ffn_in, w_ffn_out, g_final):
    args = [np.asarray(a, dtype=np.float32) for a in (
        x, g_mix, w_in, b_forget, conv_w, conv_b, w_rg, b_rg, w_ig, b_ig, lru_lambda, w_pool, pool_scale,
        w_branch_rnn, w_branch_attn, w_branch_pool, w_out, g_ffn, w_ffn_in, w_ffn_out, g_final)]
    x = args[0]
    nb, S, _ = x.shape
    NTOK = S // 4
    tt = B_TT_FULL if NTOK % B_TT_FULL == 0 else 512
    key = (S, tt)
    if key not in _CACHE:
        _CACHE[key] = build_fused(S, tt)
    P = _CACHE[key]
    in_maps = prep_inputs(*args)
    res = run_bass_kernel_spmd(P.nc, in_maps, core_ids=list(range(NCORES)))
    out = np.empty((nb, S, D), np.float32)
    for c in range(NCORES):
        b, j = c // 4, c % 4
        out[b, j * NTOK:(j + 1) * NTOK, :] = res.results[c]["oT"].T
    return out
```
